# Optimizing a Trainium2 kernel written in Bass

```python
import math
import jax, jax.numpy as jnp
from jax import lax
import numpy as np

D_MODEL = 1024
BATCH = 8
SEQ = 4096
DEPTH = 1

CHUNK = 64
D_MIX = D_MODEL
D_RWKV = D_MIX // 2
D_CONV = D_MIX - D_RWKV
HEAD_SIZE = 64
N_HEADS = D_RWKV // HEAD_SIZE
LORA_W = 64
LORA_A = 64
LORA_G = 128
CONV_WIDTH = 31
D_FF = 2816
D_SHIFT = 3 * D_RWKV + LORA_W + LORA_A + LORA_G
D_IN = D_SHIFT + 2 * D_CONV
RMS_EPS = 1e-6
GN_EPS = 64e-5
LN_EPS = 1e-5
DECAY_SCALE = math.exp(-0.5)

kernel_name = "hybrid_rwkv7_conformer_conv_macaron_block"


def rmsnorm(x, g):
    xf = x.astype(jnp.float32)
    y = xf * lax.rsqrt(jnp.mean(xf * xf, axis=-1, keepdims=True) + RMS_EPS)
    return (y * g.astype(jnp.float32)).astype(x.dtype)


def layernorm(x, g, b):
    xf = x.astype(jnp.float32)
    mu = jnp.mean(xf, axis=-1, keepdims=True)
    var = jnp.mean(jnp.square(xf - mu), axis=-1, keepdims=True)
    y = (xf - mu) * lax.rsqrt(var + LN_EPS)
    return (y * g.astype(jnp.float32) + b.astype(jnp.float32)).astype(x.dtype)


def swiglu(x, w_gu, w_down):
    gu = x @ w_gu
    gate, up = gu[..., :D_FF], gu[..., D_FF:]
    return (jax.nn.silu(gate) * up) @ w_down


def token_shift(y):
    return jnp.pad(y[:, :-1], ((0, 0), (1, 0), (0, 0)))


def rwkv7_recurrence(r, w, k, v, z, b):
    bsz, seq, nh, n = r.shape
    n_chunks = seq // CHUNK

    def to_chunks(t):
        return jnp.transpose(t, (1, 0, 2, 3)).reshape(n_chunks, CHUNK, bsz, nh, n)

    xs = tuple(to_chunks(t) for t in (r, w, k, v, z, b))

    def frame_step(state, inp):
        r_t, w_t, k_t, v_t, z_t, b_t = inp
        sz = jnp.einsum('bhij,bhj->bhi', state, z_t)
        state = (state * w_t[:, :, None, :]
                 + sz[..., None] * b_t[:, :, None, :]
                 + v_t[..., None] * k_t[:, :, None, :])
        y_t = jnp.einsum('bhij,bhj->bhi', state, r_t)
        return state, y_t

    def chunk_step(state, chunk_inp):
        return lax.scan(frame_step, state, chunk_inp)

    state0 = jnp.zeros((bsz, nh, n, n), jnp.float32)
    _, ys = lax.scan(chunk_step, state0, xs)
    ys = ys.reshape(seq, bsz, nh, n)
    return jnp.transpose(ys, (1, 0, 2, 3))


def hybrid_mixer(h, w_in, shift_mu, w_up, w0, a_up, a0, g_up, k_k, k_a, r_k,
                 gn_w, gn_b, conv_dw, conv_b, conv_ln_w, conv_ln_b, w_out):
    bsz, seq, _ = h.shape
    p = h @ w_in
    ps, pc = p[..., :D_SHIFT], p[..., D_SHIFT:]

    ps = ps + (token_shift(ps) - ps) * shift_mu
    o1, o2, o3 = D_RWKV, 2 * D_RWKV, 3 * D_RWKV
    o4, o5 = o3 + LORA_W, o3 + LORA_W + LORA_A
    r, k, v = ps[..., :o1], ps[..., o1:o2], ps[..., o2:o3]
    xw, xa, xg = ps[..., o3:o4], ps[..., o4:o5], ps[..., o5:]

    d = (w0 + jnp.tanh(xw) @ w_up).astype(jnp.float32)
    decay = jnp.exp(-DECAY_SCALE * jax.nn.sigmoid(d))
    a = jax.nn.sigmoid(a0 + xa @ a_up)
    g = jax.nn.sigmoid(xg) @ g_up

    heads = lambda t: t.reshape(bsz, seq, N_HEADS, HEAD_SIZE).astype(jnp.float32)
    kk = heads(k * k_k)
    kk = kk * lax.rsqrt(jnp.maximum(jnp.sum(kk * kk, axis=-1, keepdims=True), 1e-12))
    k = k * (1.0 + (a - 1.0) * k_a)
    rh, kh, vh, ah, wh = heads(r), heads(k), heads(v), heads(a), heads(decay)

    y = rwkv7_recurrence(rh, wh, kh, vh, -kk, kk * ah)
    mu = jnp.mean(y, axis=-1, keepdims=True)
    var = jnp.mean(jnp.square(y - mu), axis=-1, keepdims=True)
    y = (y - mu) * lax.rsqrt(var + GN_EPS)
    y = y * gn_w.astype(jnp.float32).reshape(N_HEADS, HEAD_SIZE) + gn_b.astype(jnp.float32).reshape(N_HEADS, HEAD_SIZE)
    bonus = jnp.sum(rh * kh * r_k.astype(jnp.float32), axis=-1, keepdims=True) * vh
    y = (y + bonus).reshape(bsz, seq, D_RWKV).astype(h.dtype)
    out_a = y * g

    glu = pc[..., :D_CONV] * jax.nn.sigmoid(pc[..., D_CONV:])
    c = lax.conv_general_dilated(
        glu, conv_dw[:, None, :], window_strides=(1,),
        padding=[(CONV_WIDTH - 1, 0)],
        dimension_numbers=('NWC', 'WIO', 'NWC'),
        feature_group_count=D_CONV) + conv_b
    out_b = jax.nn.silu(layernorm(c, conv_ln_w, conv_ln_b))

    return jnp.concatenate([out_a, out_b], axis=-1) @ w_out


def setup_inputs(seed: int = 0) -> dict:
    key = jax.random.key(seed)
    ks = iter(jax.random.split(key, 40))
    f32 = jnp.float32

    def nrm(shape, scale):
        return jax.random.normal(next(ks), shape, f32) * scale

    def gain(shape):
        return 1.0 + nrm(shape, 0.02)

    L = DEPTH
    return {
        "x": nrm((BATCH, SEQ, D_MODEL), 1.0),
        "ffn1_norm_pre": gain((L, D_MODEL)),
        "ffn1_norm_post": gain((L, D_MODEL)),
        "ffn1_w_gu": nrm((L, D_MODEL, 2 * D_FF), D_MODEL ** -0.5),
        "ffn1_w_down": nrm((L, D_FF, D_MODEL), D_FF ** -0.5),
        "mix_norm_pre": gain((L, D_MODEL)),
        "mix_norm_post": gain((L, D_MODEL)),
        "w_in": nrm((L, D_MODEL, D_IN), D_MODEL ** -0.5),
        "shift_mu": jax.random.uniform(next(ks), (L, D_SHIFT), f32, 0.1, 0.9),
        "w_up": nrm((L, LORA_W, D_RWKV), 0.3 * LORA_W ** -0.5),
        "w0": nrm((L, D_RWKV), 0.5),
        "a_up": nrm((L, LORA_A, D_RWKV), 0.3 * LORA_A ** -0.5),
        "a0": nrm((L, D_RWKV), 0.1),
        "g_up": nrm((L, LORA_G, D_RWKV), LORA_G ** -0.5),
        "k_k": 0.85 + nrm((L, D_RWKV), 0.02),
        "k_a": 1.0 + nrm((L, D_RWKV), 0.02),
        "r_k": nrm((L, N_HEADS, HEAD_SIZE), 0.1),
        "gn_w": gain((L, D_RWKV)),
        "gn_b": nrm((L, D_RWKV), 0.01),
        "conv_dw": nrm((L, CONV_WIDTH, D_CONV), CONV_WIDTH ** -0.5),
        "conv_b": nrm((L, D_CONV), 0.01),
        "conv_ln_w": gain((L, D_CONV)),
        "conv_ln_b": nrm((L, D_CONV), 0.01),
        "w_out": nrm((L, D_MIX, D_MODEL), D_MIX ** -0.5),
        "ffn2_norm_pre": gain((L, D_MODEL)),
        "ffn2_norm_post": gain((L, D_MODEL)),
        "ffn2_w_gu": nrm((L, D_MODEL, 2 * D_FF), D_MODEL ** -0.5),
        "ffn2_w_down": nrm((L, D_FF, D_MODEL), D_FF ** -0.5),
    }


def reference(x, ffn1_norm_pre, ffn1_norm_post, ffn1_w_gu, ffn1_w_down,
              mix_norm_pre, mix_norm_post, w_in, shift_mu, w_up, w0, a_up, a0,
              g_up, k_k, k_a, r_k, gn_w, gn_b, conv_dw, conv_b, conv_ln_w,
              conv_ln_b, w_out, ffn2_norm_pre, ffn2_norm_post, ffn2_w_gu,
              ffn2_w_down):
    for l in range(DEPTH):
        f = swiglu(rmsnorm(x, ffn1_norm_pre[l]), ffn1_w_gu[l], ffn1_w_down[l])
        x = x + 0.5 * rmsnorm(f, ffn1_norm_post[l])
        m = hybrid_mixer(rmsnorm(x, mix_norm_pre[l]), w_in[l], shift_mu[l], w_up[l],
                         w0[l], a_up[l], a0[l], g_up[l], k_k[l], k_a[l], r_k[l],
                         gn_w[l], gn_b[l], conv_dw[l], conv_b[l], conv_ln_w[l],
                         conv_ln_b[l], w_out[l])
        x = x + rmsnorm(m, mix_norm_post[l])
        f = swiglu(rmsnorm(x, ffn2_norm_pre[l]), ffn2_w_gu[l], ffn2_w_down[l])
        x = x + 0.5 * rmsnorm(f, ffn2_norm_post[l])
    return x
```

```python
import math
from functools import partial as functools_partial
from contextlib import ExitStack
import numpy as np
import concourse.bass as bass
import concourse.mybir as mybir
from concourse.bass_utils import run_bass_kernel_spmd

F32 = mybir.dt.float32
BF16 = mybir.dt.bfloat16
AF = mybir.ActivationFunctionType
ALU = mybir.AluOpType
AX = mybir.AxisListType

D = 1024
DFF = 2816
NF = 22
DRW = 512
DS = math.exp(-0.5)
TT = 512
WRITE_KEYS = ("out", "accum_out", "ap")
ENGS = ("sync", "tensor", "scalar", "vector", "gpsimd")


def _is_ap(v):
    return hasattr(v, "tensor") and hasattr(v, "ap") and hasattr(v, "offset")


def _region(ap):
    t = ap.tensor
    es = mybir.dt.size(ap.dtype)
    dims = [(int(s), int(c)) for s, c in ap.ap]
    off = int(ap.offset) * es
    space = str(ap.space)
    if "SB" in space or "PS" in space.upper():
        shp = [int(s) for s in t.shape]
        pstride = int(np.prod(shp[1:])) * mybir.dt.size(t.dtype)
        p0 = off // pstride
        f0 = off % pstride
        pc = dims[0][1]
        lo = hi = 0
        for s, c in dims[1:]:
            e = s * (c - 1) * es
            if e < 0:
                lo += e
            else:
                hi += e
        if "SB" not in space:
            return t.name, (p0 // 32 * 32, (p0 + pc + 31) // 32 * 32, 0, pstride)
        return t.name, (p0, p0 + pc, f0 + lo, f0 + hi + es)
    lo = hi = 0
    for s, c in dims:
        e = s * (c - 1) * es
        if e < 0:
            lo += e
        else:
            hi += e
    return "dram:" + t.name, (0, 1, off + lo, off + hi + es)


def _ovl(a, b):
    return a[0] < b[1] and b[0] < a[1] and a[2] < b[3] and b[2] < a[3]


def _contains(a, b):
    return a[0] <= b[0] and a[1] >= b[1] and a[2] <= b[2] and a[3] >= b[3]


class Op:
    __slots__ = ("eng", "meth", "kw", "deps", "marked", "dma_key", "tok", "idx")


class KB:
    def __init__(self, nc):
        self.nc = nc
        self.ops = {e: [] for e in ENGS}
        self.acc = {}
        self.dma_cnt = {}
        self.final = []

    def op(self, eng, meth, dma_key=None, **kw):
        o = Op()
        o.eng, o.meth, o.kw, o.marked, o.dma_key = eng, meth, kw, False, dma_key
        o.idx = len(self.ops[eng])
        is_dma = meth == "dma_start"
        if is_dma:
            assert dma_key is not None
            self.dma_cnt[dma_key] = self.dma_cnt.get(dma_key, 0) + 16
            o.tok = (dma_key, self.dma_cnt[dma_key])
        else:
            o.tok = None
        deps = {}
        accs = []
        for name, v in kw.items():
            if not _is_ap(v):
                continue
            kind = "w" if name in WRITE_KEYS else "r"
            tn, box = _region(v)
            accs.append((tn, box, kind))
            d = self.acc.get(tn)
            if not d:
                continue
            for (k2, box2), (o2, _) in d.items():
                if kind == "r" and k2[1] == "r":
                    continue
                if o2 is o:
                    continue
                if _ovl(box, box2):
                    if o2.eng == "tensor" and eng == "tensor" and o2.meth != "dma_start" and not is_dma:
                        continue
                    key = (o2.eng, o2.dma_key)
                    if key not in deps or deps[key].idx < o2.idx:
                        deps[key] = o2
        for o2 in deps.values():
            if o2.meth != "dma_start":
                o2.marked = True
        o.deps = list(deps.values())
        me = (eng, dma_key)
        for tn, box, kind in accs:
            d = self.acc.setdefault(tn, {})
            if kind == "w":
                for k in [k for k in d if _contains(box, k[1])]:
                    del d[k]
            d[((me, kind), box)] = (o, kind)
        self.ops[eng].append(o)
        return o

    def finish(self, o):
        self.final.append(o)

    def emit(self):
        nc = self.nc
        for e in ENGS:
            c = 0
            for o in self.ops[e]:
                if o.meth != "dma_start" and o.marked:
                    c += 1
                    o.tok = (e, c)
        with ExitStack() as st:
            sems = {}
            for e in ENGS:
                sems[e] = st.enter_context(nc.semaphore("s_" + e))
            for k in self.dma_cnt:
                sems[k] = st.enter_context(nc.semaphore("d_" + k))
            block = st.enter_context(nc.Block())

            def run(eng, ename):
                waited = {}
                for o in self.ops[ename]:
                    for o2 in o.deps:
                        k, v = o2.tok
                        if waited.get(k, 0) < v:
                            eng.wait_ge(sems[k], v)
                            waited[k] = v
                    inst = getattr(eng, o.meth)(**o.kw)
                    if o.meth == "dma_start":
                        inst.then_inc(sems[o.dma_key], 16)
                    elif o.marked:
                        inst.then_inc(sems[ename], 1)
                if ename == "sync":
                    for o2 in self.final:
                        k, v = o2.tok
                        if waited.get(k, 0) < v:
                            eng.wait_ge(sems[k], v)
                            waited[k] = v

            @block.sync
            def _(eng):
                run(eng, "sync")

            @block.tensor
            def _(eng):
                run(eng, "tensor")

            @block.scalar
            def _(eng):
                run(eng, "scalar")

            @block.vector
            def _(eng):
                run(eng, "vector")

            @block.gpsimd
            def _(eng):
                run(eng, "gpsimd")


C_ID = 0
C_M128 = 128
C_MAT = 256
C_BONES = 320
C_ONESM = 448
C_RESET = 576
C_HSEL = 704
C_NEGH = 736
C_END = 737


def _consts():
    c = np.zeros((128, C_END), np.float32)
    c[:, C_ID:C_ID + 128] = np.eye(128)
    p = np.arange(128)[:, None] % 64
    q = np.arange(128)[None, :]
    c[:, C_M128:C_M128 + 128] = np.where(q < 64, p < q, p <= (q - 64))
    t = np.arange(128)[:, None] % 64
    j = np.arange(64)[None, :]
    c[:, C_MAT:C_MAT + 64] = (j < t)
    c[:, C_BONES:C_BONES + 128] = (np.arange(128)[:, None] // 64 == np.arange(128)[None, :] // 64)
    c[:, C_ONESM:C_ONESM + 128] = 1.0 / 512.0
    r = np.ones((128, 128), np.float32)
    r[:, 0::64] = 0.0
    c[:, C_RESET:C_RESET + 128] = r
    hs = np.zeros((128, 4, 8), np.float32)
    for jj in range(4):
        hs[0:64, jj, 2 * jj] = 1.0
        hs[64:128, jj, 2 * jj + 1] = 1.0
    c[:, C_HSEL:C_HSEL + 32] = hs.reshape(128, 32)
    c[:, C_NEGH] = -0.5
    return c


K_MU = 0
K_W0 = 14
K_A0 = 18
K_KK = 22
K_KA = 26
K_RK = 30
K_CB = 34
K_LNW = 38
K_LNB = 42
K_G1 = 46
K_GM = 54
K_G2 = 62
K_DW = 70
K_END = 70 + 124


def _fm(v, n):
    return np.ascontiguousarray(np.asarray(v, np.float32).reshape(n, 128).T)


def build(T, stage=99, sub=99):
    NT = T // TT
    nc = bass.Bass("TRN2", target_bir_lowering=False)
    kb = KB(nc)
    kb_real = kb

    def din(name, shape, dt=F32):
        return nc.dram_tensor(name, list(shape), dt, kind="ExternalInput").ap()

    x_d = din("x", [T, D])
    out_d = nc.dram_tensor("out", [T, D], F32, kind="ExternalOutput").ap()
    wgu_d = [din("wgu1", [D, 2 * DFF]), din("wgu2", [D, 2 * DFF])]
    wdn_d = [din("wdn1", [DFF, D]), din("wdn2", [DFF, D])]
    win_d = din("win", [D, 2816])
    wout_d = din("wout", [D, D])
    cols_d = din("cols", [128, K_END])
    const_d = din("consts", [128, C_END])
    rows_d = din("rows", [5, D])
    lora_d = din("lora", [128, 512])
    gup_d = din("gup", [128, 512])

    def dscr(name, shape):
        return nc.dram_tensor(name, list(shape), BF16, kind="Internal").ap()

    wguS = [dscr("wguS1", [2 * NF, 128, 1024]), dscr("wguS2", [2 * NF, 128, 1024])]
    wdnS = [dscr("wdnS1", [2, 11, 128, 2, 512]), dscr("wdnS2", [2, 11, 128, 2, 512])]
    winS = dscr("winS", [NF, 128, 1024])
    woutS = dscr("woutS", [2, 4, 128, 2, 512])

    st = ExitStack()

    def sb(name, shape, dt=F32):
        return st.enter_context(nc.sbuf_tensor(name, list(shape), dt))

    cst = sb("cst", [128, C_END])
    cols = sb("cols_sb", [128, K_END])
    gcur = sb("gcur", [128, D])
    gnwb = sb("gnwb", [128, 2, 512])
    identb = sb("identb", [128, 128], BF16)
    bonesb = sb("bonesb", [128, 128], BF16)
    lorab = sb("lorab", [128, 512], BF16)
    gupb = sb("gupb", [128, 512], BF16)
    diag = sb("diag", [128, 124, 128], BF16)
    xt = sb("xt", [128, 4, D])
    slotA = [sb(f"slotA{i}", [128, 8, 256], BF16) for i in range(3)]
    slotB = [sb(f"slotB{i}", [128, 2, 512], BF16) for i in range(3)]
    sstat = sb("sstat", [128, 32])
    ARENA = 121 * 1024
    arena = sb("arena", [128, ARENA // 4])
    ident = cst[:, C_ID:C_ID + 128]

    class Carver:
        def __init__(self, start=18 * 1024, limit=ARENA, nxt=None):
            self.off = start
            self.limit = limit
            self.nxt = nxt

        def get(self, shape, dt=F32):
            n = int(np.prod(shape[1:])) * mybir.dt.size(dt)
            n = (n + 31) // 32 * 32
            if self.off + n > self.limit and self.nxt is not None:
                return self.nxt.get(shape, dt)
            assert self.off + n <= self.limit, (self.off, n, self.limit)
            a = arena[:, self.off // 4:(self.off + n) // 4]
            self.off += n
            if dt != F32:
                a = a.bitcast(dt)
            a = a[0:shape[0], 0:int(np.prod(shape[1:]))]
            if len(shape) == 3:
                a = a.rearrange("p (a b) -> p a b", b=shape[2])
            elif len(shape) == 4:
                a = a.rearrange("p (a b c) -> p a b c", b=shape[2], c=shape[3])
            return a

    _c0 = Carver(start=0)
    xs = [_c0.get([128, D]), _c0.get([128, D])]
    xnT = _c0.get([128, 8, TT], BF16)
    gjunk = _c0.get([128, D], BF16)
    assert _c0.off == 18 * 1024
    ps = [st.enter_context(nc.psum_tensor(f"ps{i}", [128, 512], F32)) for i in range(8)]

    V, S_, G, PE, SY = "vector", "scalar", "gpsimd", "tensor", "sync"

    kb.op(SY, "dma_start", dma_key="c0", out=cst[:], in_=const_d)
    kb.op(SY, "dma_start", dma_key="c1", out=cols[:], in_=cols_d)
    kb.op(SY, "dma_start", dma_key="c3", out=gnwb[:].rearrange("p a b -> p (a b)"),
          in_=rows_d[3:4, :].partition_broadcast(128))
    cv = Carver(start=0)
    stg0 = cv.get([128, 512])
    stg1 = cv.get([128, 512])
    kb.op(SY, "dma_start", dma_key="c4", out=stg0, in_=lora_d)
    kb.op(SY, "dma_start", dma_key="c5", out=stg1, in_=gup_d)
    kb.op(V, "tensor_copy", out=lorab[:], in_=stg0)
    kb.op(V, "tensor_copy", out=gupb[:], in_=stg1)
    kb.op(V, "tensor_copy", out=identb[:], in_=ident)
    kb.op(V, "tensor_copy", out=bonesb[:], in_=cst[:, C_BONES:C_BONES + 128])
    for i in range(124):
        kb.op(V if i % 2 else G, "tensor_scalar", out=diag[:, i, :], in0=ident,
              scalar1=cols[:, K_DW + i:K_DW + i + 1], scalar2=None, op0=ALU.mult)

    NSTG, NACC = 4, 3
    stage32 = [cv.get([128, 2048]) for _ in range(NSTG)]
    accb = [cv.get([128, 11, 1024], BF16) for _ in range(NACC)]
    cctr = [0]
    sctr = [0]
    hctr = [0]

    def cast(out, in_):
        eng = (S_, G, V)[cctr[0] % 3]
        cctr[0] += 1
        if eng == S_:
            kb.op(S_, "activation", out=out, in_=in_, func=AF.Copy)
        else:
            kb.op(eng, "tensor_copy", out=out, in_=in_)

    pend_st = []

    def flush_st():
        while pend_st:
            pend_st.pop(0)()

    def conv_A(src, dstS, nhalves):
        for qq in range(nhalves * 2):
            ai = hctr[0] % NACC
            a = accb[ai]
            hctr[0] += 1
            av = a.rearrange("p f (k c) -> p f k c", c=128)
            for kc in range(8):
                i = sctr[0] % NSTG
                sctr[0] += 1
                stg = stage32[i][:, 0:1408]
                kb.op(SY, "dma_start", dma_key=f"cv{i}", out=stg, in_=src[kc * 128:(kc + 1) * 128, qq * 1408:(qq + 1) * 1408])
                cast(av[:, :, kc, :], stg.rearrange("p (f c) -> p f c", c=128))
                if kc == 3:
                    flush_st()
            pend_st.append(lambda ai=ai, qq=qq, a=a, dstS=dstS: kb.op(
                SY, "dma_start", dma_key=f"csA{ai}", out=dstS[qq * 11:qq * 11 + 11].rearrange("f p x -> p f x"), in_=a[:]))

    def conv_B(src, dstS, nk2):
        sv = src.rearrange("(k s p) n -> k p s n", s=2, p=128)
        for k2 in range(nk2):
            i = sctr[0] % NSTG
            sctr[0] += 1
            stg = stage32[i]
            ai = hctr[0] % NACC
            a = accb[ai]
            hctr[0] += 1
            bst = a.rearrange("p f x -> p (f x)")[:, 0:2048]
            kb.op(SY, "dma_start", dma_key=f"cv{i}", out=stg.rearrange("p (s n) -> p s n", s=2), in_=sv[k2])
            cast(bst.rearrange("p (h s n) -> p h s n", h=2, s=2), stg.rearrange("p (s h n) -> p h s n", s=2, h=2))
            flush_st()
            for hf in range(2):
                pend_st.append(lambda ai=ai, hf=hf, k2=k2, bst=bst, dstS=dstS: kb.op(
                    SY, "dma_start", dma_key=f"csB{ai}_{hf}", out=dstS[hf, k2].rearrange("p s n -> p (s n)"),
                    in_=bst[:, hf * 1024:(hf + 1) * 1024]))

    conv_A(wgu_d[0], wguS[0], 2)
    conv_B(wdn_d[0], wdnS[0], 11)
    conv_A(win_d, winS, 1)
    conv_B(wout_d, woutS, 4)
    conv_A(wgu_d[1], wguS[1], 2)
    conv_B(wdn_d[1], wdnS[1], 11)
    flush_st()

    actr = [0]
    bctr = [0]

    def loadA(src):
        i = actr[0] % 3
        actr[0] += 1
        kb.op(SY, "dma_start", dma_key=f"A{i}", out=slotA[i][:], in_=src)
        return slotA[i]

    def loadA2(src):
        i = actr[0] % 3
        actr[0] += 1
        v = slotA[i][:].rearrange("p k c -> p (k c)").rearrange("p (g x) -> p g x", g=2)
        kb.op(SY, "dma_start", dma_key=f"A{i}", out=v, in_=src)
        return slotA[i][:].rearrange("p k c -> p (k c)").rearrange("p (g k c) -> p g k c", g=2, k=8)

    def loadB(src):
        i = bctr[0] % 3
        bctr[0] += 1
        kb.op(SY, "dma_start", dma_key=f"B{i}", out=slotB[i][:], in_=src)
        return slotB[i]

    def rstd_from_ss(ss_ap, out_ap, n, eps, npart=128):
        kb.op(V, "tensor_scalar", out=out_ap, in0=ss_ap, scalar1=1.0 / n, scalar2=eps, op0=ALU.mult, op1=ALU.add)
        kb.op(S_, "activation", out=out_ap, in_=out_ap, func=AF.Sqrt)
        kb.op(V, "reciprocal", out=out_ap, in_=out_ap)

    def prenorm(gcol, blocks=(0, 1, 2, 3)):
        junk = gjunk
        for b in blocks:
            kb.op(S_, "activation", out=junk, in_=xt[:, b, :], func=AF.Square, accum_out=sstat[:, b:b + 1])
            rstd_from_ss(sstat[:, b:b + 1], sstat[:, 4 + b:5 + b], D, 1e-6)
            xb = xs[b % 2]
            kb.op(V, "tensor_scalar", out=xb, in0=xt[:, b, :], scalar1=sstat[:, 4 + b:5 + b], scalar2=None,
                  op0=ALU.mult)
            for hf in range(2):
                pb = ps[(b % 2) * 2 + hf]
                for q in range(4):
                    kc = hf * 4 + q
                    kb.op(PE, "transpose", out=pb[:, q * 128:(q + 1) * 128], in_=xb[:, kc * 128:(kc + 1) * 128],
                          identity=ident)
                kb.op(V, "tensor_tensor", out=xnT[:, hf * 4:hf * 4 + 4, b * 128:(b + 1) * 128],
                      in0=pb[:].rearrange("p (a b) -> p a b", b=128),
                      in1=cols[:, gcol + hf * 4:gcol + hf * 4 + 4].unsqueeze(2).to_broadcast([128, 4, 128]),
                      op=ALU.mult)

    def load_gain(gi):
        kb.op(SY, "dma_start", dma_key="gc", out=gcur[:], in_=rows_d[gi:gi + 1, :].partition_broadcast(128))
        if gi != 1:
            kb.op(V, "tensor_scalar", out=gcur[:], in0=gcur[:], scalar1=0.5, scalar2=None, op0=ALU.mult)

    def postnorm_residual(gi, banks, tmp, blocks):
        for b in blocks:
            c0 = 16 + 4 * b
            for hf in range(2):
                kb.op(S_, "activation", out=gjunk[:, hf * 512:(hf + 1) * 512], in_=banks[b][hf][:], func=AF.Square,
                      accum_out=sstat[:, c0 + hf:c0 + hf + 1])
            kb.op(V, "tensor_tensor", out=sstat[:, c0 + 2:c0 + 3], in0=sstat[:, c0:c0 + 1], in1=sstat[:, c0 + 1:c0 + 2], op=ALU.add)
            rstd_from_ss(sstat[:, c0 + 2:c0 + 3], sstat[:, c0 + 3:c0 + 4], D, 1e-6)
            for hf in range(2):
                t = tmp[((b % 2) * 2 + hf) % len(tmp)]
                kb.op(V, "scalar_tensor_tensor", out=t, in0=banks[b][hf][:], scalar=sstat[:, c0 + 3:c0 + 4],
                      in1=gcur[:, hf * 512:(hf + 1) * 512], op0=ALU.mult, op1=ALU.mult)
                kb.op(G, "tensor_tensor", out=xt[:, b, hf * 512:(hf + 1) * 512], in0=xt[:, b, hf * 512:(hf + 1) * 512],
                      in1=t, op=ALU.add)

    def tokmajor_proj(lhs_fn, nk2, wS, gi, tmp, after_half=None):
        banks = {b: [ps[b * 2 + hf] for hf in range(2)] for b in range(4)}
        for hf in range(2):
            for k2 in range(nk2):
                sl = loadB(wS[hf, k2])
                for s in range(2):
                    kc = k2 * 2 + s
                    for b in range(4):
                        kb.op(PE, "matmul", out=banks[b][hf][:], lhsT=lhs_fn(kc, b), rhs=sl[:, s, :],
                              start=(kc == 0), stop=(kc == nk2 * 2 - 1))
        for b in range(4):
            postnorm_residual(gi, banks, tmp, (b,))
            if after_half is not None:
                after_half((b,))

    def ffn(l, after_half=None):
        cvf = Carver()
        hT = cvf.get([128, NF, TT], BF16)
        sg = [cvf.get([128, TT]) for _ in range(2)]
        tmp = [cvf.get([128, 512]) for _ in range(4)]
        load_gain(0 if l == 0 else 2)
        for f in range(NF):
            sl = loadA(wguS[l].rearrange("(g f) p x -> f p g x", g=2)[f]).rearrange("p k (g c) -> p g k c", g=2) if False else loadA2(wguS[l].rearrange("(g f) p x -> f p g x", g=2)[f])
            pg = ps[4 + (f % 2) * 2]
            pu = ps[5 + (f % 2) * 2]
            for kc in range(8):
                kb.op(PE, "matmul", out=pg[:], lhsT=sl[:, 0, kc, :], rhs=xnT[:, kc, :], start=(kc == 0), stop=(kc == 7))
            for kc in range(8):
                kb.op(PE, "matmul", out=pu[:], lhsT=sl[:, 1, kc, :], rhs=xnT[:, kc, :], start=(kc == 0), stop=(kc == 7))
            kb.op(S_, "activation", out=sg[f % 2], in_=pg[:], func=AF.Silu)
            kb.op(V, "tensor_tensor", out=hT[:, f, :], in0=sg[f % 2], in1=pu[:], op=ALU.mult)
        tokmajor_proj(lambda kc, b: hT[:, kc, b * 128:(b + 1) * 128], 11, wdnS[l], 0 if l == 0 else 2, tmp, after_half)

    carry = sb("carry", [128, 14])
    glu = sb("glu", [128, 4, 30 + TT], BF16)
    STf = sb("STf", [128, 4, 64])
    STb = sb("STb", [128, 4, 64], BF16)

    def mset(eng, ap, val):
        kb.op(eng, "memset", ap=ap, constant=val)

    def mixer(ti, after_half=None):
        cvm = Carver()
        g = cvm.get
        praw = [g([128, 513]) for _ in range(2)]
        dtmp = g([128, 512])
        xwa = g([128, TT])
        xg = g([128, TT])
        cT = g([128, 4, TT])
        csq = g([128, TT])
        mean = g([128, TT])
        rstd = g([128, TT])
        tmp = [g([128, 512]) for _ in range(2)]
        endA = cvm.off
        rkvT = g([128, 12, TT])
        sgx = g([128, TT], BF16)
        xwab = g([128, TT], BF16)
        catT = g([128, 8, TT], BF16)
        endP = cvm.off
        load_gain(1)
        dst_of = {}
        for c in range(12):
            dst_of[c] = rkvT[:, c, :]
        dst_of[12] = xwa
        dst_of[13] = xg
        order = list(range(14)) + [14, 18, 15, 19, 16, 20, 17, 21]
        slots = {}
        sigt = {}
        for n, c in enumerate(order):
            c2 = c // 2
            if c2 not in slots:
                slots[c2] = loadA2(winS[2 * c2:2 * c2 + 2].rearrange("g p x -> p g x"))
            sl = slots[c2]
            pb = ps[4 + n % 4]
            for kc in range(8):
                kb.op(PE, "matmul", out=pb[:], lhsT=sl[:, c % 2, kc, :], rhs=xnT[:, kc, :],
                      start=(kc == 0), stop=(kc == 7))
            if c < 14:
                pr = praw[c % 2]
                kb.op(S_, "activation", out=pr[:, 1:513], in_=pb[:], func=AF.Copy)
                kb.op(G, "tensor_copy", out=pr[:, 0:1], in_=carry[:, c:c + 1])
                kb.op(G, "tensor_tensor", out=dtmp, in0=pr[:, 0:512], in1=pr[:, 1:513], op=ALU.subtract)
                kb.op(V, "scalar_tensor_tensor", out=dst_of[c], in0=dtmp, scalar=cols[:, K_MU + c:K_MU + c + 1],
                      in1=pr[:, 1:513], op0=ALU.mult, op1=ALU.add)
                kb.op(G, "tensor_copy", out=carry[:, c:c + 1], in_=pr[:, 512:513])
            elif c < 18:
                sigt[c] = pb
            else:
                kb.op(S_, "activation", out=tmp[c % 2], in_=pb[:], func=AF.Sigmoid)
                kb.op(V, "tensor_tensor", out=glu[:, c - 18, 30:30 + TT], in0=tmp[c % 2], in1=sigt[c - 4][:], op=ALU.mult)
        kb.op(S_, "activation", out=xwab[0:64, :], in_=xwa[0:64, :], func=AF.Tanh)
        kb.op(V, "tensor_copy", out=xwab[64:128, :], in_=xwa[64:128, :])
        kb.op(S_, "activation", out=sgx, in_=xg, func=AF.Sigmoid)
        def conv_section():
            Lc = []

            class _KBc:
                @staticmethod
                def op(*a, **k):
                    Lc.append(functools_partial(kb_real.op, *a, **k))

            kb = _KBc
            for j in range(4):
                pb = ps[4 + j % 2]
                for tau in range(31):
                    kb.op(PE, "matmul", out=pb[:], lhsT=diag[:, j * 31 + tau, :], rhs=glu[:, j, tau:tau + TT], start=(tau == 0),
                          stop=(tau == 30))
                kb.op(V, "tensor_scalar", out=cT[:, j, :], in0=pb[:], scalar1=cols[:, K_CB + j:K_CB + j + 1], scalar2=None,
                      op0=ALU.add)
            for j in range(4):
                kb.op(G, "tensor_copy", out=glu[:, j, 0:30], in_=glu[:, j, TT:TT + 30])
            onesm = cst[:, C_ONESM:C_ONESM + 128]
            pm, pv = ps[2], ps[3]
            for j in range(4):
                kb.op(PE, "matmul", out=pm[:], lhsT=onesm, rhs=cT[:, j, :], start=(j == 0), stop=(j == 3))
            for j in range(4):
                kb.op(S_, "activation", out=csq, in_=cT[:, j, :], func=AF.Square)
                kb.op(PE, "matmul", out=pv[:], lhsT=onesm, rhs=csq, start=(j == 0), stop=(j == 3))
            kb.op(S_, "activation", out=mean, in_=pm[:], func=AF.Copy)
            kb.op(V, "tensor_tensor", out=rstd, in0=mean, in1=mean, op=ALU.mult)
            kb.op(V, "tensor_tensor", out=rstd, in0=pv[:], in1=rstd, op=ALU.subtract)
            kb.op(V, "tensor_scalar", out=rstd, in0=rstd, scalar1=1e-5, scalar2=None, op0=ALU.add)
            kb.op(S_, "activation", out=rstd, in_=rstd, func=AF.Sqrt)
            kb.op(V, "reciprocal", out=rstd, in_=rstd)
            for j in range(4):
                kb.op(V, "tensor_tensor", out=csq, in0=cT[:, j, :], in1=mean, op=ALU.subtract)
                kb.op(V, "tensor_tensor", out=csq, in0=csq, in1=rstd, op=ALU.mult)
                kb.op(S_, "activation", out=catT[:, 4 + j, :], in_=csq, func=AF.Silu,
                      scale=cols[:, K_LNW + j:K_LNW + j + 1], bias=cols[:, K_LNB + j:K_LNB + j + 1])

            return Lc

        from functools import partial as Pp
        _tail = Carver(start=endP)
        gP = _tail.get
        gA = Carver(start=0, limit=endA, nxt=_tail).get
        g = gP

        def t4():
            return g([128, 4, 128])

        tA, tB, tC, tD, tE, tF, tG, tH, tI, sgdB, aB = [t4() for _ in range(11)]
        rk2 = [t4(), t4()]
        sqb = g([128, 4, 128], BF16)
        LT = g([128, 4, 2, 128], BF16)
        RT = g([128, 4, 2, 128], BF16)
        FBK = g([128, 4, 2, 128], BF16)
        FZV = g([128, 4, 2, 128], BF16)
        ZR2 = [g([128, 4, 2, 128], BF16) for _ in range(2)]
        wc2 = [g([128, 4, 2]) for _ in range(2)]
        g = gA
        TBK = [g([128, 512], BF16) for _ in range(2)]
        TZ = [g([64, 512], BF16) for _ in range(2)]
        UV = [g([128, 8, 64], BF16) for _ in range(2)]
        AAm = [g([128, 8, 128], BF16) for _ in range(2)]
        ATm = [g([64, 8, 64], BF16) for _ in range(2)]
        Pk = [[g([64, 8, 64], BF16) for _ in range(2)] for _ in range(2)]
        PTk = [[g([64, 8, 64], BF16) for _ in range(2)] for _ in range(2)]
        Tm = [g([64, 8, 64], BF16) for _ in range(2)]
        XvT = [g([64, 8, 64], BF16) for _ in range(2)]
        UVT = [g([64, 8, 64]) for _ in range(2)]
        Y1 = g([128, 8, 64])
        ysq = g([128, 8, 64])
        yyc = [g([128, 8, 64]) for _ in range(2)]
        G1 = [g([128, 512]) for _ in range(2)]
        G0 = [g([128, 8, 64]) for _ in range(2)]
        st8c = [g([128, 48]) for _ in range(2)]
        srkc = [g([128, 8]) for _ in range(2)]

        def v4(ap):
            return ap.rearrange("p j (c t) -> p j c t", c=2)

        def fl(ap):
            return ap.rearrange("p j t -> p (j t)")

        def prep(b):
            L = []
            A = lambda *a, **k: L.append(Pp(kb.op, *a, **k))
            par = b % 2
            ZR, wc, rk = ZR2[par], wc2[par], rk2[par]
            bs = slice(b * 128, (b + 1) * 128)
            rT, kT, vT = rkvT[:, 0:4, bs], rkvT[:, 4:8, bs], rkvT[:, 8:12, bs]
            psd, psa, pn = ps[6], ps[6], ps[6]
            for j in range(4):
                A(PE, "matmul", out=psd[:, j * 128:(j + 1) * 128], lhsT=lorab[0:64, j * 128:(j + 1) * 128],
                  rhs=xwab[0:64, bs], start=True, stop=True)
            for j in range(4):
                A(S_, "activation", out=sgdB[:, j, :], in_=psd[:, j * 128:(j + 1) * 128], func=AF.Sigmoid,
                  bias=cols[:, K_W0 + j:K_W0 + j + 1])
            for j in range(4):
                A(PE, "matmul", out=psa[:, j * 128:(j + 1) * 128], lhsT=lorab[64:128, j * 128:(j + 1) * 128],
                  rhs=xwab[64:128, bs], start=True, stop=True)
            for j in range(4):
                A(S_, "activation", out=aB[:, j, :], in_=psa[:, j * 128:(j + 1) * 128], func=AF.Sigmoid,
                  bias=cols[:, K_A0 + j:K_A0 + j + 1])
            for j in range(4):
                A(V, "tensor_scalar", out=tA[:, j, :], in0=kT[:, j, :], scalar1=cols[:, K_KK + j:K_KK + j + 1],
                  scalar2=None, op0=ALU.mult)
            A(G, "tensor_tensor", out=sqb, in0=tA, in1=tA, op=ALU.mult)
            A(PE, "matmul", out=pn[:], lhsT=bonesb[:], rhs=fl(sqb), start=True, stop=True)
            A(V, "tensor_scalar", out=fl(tB), in0=pn[:], scalar1=1e-12, scalar2=None, op0=ALU.max)
            nsplit = len(L)
            A(S_, "activation", out=fl(tB), in_=fl(tB), func=AF.Ln)
            A(S_, "activation", out=fl(tB), in_=fl(tB), func=AF.Exp, scale=-0.5)
            A(V, "tensor_tensor", out=tA, in0=tA, in1=tB, op=ALU.mult)
            for j in range(4):
                A(V, "tensor_scalar", out=tC[:, j, :], in0=aB[:, j, :], scalar1=-1.0,
                  scalar2=cols[:, K_KA + j:K_KA + j + 1], op0=ALU.add, op1=ALU.mult)
            A(V, "scalar_tensor_tensor", out=tC, in0=tC, scalar=1.0, in1=kT, op0=ALU.add, op1=ALU.mult)
            A(G, "tensor_tensor", out=tD, in0=tA, in1=aB, op=ALU.mult)
            for j in range(4):
                A(G, "tensor_tensor", out=rk[:, j, :], in0=rT[:, j, :], in1=tC[:, j, :], op=ALU.mult)
                A(G, "tensor_scalar", out=rk[:, j, :], in0=rk[:, j, :], scalar1=cols[:, K_RK + j:K_RK + j + 1],
                  scalar2=None, op0=ALU.mult)
            for j in range(4):
                A(V, "tensor_tensor_scan", out=tE[:, j, :], data0=cst[:, C_RESET:C_RESET + 128],
                  data1=sgdB[:, j, :], initial=0.0, op0=ALU.mult, op1=ALU.add)
            A(G, "tensor_tensor", out=tF, in0=tE, in1=sgdB, op=ALU.subtract)
            A(G, "tensor_tensor", out=v4(tG), in0=v4(tE)[:, :, :, 63:64].to_broadcast([128, 4, 2, 64]), in1=v4(tE),
              op=ALU.subtract)
            A(S_, "activation", out=tH, in_=tE, func=AF.Exp, scale=-DS)
            A(S_, "activation", out=tI, in_=tE, func=AF.Exp, scale=DS)
            A(S_, "activation", out=tF, in_=tF, func=AF.Exp, scale=-DS)
            A(S_, "activation", out=tG, in_=tG, func=AF.Exp, scale=-DS)
            A(G, "tensor_copy", out=wc, in_=v4(tH)[:, :, :, 63])
            A(V, "tensor_tensor", out=RT[:, :, :, 64:128], in0=v4(rT), in1=v4(tH), op=ALU.mult)
            A(G, "tensor_tensor", out=ZR[:, :, :, 64:128], in0=v4(rT), in1=v4(tH), op=ALU.mult)
            A(V, "scalar_tensor_tensor", out=tB, in0=tA, scalar=-1.0, in1=tF, op0=ALU.mult, op1=ALU.mult)
            A(G, "tensor_copy", out=RT[:, :, :, 0:64], in_=v4(tB))
            A(G, "tensor_copy", out=FZV[:, :, :, 0:64], in_=v4(tB))
            A(G, "tensor_copy", out=FZV[:, :, :, 64:128], in_=v4(vT))
            A(V, "tensor_tensor", out=LT[:, :, :, 0:64], in0=v4(tD), in1=v4(tI), op=ALU.mult)
            A(V, "tensor_tensor", out=LT[:, :, :, 64:128], in0=v4(tC), in1=v4(tI), op=ALU.mult)
            A(G, "tensor_tensor", out=FBK[:, :, :, 0:64], in0=v4(tD), in1=v4(tG), op=ALU.mult)
            A(G, "tensor_tensor", out=FBK[:, :, :, 64:128], in0=v4(tC), in1=v4(tG), op=ALU.mult)
            return L[:nsplit], L[nsplit:]

        def parallel_part(b):
            par = b % 2
            ZR, rk = ZR2[par], rk2[par]
            CS = (0, 1)
            Lp = []

            class _KB:
                @staticmethod
                def op(*a, **k):
                    Lp.append(Pp(kb_real.op, *a, **k))

            kb = _KB
            H = slice(64, 128)
            for c in CS:
                B0 = 3 * c
                pT1, pT2 = ps[B0], ps[B0 + 1]
                for j in range(4):
                    kb.op(PE, "matmul", out=pT1[:, j * 128:(j + 1) * 128], lhsT=FBK[:, j, c, :], rhs=identb[:], start=True,
                          stop=True)
                for j in range(4):
                    kb.op(PE, "matmul", out=pT2[:, j * 128:(j + 1) * 128], lhsT=FZV[:, j, c, :], rhs=identb[:], start=True,
                          stop=True)
                kb.op(S_, "activation", out=TBK[c], in_=pT1[:], func=AF.Copy)
                kb.op(S_, "activation", out=TZ[c], in_=pT2[0:64, :], func=AF.Copy)
                kb.op(S_, "activation", out=UV[c][64:128].rearrange("p h i -> p (h i)"), in_=pT2[64:128, :], func=AF.Copy)
            for c in CS:
                B0 = 3 * c
                for h in range(8):
                    j, hp = h // 2, (h % 2) * 64
                    kb.op(PE, "matmul", out=ps[B0 + h % 2][:, j * 128:(j + 1) * 128], lhsT=LT[hp:hp + 64, j, c, :],
                          rhs=RT[hp:hp + 64, j, c, :], start=True, stop=True)
                AAv = AAm[c].rearrange("p (j e) q -> p j e q", e=2)
                for e in range(2):
                    kb.op(V, "tensor_tensor", out=AAv[:, :, e, :], in0=ps[B0 + e][:].rearrange("p (j q) -> p j q", q=128),
                          in1=cst[:, C_M128:C_M128 + 128].unsqueeze(1).to_broadcast([128, 4, 128]), op=ALU.mult)
            for c in CS:
                B0 = 3 * c
                for h in range(8):
                    kb.op(PE, "matmul", out=ps[B0 + 2][0:64, h * 64:(h + 1) * 64], lhsT=AAm[c][0:64, h, 0:64],
                          rhs=identb[0:64, 0:64], start=True, stop=True)
                kb.op(S_, "activation", out=ATm[c].rearrange("p h q -> p (h q)"), in_=ps[B0 + 2][0:64, :], func=AF.Copy)
                kb.op(G, "tensor_tensor", out=Tm[c], in0=AAm[c][0:64, :, 0:64],
                      in1=identb[0:64, 0:64].unsqueeze(1).to_broadcast([64, 8, 64]), op=ALU.add)
            cur = {}
            for c in CS:
                cur[c] = ((lambda h, c=c: AAm[c][0:64, h, 0:64]), (lambda h, c=c: ATm[c][:, h, :]))
            for hop in range(1, 7):
                for c in CS:
                    B0 = 3 * c
                    Pc, PTc = cur[c]
                    pP, pPT, pD = ps[B0], ps[B0 + 1], ps[B0 + 2]
                    Tc = Tm[c]
                    if hop <= 4:
                        for h in range(8):
                            kb.op(PE, "matmul", out=pP[0:64, h * 64:(h + 1) * 64], lhsT=PTc(h), rhs=Pc(h), start=True, stop=True)
                    if hop <= 5:
                        for h in range(8):
                            kb.op(PE, "matmul", out=pPT[0:64, h * 64:(h + 1) * 64], lhsT=Pc(h), rhs=PTc(h), start=True, stop=True)
                    if hop >= 2:
                        for h in range(8):
                            kb.op(PE, "matmul", out=pD[0:64, h * 64:(h + 1) * 64], lhsT=PTc(h), rhs=Tc[:, h, :], start=True, stop=True)
                    nP, nPT = Pk[c][hop % 2], PTk[c][hop % 2]
                    if hop <= 4:
                        kb.op(S_, "activation", out=nP.rearrange("p h q -> p (h q)"), in_=pP[0:64, :], func=AF.Copy)
                    if hop <= 5:
                        kb.op(S_, "activation", out=nPT.rearrange("p h q -> p (h q)"), in_=pPT[0:64, :], func=AF.Copy)
                    if hop >= 2:
                        kb.op(V, "tensor_tensor", out=Tc.rearrange("p h q -> p (h q)"), in0=pD[0:64, :],
                              in1=Tc.rearrange("p h q -> p (h q)"), op=ALU.add)
                    cur[c] = ((lambda h, t=nP: t[:, h, :]), (lambda h, t=nPT: t[:, h, :]))
            for c in CS:
                B0 = 3 * c
                for h in range(8):
                    kb.op(PE, "matmul", out=ps[B0][0:64, h * 64:(h + 1) * 64], lhsT=AAm[c][64:128, h, 0:64],
                          rhs=UV[c][64:128, h, :], start=True, stop=True)
                kb.op(S_, "activation", out=XvT[c].rearrange("p h q -> p (h q)"), in_=ps[B0][0:64, :], func=AF.Copy)
            for c in CS:
                B0 = 3 * c
                for h in range(8):
                    kb.op(PE, "matmul", out=ps[B0 + 1][0:64, h * 64:(h + 1) * 64], lhsT=Tm[c][:, h, :], rhs=XvT[c][:, h, :],
                          start=True, stop=True)
                kb.op(S_, "activation", out=UVT[c].rearrange("p h q -> p (h q)"), in_=ps[B0 + 1][0:64, :], func=AF.Copy)
                for h in range(8):
                    j, hp = h // 2, (h % 2) * 64
                    kb.op(PE, "matmul", out=ps[B0 + 2][hp:hp + 64, j * 64:(j + 1) * 64], lhsT=TZ[c][:, h * 64:(h + 1) * 64],
                          rhs=Tm[c][:, h, :], start=True, stop=True)
                kb.op(V, "tensor_copy", out=ZR[:, :, c, 0:64], in_=ps[B0 + 2][:, 0:256].rearrange("p (j q) -> p j q", q=64))
            Lmain = Lp
            Lp = []

            class _KB2:
                @staticmethod
                def op(*a, **k):
                    Lp.append(Pp(kb_real.op, *a, **k))

            kb = _KB2
            for c in CS:
                B0 = 3 * c
                ts = slice(b * 128 + c * 64, b * 128 + c * 64 + 64)
                for j in range(4):
                    kb.op(PE, "matmul", out=ps[B0][H, 0:8], lhsT=rk[:, j, c * 64:(c + 1) * 64],
                          rhs=cst[:, C_HSEL + j * 8:C_HSEL + j * 8 + 8], start=(j == 0), stop=(j == 3))
                kb.op(PE, "matmul", out=ps[B0 + 1][H, :], lhsT=sgx[:, ts], rhs=gupb[:], start=True, stop=True)
                kb.op(V, "tensor_copy", out=srkc[c][H], in_=ps[B0][H, 0:8])
                kb.op(V, "tensor_tensor", out=G1[c][H], in0=ps[B0 + 1][H, :], in1=gnwb[H, 0, :], op=ALU.mult)
                kb.op(G, "tensor_tensor", out=G0[c][H], in0=UV[c][H], in1=srkc[c][H].unsqueeze(2).to_broadcast([64, 8, 64]),
                      op=ALU.mult)
                g0f = G0[c][H].rearrange("p h i -> p (h i)")
                kb.op(G, "tensor_tensor", out=g0f, in0=g0f, in1=gnwb[H, 1, :], op=ALU.add)
                kb.op(V, "tensor_tensor", out=g0f, in0=g0f, in1=ps[B0 + 1][H, :], op=ALU.mult)
            return Lmain, Lp

        def sequential_part(b):
            Ls = [[], []]
            par = b % 2
            ZR, wc, rk = ZR2[par], wc2[par], rk2[par]
            H = slice(64, 128)
            for c in (0, 1):
                A = lambda *a, _c=c, **k: Ls[_c].append(Pp(kb.op, *a, **k))
                B0 = 3 * c
                AA = AAm[c]
                ts = slice(b * 128 + c * 64, b * 128 + c * 64 + 64)
                for h in range(8):
                    j, hp = h // 2, (h % 2) * 64
                    A(PE, "matmul", out=ps[B0 + h % 2][:, j * 64:(j + 1) * 64], lhsT=ZR[hp:hp + 64, j, c, :],
                      rhs=STb[hp:hp + 64, j, :], start=True, stop=True)
                UVv = UV[c].rearrange("p (j e) i -> p j e i", e=2)
                UVTv = UVT[c].rearrange("p (j e) i -> p j e i", e=2)
                Y1v = Y1.rearrange("p (j e) i -> p j e i", e=2)
                for e in range(2):
                    A(V, "tensor_tensor", out=UVv[0:64, :, e, :],
                      in0=ps[B0 + e][0:64, 0:256].rearrange("p (j i) -> p j i", i=64), in1=UVTv[:, :, e, :], op=ALU.add)
                for e in range(2):
                    A(S_, "activation", out=Y1v[64:128, :, e, :],
                      in_=ps[B0 + e][64:128, 0:256].rearrange("p (j i) -> p j i", i=64), func=AF.Copy)
                for h in range(8):
                    j, hp = h // 2, (h % 2) * 64
                    A(PE, "matmul", out=ps[B0 + 2][hp:hp + 64, j * 64:(j + 1) * 64], lhsT=TBK[c][:, h * 64:(h + 1) * 64],
                      rhs=UV[c][:, h, :], start=True, stop=True)
                for j in range(4):
                    A(V, "scalar_tensor_tensor", out=STf[:, j, :], in0=STf[:, j, :], scalar=wc[:, j, c:c + 1],
                      in1=ps[B0 + 2][:, j * 64:(j + 1) * 64], op0=ALU.mult, op1=ALU.add)
                A(S_, "activation", out=STb[:], in_=STf[:], func=AF.Copy)
                for h in range(8):
                    A(PE, "matmul", out=ps[B0][64:128, h * 64:(h + 1) * 64], lhsT=AA[:, h, 64:128], rhs=UV[c][:, h, :],
                      start=True, stop=True)
                A(V, "tensor_tensor", out=yyc[c][H].rearrange("p h i -> p (h i)"), in0=ps[B0][H, :],
                  in1=Y1[H].rearrange("p h i -> p (h i)"), op=ALU.add)
            return Ls

        def post_part(b, c):
            L = []
            A = lambda *a, **k: L.append(Pp(kb.op, *a, **k))
            H = slice(64, 128)
            B0 = 3 * c
            ts = slice(b * 128 + c * 64, b * 128 + c * 64 + 64)
            yy, st8 = yyc[c], st8c[c]
            A(V, "tensor_reduce", out=st8[H, 0:8], in_=yy[H], axis=AX.X, op=ALU.add)
            A(G, "tensor_tensor", out=ysq[H], in0=yy[H], in1=yy[H], op=ALU.mult)
            A(V, "tensor_reduce", out=st8[H, 8:16], in_=ysq[H], axis=AX.X, op=ALU.add)
            A(V, "tensor_scalar", out=st8[H, 16:24], in0=st8[H, 0:8], scalar1=1.0 / 64, scalar2=None, op0=ALU.mult)
            A(V, "tensor_tensor", out=st8[H, 24:32], in0=st8[H, 16:24], in1=st8[H, 16:24], op=ALU.mult)
            A(V, "scalar_tensor_tensor", out=st8[H, 32:40], in0=st8[H, 8:16], scalar=1.0 / 64, in1=st8[H, 24:32],
              op0=ALU.mult, op1=ALU.subtract)
            A(V, "tensor_scalar", out=st8[H, 32:40], in0=st8[H, 32:40], scalar1=64e-5, scalar2=None, op0=ALU.add)
            A(S_, "activation", out=st8[H, 32:40], in_=st8[H, 32:40], func=AF.Sqrt)
            A(V, "reciprocal", out=st8[H, 40:48], in_=st8[H, 32:40])
            A(G, "tensor_tensor", out=yy[H], in0=yy[H], in1=st8[H, 16:24].unsqueeze(2).to_broadcast([64, 8, 64]),
              op=ALU.subtract)
            A(V, "tensor_tensor", out=yy[H], in0=yy[H], in1=st8[H, 40:48].unsqueeze(2).to_broadcast([64, 8, 64]),
              op=ALU.mult)
            yf = yy[H].rearrange("p h i -> p (h i)")
            A(G, "tensor_tensor", out=yf, in0=yf, in1=G1[c][H], op=ALU.mult)
            A(G, "tensor_tensor", out=yf, in0=yf, in1=G0[c][H].rearrange("p h i -> p (h i)"), op=ALU.add)
            ntail = len(L)
            for j in range(4):
                A(PE, "transpose", out=ps[7][:, j * 64:(j + 1) * 64],
                  in_=yy[H, 2 * j:2 * j + 2, :].rearrange("p h i -> p (h i)"), identity=cst[H, C_ID + 64:C_ID + 128])
            A(S_, "activation", out=catT[:, 0:4, ts], in_=ps[7][:, 0:256].rearrange("p (j t) -> p j t", t=64),
              func=AF.Copy)
            return L[:ntail], L[ntail:]

        def merged(La, Lb):
            na, nb = len(La), len(Lb)
            ib = 0
            for ia, f in enumerate(La):
                f()
                tgt = (ia + 1) * nb // max(na, 1)
                while ib < tgt:
                    Lb[ib]()
                    ib += 1
            while ib < nb:
                Lb[ib]()
                ib += 1

        p1, p2 = prep(0)
        merged(conv_section(), p1 + p2)
        pend = []
        for b in range(4):
            Lmain, Lepi = parallel_part(b)
            S0, S1 = sequential_part(b)
            p1, p2 = prep(b + 1) if b < 3 else ([], [])
            pp = p1 + p2
            nfirst = (len(pp) * 3) // 5
            merged(Lmain, pend + pp[:nfirst])
            merged(Lepi + S0 + S1, pp[nfirst:])
            e0, t0 = post_part(b, 0)
            e1, t1 = post_part(b, 1)
            pend = e0 + t0 + e1 + t1
        merged(pend, [])
        tokmajor_proj(lambda kc, b: catT[:, kc, b * 128:(b + 1) * 128], 4, woutS, 1, tmp, after_half)

    mset(G, carry[:], 0.0)
    mset(G, glu[:], 0.0)
    mset(G, STf[:], 0.0)
    mset(G, STb[:], 0.0)

    xv = x_d.rearrange("(n b p) d -> n p b d", b=4, p=128)
    ov = out_d.rearrange("(n b p) d -> n p b d", b=4, p=128)
    last = []

    def load_x(ti, blocks):
        for b in blocks:
            kb.op(SY, "dma_start", dma_key=f"xl{b}", out=xt[:, b, :], in_=xv[ti][:, b, :])

    load_x(0, range(4))
    prenorm(K_G1)
    for ti in range(NT):
        ffn(0, lambda blocks: prenorm(K_GM, blocks))
        mixer(ti, lambda blocks: prenorm(K_G2, blocks))

        def tail_hook(blocks, ti=ti):
            for b in blocks:
                o = kb.op(SY, "dma_start", dma_key=f"xs{b}", out=ov[ti][:, b, :], in_=xt[:, b, :])
                if ti == NT - 1:
                    last.append(o)
            if ti + 1 < NT:
                load_x(ti + 1, blocks)
                prenorm(K_G1, blocks)

        ffn(1, tail_hook)
    for o in last:
        kb.finish(o)
    kb.emit()
    st.close()
    return nc


def make_inputs(T, x, p):
    g = lambda k: np.asarray(p[k], np.float32)[0]
    cols = np.zeros((128, K_END), np.float32)
    cols[:, K_MU:K_MU + 14] = _fm(g("shift_mu"), 14)
    cols[:, K_W0:K_W0 + 4] = _fm(g("w0"), 4)
    cols[:, K_A0:K_A0 + 4] = _fm(g("a0"), 4)
    cols[:, K_KK:K_KK + 4] = _fm(g("k_k"), 4)
    cols[:, K_KA:K_KA + 4] = _fm(g("k_a"), 4)
    cols[:, K_RK:K_RK + 4] = _fm(g("r_k").reshape(-1), 4)
    cols[:, K_CB:K_CB + 4] = _fm(g("conv_b"), 4)
    cols[:, K_LNW:K_LNW + 4] = _fm(g("conv_ln_w"), 4)
    cols[:, K_LNB:K_LNB + 4] = _fm(g("conv_ln_b"), 4)
    cols[:, K_G1:K_G1 + 8] = _fm(g("ffn1_norm_pre"), 8)
    cols[:, K_GM:K_GM + 8] = _fm(g("mix_norm_pre"), 8)
    cols[:, K_G2:K_G2 + 8] = _fm(g("ffn2_norm_pre"), 8)
    dw = g("conv_dw")
    cols[:, K_DW:K_DW + 124] = dw.T.reshape(4, 128, 31).transpose(1, 0, 2).reshape(128, 124)
    rows = np.zeros((5, D), np.float32)
    rows[0] = g("ffn1_norm_post")
    rows[1] = g("mix_norm_post")
    rows[2] = g("ffn2_norm_post")
    rows[3, 0:512] = g("gn_w")
    rows[3, 512:1024] = g("gn_b")
    lora = np.concatenate([g("w_up"), g("a_up")], 0)
    shared = {
        "wgu1": g("ffn1_w_gu"), "wgu2": g("ffn2_w_gu"), "wdn1": g("ffn1_w_down"), "wdn2": g("ffn2_w_down"),
        "win": g("w_in"), "wout": g("w_out"), "cols": cols, "consts": _consts(), "rows": rows,
        "lora": np.ascontiguousarray(lora), "gup": g("g_up"),
    }
    return shared


_CACHE = {}


def kernel(**inputs):
    x = np.asarray(inputs["x"], np.float32)
    B, T, _ = x.shape
    shared = make_inputs(T, x, inputs)
    if T not in _CACHE:
        _CACHE[T] = build(T)
    nc = _CACHE[T]
    in_maps = []
    for b in range(B):
        m = dict(shared)
        m["x"] = np.ascontiguousarray(x[b])
        in_maps.append(m)
    res = run_bass_kernel_spmd(nc, in_maps, core_ids=list(range(B)))
    return np.stack([np.asarray(r["out"], np.float32) for r in res.results], 0)
```

```python
import math
from functools import partial as functools_partial
from contextlib import ExitStack
import numpy as np
import concourse.bass as bass
import concourse.mybir as mybir
from concourse.bass_utils import run_bass_kernel_spmd

F32 = mybir.dt.float32
BF16 = mybir.dt.bfloat16
AF = mybir.ActivationFunctionType
ALU = mybir.AluOpType
AX = mybir.AxisListType

D = 1024
DFF = 2816
NF = 22
DRW = 512
DS = math.exp(-0.5)
TT = 512
WRITE_KEYS = ("out", "accum_out", "ap")
ENGS = ("sync", "tensor", "scalar", "vector", "gpsimd")


def _is_ap(v):
    return hasattr(v, "tensor") and hasattr(v, "ap") and hasattr(v, "offset")


def _region(ap):
    t = ap.tensor
    es = mybir.dt.size(ap.dtype)
    dims = [(int(s), int(c)) for s, c in ap.ap]
    off = int(ap.offset) * es
    space = str(ap.space)
    if "SB" in space or "PS" in space.upper():
        shp = [int(s) for s in t.shape]
        pstride = int(np.prod(shp[1:])) * mybir.dt.size(t.dtype)
        p0 = off // pstride
        f0 = off % pstride
        pc = dims[0][1]
        lo = hi = 0
        for s, c in dims[1:]:
            e = s * (c - 1) * es
            if e < 0:
                lo += e
            else:
                hi += e
        if "SB" not in space:
            return t.name, (p0 // 32 * 32, (p0 + pc + 31) // 32 * 32, 0, pstride)
        return t.name, (p0, p0 + pc, f0 + lo, f0 + hi + es)
    lo = hi = 0
    for s, c in dims:
        e = s * (c - 1) * es
        if e < 0:
            lo += e
        else:
            hi += e
    return "dram:" + t.name, (0, 1, off + lo, off + hi + es)


def _ovl(a, b):
    return a[0] < b[1] and b[0] < a[1] and a[2] < b[3] and b[2] < a[3]


def _contains(a, b):
    return a[0] <= b[0] and a[1] >= b[1] and a[2] <= b[2] and a[3] >= b[3]


class Op:
    __slots__ = ("eng", "meth", "kw", "deps", "marked", "dma_key", "tok", "idx")


class KB:
    def __init__(self, nc):
        self.nc = nc
        self.ops = {e: [] for e in ENGS}
        self.acc = {}
        self.dma_cnt = {}
        self.final = []
        self.capture = None

    def op(self, eng, meth, dma_key=None, **kw):
        if self.capture is not None:
            self.capture.append((eng, meth, dma_key, kw))
            return None
        o = Op()
        o.eng, o.meth, o.kw, o.marked, o.dma_key = eng, meth, kw, False, dma_key
        o.idx = len(self.ops[eng])
        is_dma = meth == "dma_start"
        if is_dma:
            assert dma_key is not None
            self.dma_cnt[dma_key] = self.dma_cnt.get(dma_key, 0) + 16
            o.tok = (dma_key, self.dma_cnt[dma_key])
        else:
            o.tok = None
        deps = {}
        accs = []
        for name, v in kw.items():
            if not _is_ap(v):
                continue
            kind = "w" if name in WRITE_KEYS else "r"
            tn, box = _region(v)
            accs.append((tn, box, kind))
            d = self.acc.get(tn)
            if not d:
                continue
            for (k2, box2), (o2, _) in d.items():
                if kind == "r" and k2[1] == "r":
                    continue
                if o2 is o:
                    continue
                if _ovl(box, box2):
                    if o2.eng == "tensor" and eng == "tensor" and o2.meth != "dma_start" and not is_dma:
                        continue
                    key = (o2.eng, o2.dma_key)
                    if key not in deps or deps[key].idx < o2.idx:
                        deps[key] = o2
        for o2 in deps.values():
            if o2.meth != "dma_start":
                o2.marked = True
        o.deps = list(deps.values())
        me = (eng, dma_key)
        for tn, box, kind in accs:
            d = self.acc.setdefault(tn, {})
            if kind == "w":
                for k in [k for k in d if _contains(box, k[1])]:
                    del d[k]
            d[((me, kind), box)] = (o, kind)
        self.ops[eng].append(o)
        return o

    def finish(self, o):
        self.final.append(o)

    def emit(self):
        nc = self.nc
        for e in ENGS:
            c = 0
            for o in self.ops[e]:
                if o.meth != "dma_start" and o.marked:
                    c += 1
                    o.tok = (e, c)
        with ExitStack() as st:
            sems = {}
            for e in ENGS:
                sems[e] = st.enter_context(nc.semaphore("s_" + e))
            for k in self.dma_cnt:
                sems[k] = st.enter_context(nc.semaphore("d_" + k))
            block = st.enter_context(nc.Block())

            def run(eng, ename):
                waited = {}
                for o in self.ops[ename]:
                    for o2 in o.deps:
                        k, v = o2.tok
                        if waited.get(k, 0) < v:
                            eng.wait_ge(sems[k], v)
                            waited[k] = v
                    inst = getattr(eng, o.meth)(**o.kw)
                    if o.meth == "dma_start":
                        inst.then_inc(sems[o.dma_key], 16)
                    elif o.marked:
                        inst.then_inc(sems[ename], 1)
                if ename == "sync":
                    for o2 in self.final:
                        k, v = o2.tok
                        if waited.get(k, 0) < v:
                            eng.wait_ge(sems[k], v)
                            waited[k] = v

            @block.sync
            def _(eng):
                run(eng, "sync")

            @block.tensor
            def _(eng):
                run(eng, "tensor")

            @block.scalar
            def _(eng):
                run(eng, "scalar")

            @block.vector
            def _(eng):
                run(eng, "vector")

            @block.gpsimd
            def _(eng):
                run(eng, "gpsimd")


C_ID = 0
C_M128 = 128
C_MAT = 256
C_BONES = 320
C_ONESM = 448
C_RESET = 576
C_HSEL = 704
C_NEGH = 736
C_END = 737


def _consts():
    c = np.zeros((128, C_END), np.float32)
    c[:, C_ID:C_ID + 128] = np.eye(128)
    p = np.arange(128)[:, None] % 64
    q = np.arange(128)[None, :]
    c[:, C_M128:C_M128 + 128] = np.where(q < 64, p < q, p <= (q - 64))
    t = np.arange(128)[:, None] % 64
    j = np.arange(64)[None, :]
    c[:, C_MAT:C_MAT + 64] = (j < t)
    c[:, C_BONES:C_BONES + 128] = (np.arange(128)[:, None] // 64 == np.arange(128)[None, :] // 64)
    c[:, C_ONESM:C_ONESM + 128] = 1.0 / 512.0
    r = np.ones((128, 128), np.float32)
    r[:, 0::64] = 0.0
    c[:, C_RESET:C_RESET + 128] = r
    hs = np.zeros((128, 4, 8), np.float32)
    for jj in range(4):
        hs[0:64, jj, 2 * jj] = 1.0
        hs[64:128, jj, 2 * jj + 1] = 1.0
    c[:, C_HSEL:C_HSEL + 32] = hs.reshape(128, 32)
    c[:, C_NEGH] = -0.5
    return c


K_MU = 0
K_W0 = 14
K_A0 = 18
K_KK = 22
K_KA = 26
K_RK = 30
K_CB = 34
K_LNW = 38
K_LNB = 42
K_G1 = 46
K_GM = 54
K_G2 = 62
K_DW = 70
K_END = 70 + 124


def _fm(v, n):
    return np.ascontiguousarray(np.asarray(v, np.float32).reshape(n, 128).T)


def build(T, stage=99, sub=99):
    NT = T // TT
    nc = bass.Bass("TRN2", target_bir_lowering=False)
    kb = KB(nc)
    kb_real = kb

    def din(name, shape, dt=F32):
        return nc.dram_tensor(name, list(shape), dt, kind="ExternalInput").ap()

    x_d = din("x", [T, D])
    out_d = nc.dram_tensor("out", [T, D], F32, kind="ExternalOutput").ap()
    wgu_d = [din("wgu1", [D, 2 * DFF]), din("wgu2", [D, 2 * DFF])]
    wdn_d = [din("wdn1", [DFF, D]), din("wdn2", [DFF, D])]
    win_d = din("win", [D, 2816])
    wout_d = din("wout", [D, D])
    cols_d = din("cols", [128, K_END])
    const_d = din("consts", [128, C_END])
    rows_d = din("rows", [5, D])
    lora_d = din("lora", [128, 512])
    gup_d = din("gup", [128, 512])

    def dscr(name, shape):
        return nc.dram_tensor(name, list(shape), BF16, kind="Internal").ap()

    wguS = [dscr("wguS1", [2 * NF, 128, 1024]), dscr("wguS2", [2 * NF, 128, 1024])]
    wdnS = [dscr("wdnS1", [2, 11, 128, 2, 512]), dscr("wdnS2", [2, 11, 128, 2, 512])]
    winS = dscr("winS", [NF, 128, 1024])
    woutS = dscr("woutS", [2, 4, 128, 2, 512])

    st = ExitStack()

    def sb(name, shape, dt=F32):
        return st.enter_context(nc.sbuf_tensor(name, list(shape), dt))

    cst = sb("cst", [128, C_END])
    cols = sb("cols_sb", [128, K_END])
    gcur = sb("gcur", [128, D])
    gnwb = sb("gnwb", [128, 2, 512])
    identb = sb("identb", [128, 128], BF16)
    bonesb = sb("bonesb", [128, 128], BF16)
    lorab = sb("lorab", [128, 512], BF16)
    gupb = sb("gupb", [128, 512], BF16)
    diag = sb("diag", [128, 124, 128], BF16)
    xt = sb("xt", [128, 4, D])
    slotA = [sb(f"slotA{i}", [128, 8, 256], BF16) for i in range(3)]
    slotB = [sb(f"slotB{i}", [128, 2, 512], BF16) for i in range(3)]
    sstat = sb("sstat", [128, 32])
    ARENA = 121 * 1024
    arena = sb("arena", [128, ARENA // 4])
    ident = cst[:, C_ID:C_ID + 128]

    class Carver:
        def __init__(self, start=18 * 1024, limit=ARENA, nxt=None):
            self.off = start
            self.limit = limit
            self.nxt = nxt

        def get(self, shape, dt=F32):
            n = int(np.prod(shape[1:])) * mybir.dt.size(dt)
            n = (n + 31) // 32 * 32
            if self.off + n > self.limit and self.nxt is not None:
                return self.nxt.get(shape, dt)
            assert self.off + n <= self.limit, (self.off, n, self.limit)
            a = arena[:, self.off // 4:(self.off + n) // 4]
            self.off += n
            if dt != F32:
                a = a.bitcast(dt)
            a = a[0:shape[0], 0:int(np.prod(shape[1:]))]
            if len(shape) == 3:
                a = a.rearrange("p (a b) -> p a b", b=shape[2])
            elif len(shape) == 4:
                a = a.rearrange("p (a b c) -> p a b c", b=shape[2], c=shape[3])
            return a

    _c0 = Carver(start=0)
    xs = [_c0.get([128, D]), _c0.get([128, D])]
    xnT = _c0.get([128, 8, TT], BF16)
    gjunk = _c0.get([128, D], BF16)
    assert _c0.off == 18 * 1024
    ps = [st.enter_context(nc.psum_tensor(f"ps{i}", [128, 512], F32)) for i in range(8)]

    V, S_, G, PE, SY = "vector", "scalar", "gpsimd", "tensor", "sync"

    kb.op(SY, "dma_start", dma_key="c0", out=cst[:], in_=const_d)
    kb.op(SY, "dma_start", dma_key="c1", out=cols[:], in_=cols_d)
    kb.op(SY, "dma_start", dma_key="c3", out=gnwb[:].rearrange("p a b -> p (a b)"),
          in_=rows_d[3:4, :].partition_broadcast(128))
    cv = Carver(start=0)
    stg0 = cv.get([128, 512])
    stg1 = cv.get([128, 512])
    kb.op(SY, "dma_start", dma_key="c4", out=stg0, in_=lora_d)
    kb.op(SY, "dma_start", dma_key="c5", out=stg1, in_=gup_d)
    kb.op(V, "tensor_copy", out=lorab[:], in_=stg0)
    kb.op(V, "tensor_copy", out=gupb[:], in_=stg1)
    kb.op(V, "tensor_copy", out=identb[:], in_=ident)
    kb.op(V, "tensor_copy", out=bonesb[:], in_=cst[:, C_BONES:C_BONES + 128])
    for i in range(124):
        kb.op(V if i % 2 else G, "tensor_scalar", out=diag[:, i, :], in0=ident,
              scalar1=cols[:, K_DW + i:K_DW + i + 1], scalar2=None, op0=ALU.mult)

    NSTG, NACC = 4, 3
    stage32 = [cv.get([128, 2048]) for _ in range(NSTG)]
    accb = [cv.get([128, 11, 1024], BF16) for _ in range(NACC)]
    cctr = [0]
    sctr = [0]
    hctr = [0]

    def cast(out, in_):
        eng = (S_, G, V)[cctr[0] % 3]
        cctr[0] += 1
        if eng == S_:
            kb.op(S_, "activation", out=out, in_=in_, func=AF.Copy)
        else:
            kb.op(eng, "tensor_copy", out=out, in_=in_)

    pend_st = []

    def flush_st():
        while pend_st:
            pend_st.pop(0)()

    def conv_A(src, dstS, nhalves):
        for qq in range(nhalves * 2):
            ai = hctr[0] % NACC
            a = accb[ai]
            hctr[0] += 1
            av = a.rearrange("p f (k c) -> p f k c", c=128)
            for kc in range(8):
                i = sctr[0] % NSTG
                sctr[0] += 1
                stg = stage32[i][:, 0:1408]
                kb.op(SY, "dma_start", dma_key=f"cv{i}", out=stg, in_=src[kc * 128:(kc + 1) * 128, qq * 1408:(qq + 1) * 1408])
                cast(av[:, :, kc, :], stg.rearrange("p (f c) -> p f c", c=128))
                if kc == 3:
                    flush_st()
            pend_st.append(lambda ai=ai, qq=qq, a=a, dstS=dstS: kb.op(
                SY, "dma_start", dma_key=f"csA{ai}", out=dstS[qq * 11:qq * 11 + 11].rearrange("f p x -> p f x"), in_=a[:]))

    def conv_B(src, dstS, nk2):
        sv = src.rearrange("(k s p) n -> k p s n", s=2, p=128)
        for k2 in range(nk2):
            i = sctr[0] % NSTG
            sctr[0] += 1
            stg = stage32[i]
            ai = hctr[0] % NACC
            a = accb[ai]
            hctr[0] += 1
            bst = a.rearrange("p f x -> p (f x)")[:, 0:2048]
            kb.op(SY, "dma_start", dma_key=f"cv{i}", out=stg.rearrange("p (s n) -> p s n", s=2), in_=sv[k2])
            cast(bst.rearrange("p (h s n) -> p h s n", h=2, s=2), stg.rearrange("p (s h n) -> p h s n", s=2, h=2))
            flush_st()
            for hf in range(2):
                pend_st.append(lambda ai=ai, hf=hf, k2=k2, bst=bst, dstS=dstS: kb.op(
                    SY, "dma_start", dma_key=f"csB{ai}_{hf}", out=dstS[hf, k2].rearrange("p s n -> p (s n)"),
                    in_=bst[:, hf * 1024:(hf + 1) * 1024]))

    conv_A(wgu_d[0], wguS[0], 2)
    conv_B(wdn_d[0], wdnS[0], 11)
    conv_A(win_d, winS, 1)
    conv_B(wout_d, woutS, 4)
    conv_A(wgu_d[1], wguS[1], 2)
    conv_B(wdn_d[1], wdnS[1], 11)
    flush_st()

    actr = [0]
    bctr = [0]

    def loadA(src):
        i = actr[0] % 3
        actr[0] += 1
        kb.op(SY, "dma_start", dma_key=f"A{i}", out=slotA[i][:], in_=src)
        return slotA[i]

    def loadA2(src):
        i = actr[0] % 3
        actr[0] += 1
        v = slotA[i][:].rearrange("p k c -> p (k c)").rearrange("p (g x) -> p g x", g=2)
        kb.op(SY, "dma_start", dma_key=f"A{i}", out=v, in_=src)
        return slotA[i][:].rearrange("p k c -> p (k c)").rearrange("p (g k c) -> p g k c", g=2, k=8)

    def loadB(src):
        i = bctr[0] % 3
        bctr[0] += 1
        kb.op(SY, "dma_start", dma_key=f"B{i}", out=slotB[i][:], in_=src)
        return slotB[i]

    def rstd_from_ss(ss_ap, out_ap, n, eps, npart=128):
        kb.op(V, "tensor_scalar", out=out_ap, in0=ss_ap, scalar1=1.0 / n, scalar2=eps, op0=ALU.mult, op1=ALU.add)
        kb.op(S_, "activation", out=out_ap, in_=out_ap, func=AF.Sqrt)
        kb.op(V, "reciprocal", out=out_ap, in_=out_ap)

    def prenorm(gcol, blocks=(0, 1, 2, 3)):
        junk = gjunk
        for b in blocks:
            kb.op(S_, "activation", out=junk, in_=xt[:, b, :], func=AF.Square, accum_out=sstat[:, b:b + 1])
            rstd_from_ss(sstat[:, b:b + 1], sstat[:, 4 + b:5 + b], D, 1e-6)
            xb = xs[b % 2]
            kb.op(V, "tensor_scalar", out=xb, in0=xt[:, b, :], scalar1=sstat[:, 4 + b:5 + b], scalar2=None,
                  op0=ALU.mult)
            for hf in range(2):
                pb = ps[(b % 2) * 2 + hf]
                for q in range(4):
                    kc = hf * 4 + q
                    kb.op(PE, "transpose", out=pb[:, q * 128:(q + 1) * 128], in_=xb[:, kc * 128:(kc + 1) * 128],
                          identity=ident)
                kb.op(V, "tensor_tensor", out=xnT[:, hf * 4:hf * 4 + 4, b * 128:(b + 1) * 128],
                      in0=pb[:].rearrange("p (a b) -> p a b", b=128),
                      in1=cols[:, gcol + hf * 4:gcol + hf * 4 + 4].unsqueeze(2).to_broadcast([128, 4, 128]),
                      op=ALU.mult)

    def load_gain(gi):
        kb.op(SY, "dma_start", dma_key="gc", out=gcur[:], in_=rows_d[gi:gi + 1, :].partition_broadcast(128))
        if gi != 1:
            kb.op(V, "tensor_scalar", out=gcur[:], in0=gcur[:], scalar1=0.5, scalar2=None, op0=ALU.mult)

    def postnorm_residual(gi, banks, tmp, blocks):
        for b in blocks:
            c0 = 16 + 4 * b
            for hf in range(2):
                kb.op(S_, "activation", out=gjunk[:, hf * 512:(hf + 1) * 512], in_=banks[b][hf][:], func=AF.Square,
                      accum_out=sstat[:, c0 + hf:c0 + hf + 1])
            kb.op(V, "tensor_tensor", out=sstat[:, c0 + 2:c0 + 3], in0=sstat[:, c0:c0 + 1], in1=sstat[:, c0 + 1:c0 + 2], op=ALU.add)
            rstd_from_ss(sstat[:, c0 + 2:c0 + 3], sstat[:, c0 + 3:c0 + 4], D, 1e-6)
            for hf in range(2):
                t = tmp[((b % 2) * 2 + hf) % len(tmp)]
                kb.op(V, "scalar_tensor_tensor", out=t, in0=banks[b][hf][:], scalar=sstat[:, c0 + 3:c0 + 4],
                      in1=gcur[:, hf * 512:(hf + 1) * 512], op0=ALU.mult, op1=ALU.mult)
                kb.op(G, "tensor_tensor", out=xt[:, b, hf * 512:(hf + 1) * 512], in0=xt[:, b, hf * 512:(hf + 1) * 512],
                      in1=t, op=ALU.add)

    def tokmajor_proj(lhs_fn, nk2, wS, gi, tmp, after_half=None):
        banks = {b: [ps[b * 2 + hf] for hf in range(2)] for b in range(4)}
        for hf in range(2):
            for k2 in range(nk2):
                sl = loadB(wS[hf, k2])
                for s in range(2):
                    kc = k2 * 2 + s
                    for b in range(4):
                        kb.op(PE, "matmul", out=banks[b][hf][:], lhsT=lhs_fn(kc, b), rhs=sl[:, s, :],
                              start=(kc == 0), stop=(kc == nk2 * 2 - 1))
        for pair in ((0, 1), (2, 3)):
            lists = []
            for b in pair:
                kb.capture = []
                postnorm_residual(gi, banks, tmp, (b,))
                if after_half is not None:
                    after_half((b,))
                lists.append(kb.capture)
                kb.capture = None
            for i in range(max(len(l) for l in lists)):
                for l in lists:
                    if i < len(l):
                        eng, meth, dk, kw = l[i]
                        kb.op(eng, meth, dma_key=dk, **kw)

    def ffn(l, after_half=None):
        cvf = Carver()
        hT = cvf.get([128, NF, TT], BF16)
        sg = [cvf.get([128, TT]) for _ in range(2)]
        tmp = [cvf.get([128, 512]) for _ in range(4)]
        load_gain(0 if l == 0 else 2)
        for f in range(NF):
            sl = loadA(wguS[l].rearrange("(g f) p x -> f p g x", g=2)[f]).rearrange("p k (g c) -> p g k c", g=2) if False else loadA2(wguS[l].rearrange("(g f) p x -> f p g x", g=2)[f])
            pg = ps[4 + (f % 2) * 2]
            pu = ps[5 + (f % 2) * 2]
            for kc in range(8):
                kb.op(PE, "matmul", out=pg[:], lhsT=sl[:, 0, kc, :], rhs=xnT[:, kc, :], start=(kc == 0), stop=(kc == 7))
            for kc in range(8):
                kb.op(PE, "matmul", out=pu[:], lhsT=sl[:, 1, kc, :], rhs=xnT[:, kc, :], start=(kc == 0), stop=(kc == 7))
            kb.op(S_, "activation", out=sg[f % 2], in_=pg[:], func=AF.Silu)
            kb.op(V, "tensor_tensor", out=hT[:, f, :], in0=sg[f % 2], in1=pu[:], op=ALU.mult)
        tokmajor_proj(lambda kc, b: hT[:, kc, b * 128:(b + 1) * 128], 11, wdnS[l], 0 if l == 0 else 2, tmp, after_half)

    carry = sb("carry", [128, 14])
    glu = sb("glu", [128, 4, 30 + TT], BF16)
    STf = sb("STf", [128, 4, 64])
    STb = sb("STb", [128, 4, 64], BF16)

    def mset(eng, ap, val):
        kb.op(eng, "memset", ap=ap, constant=val)

    def mixer(ti, after_half=None):
        cvm = Carver()
        g = cvm.get
        praw = [g([128, 513]) for _ in range(2)]
        dtmp = g([128, 512])
        xwa = g([128, TT])
        xg = g([128, TT])
        cT = g([128, 4, TT])
        csq = g([128, TT])
        mean = g([128, TT])
        rstd = g([128, TT])
        tmp = [g([128, 512]) for _ in range(2)]
        endA = cvm.off
        rkvT = g([128, 12, TT])
        sgx = g([128, TT], BF16)
        xwab = g([128, TT], BF16)
        catT = g([128, 8, TT], BF16)
        endP = cvm.off
        load_gain(1)
        dst_of = {}
        for c in range(12):
            dst_of[c] = rkvT[:, c, :]
        dst_of[12] = xwa
        dst_of[13] = xg
        order = list(range(14)) + [14, 18, 15, 19, 16, 20, 17, 21]
        slots = {}
        sigt = {}
        for n, c in enumerate(order):
            c2 = c // 2
            if c2 not in slots:
                slots[c2] = loadA2(winS[2 * c2:2 * c2 + 2].rearrange("g p x -> p g x"))
            sl = slots[c2]
            pb = ps[4 + n % 4]
            for kc in range(8):
                kb.op(PE, "matmul", out=pb[:], lhsT=sl[:, c % 2, kc, :], rhs=xnT[:, kc, :],
                      start=(kc == 0), stop=(kc == 7))
            if c < 14:
                pr = praw[c % 2]
                kb.op(S_, "activation", out=pr[:, 1:513], in_=pb[:], func=AF.Copy)
                kb.op(G, "tensor_copy", out=pr[:, 0:1], in_=carry[:, c:c + 1])
                kb.op(G, "tensor_tensor", out=dtmp, in0=pr[:, 0:512], in1=pr[:, 1:513], op=ALU.subtract)
                kb.op(V, "scalar_tensor_tensor", out=dst_of[c], in0=dtmp, scalar=cols[:, K_MU + c:K_MU + c + 1],
                      in1=pr[:, 1:513], op0=ALU.mult, op1=ALU.add)
                kb.op(G, "tensor_copy", out=carry[:, c:c + 1], in_=pr[:, 512:513])
            elif c < 18:
                sigt[c] = pb
            else:
                kb.op(S_, "activation", out=tmp[c % 2], in_=pb[:], func=AF.Sigmoid)
                kb.op(V, "tensor_tensor", out=glu[:, c - 18, 30:30 + TT], in0=tmp[c % 2], in1=sigt[c - 4][:], op=ALU.mult)
        kb.op(S_, "activation", out=xwab[0:64, :], in_=xwa[0:64, :], func=AF.Tanh)
        kb.op(V, "tensor_copy", out=xwab[64:128, :], in_=xwa[64:128, :])
        kb.op(S_, "activation", out=sgx, in_=xg, func=AF.Sigmoid)
        def conv_section():
            Lc = []

            class _KBc:
                @staticmethod
                def op(*a, **k):
                    Lc.append(functools_partial(kb_real.op, *a, **k))

            kb = _KBc
            for j in range(4):
                pb = ps[4 + j % 2]
                for tau in range(31):
                    kb.op(PE, "matmul", out=pb[:], lhsT=diag[:, j * 31 + tau, :], rhs=glu[:, j, tau:tau + TT], start=(tau == 0),
                          stop=(tau == 30))
                kb.op(V, "tensor_scalar", out=cT[:, j, :], in0=pb[:], scalar1=cols[:, K_CB + j:K_CB + j + 1], scalar2=None,
                      op0=ALU.add)
            for j in range(4):
                kb.op(G, "tensor_copy", out=glu[:, j, 0:30], in_=glu[:, j, TT:TT + 30])
            onesm = cst[:, C_ONESM:C_ONESM + 128]
            pm, pv = ps[2], ps[3]
            for j in range(4):
                kb.op(PE, "matmul", out=pm[:], lhsT=onesm, rhs=cT[:, j, :], start=(j == 0), stop=(j == 3))
            for j in range(4):
                kb.op(S_, "activation", out=csq, in_=cT[:, j, :], func=AF.Square)
                kb.op(PE, "matmul", out=pv[:], lhsT=onesm, rhs=csq, start=(j == 0), stop=(j == 3))
            kb.op(S_, "activation", out=mean, in_=pm[:], func=AF.Copy)
            kb.op(V, "tensor_tensor", out=rstd, in0=mean, in1=mean, op=ALU.mult)
            kb.op(V, "tensor_tensor", out=rstd, in0=pv[:], in1=rstd, op=ALU.subtract)
            kb.op(V, "tensor_scalar", out=rstd, in0=rstd, scalar1=1e-5, scalar2=None, op0=ALU.add)
            kb.op(S_, "activation", out=rstd, in_=rstd, func=AF.Sqrt)
            kb.op(V, "reciprocal", out=rstd, in_=rstd)
            for j in range(4):
                kb.op(V, "tensor_tensor", out=csq, in0=cT[:, j, :], in1=mean, op=ALU.subtract)
                kb.op(V, "tensor_tensor", out=csq, in0=csq, in1=rstd, op=ALU.mult)
                kb.op(S_, "activation", out=catT[:, 4 + j, :], in_=csq, func=AF.Silu,
                      scale=cols[:, K_LNW + j:K_LNW + j + 1], bias=cols[:, K_LNB + j:K_LNB + j + 1])

            return Lc

        from functools import partial as Pp
        _tail = Carver(start=endP)
        gP = _tail.get
        gA = Carver(start=0, limit=endA, nxt=_tail).get
        g = gP

        def t4():
            return g([128, 4, 128])

        tA, tB, tC, tD, tE, tF, tG, tH, tI, sgdB, aB = [t4() for _ in range(11)]
        rk2 = [t4(), t4()]
        sqb = g([128, 4, 128], BF16)
        LT = g([128, 4, 2, 128], BF16)
        RT = g([128, 4, 2, 128], BF16)
        FBK = g([128, 4, 2, 128], BF16)
        FZV = g([128, 4, 2, 128], BF16)
        ZR2 = [g([128, 4, 2, 128], BF16) for _ in range(2)]
        wc2 = [g([128, 4, 2]) for _ in range(2)]
        g = gA
        TBK = [g([128, 512], BF16) for _ in range(2)]
        TZ = [g([64, 512], BF16) for _ in range(2)]
        UV = [g([128, 8, 64], BF16) for _ in range(2)]
        AAm = [g([128, 8, 128], BF16) for _ in range(2)]
        ATm = [g([64, 8, 64], BF16) for _ in range(2)]
        Pk = [[g([64, 8, 64], BF16) for _ in range(2)] for _ in range(2)]
        PTk = [[g([64, 8, 64], BF16) for _ in range(2)] for _ in range(2)]
        Tm = [g([64, 8, 64], BF16) for _ in range(2)]
        XvT = [g([64, 8, 64], BF16) for _ in range(2)]
        UVT = [g([64, 8, 64]) for _ in range(2)]
        Y1 = g([128, 8, 64])
        ysq = g([128, 8, 64])
        yyc = [g([128, 8, 64]) for _ in range(2)]
        G1 = [g([128, 512]) for _ in range(2)]
        G0 = [g([128, 8, 64]) for _ in range(2)]
        st8c = [g([128, 48]) for _ in range(2)]
        srkc = [g([128, 8]) for _ in range(2)]

        def v4(ap):
            return ap.rearrange("p j (c t) -> p j c t", c=2)

        def fl(ap):
            return ap.rearrange("p j t -> p (j t)")

        def prep(b):
            L = []
            A = lambda *a, **k: L.append(Pp(kb.op, *a, **k))
            par = b % 2
            ZR, wc, rk = ZR2[par], wc2[par], rk2[par]
            bs = slice(b * 128, (b + 1) * 128)
            rT, kT, vT = rkvT[:, 0:4, bs], rkvT[:, 4:8, bs], rkvT[:, 8:12, bs]
            psd, psa, pn = ps[6], ps[6], ps[6]
            for j in range(4):
                A(PE, "matmul", out=psd[:, j * 128:(j + 1) * 128], lhsT=lorab[0:64, j * 128:(j + 1) * 128],
                  rhs=xwab[0:64, bs], start=True, stop=True)
            for j in range(4):
                A(S_, "activation", out=sgdB[:, j, :], in_=psd[:, j * 128:(j + 1) * 128], func=AF.Sigmoid,
                  bias=cols[:, K_W0 + j:K_W0 + j + 1])
            for j in range(4):
                A(PE, "matmul", out=psa[:, j * 128:(j + 1) * 128], lhsT=lorab[64:128, j * 128:(j + 1) * 128],
                  rhs=xwab[64:128, bs], start=True, stop=True)
            for j in range(4):
                A(S_, "activation", out=aB[:, j, :], in_=psa[:, j * 128:(j + 1) * 128], func=AF.Sigmoid,
                  bias=cols[:, K_A0 + j:K_A0 + j + 1])
            for j in range(4):
                A(V, "tensor_scalar", out=tA[:, j, :], in0=kT[:, j, :], scalar1=cols[:, K_KK + j:K_KK + j + 1],
                  scalar2=None, op0=ALU.mult)
            A(G, "tensor_tensor", out=sqb, in0=tA, in1=tA, op=ALU.mult)
            A(PE, "matmul", out=pn[:], lhsT=bonesb[:], rhs=fl(sqb), start=True, stop=True)
            A(V, "tensor_scalar", out=fl(tB), in0=pn[:], scalar1=1e-12, scalar2=None, op0=ALU.max)
            nsplit = len(L)
            A(S_, "activation", out=fl(tB), in_=fl(tB), func=AF.Ln)
            A(S_, "activation", out=fl(tB), in_=fl(tB), func=AF.Exp, scale=-0.5)
            A(V, "tensor_tensor", out=tA, in0=tA, in1=tB, op=ALU.mult)
            for j in range(4):
                A(V, "tensor_scalar", out=tC[:, j, :], in0=aB[:, j, :], scalar1=-1.0,
                  scalar2=cols[:, K_KA + j:K_KA + j + 1], op0=ALU.add, op1=ALU.mult)
            A(V, "scalar_tensor_tensor", out=tC, in0=tC, scalar=1.0, in1=kT, op0=ALU.add, op1=ALU.mult)
            A(G, "tensor_tensor", out=tD, in0=tA, in1=aB, op=ALU.mult)
            for j in range(4):
                A(G, "tensor_tensor", out=rk[:, j, :], in0=rT[:, j, :], in1=tC[:, j, :], op=ALU.mult)
                A(G, "tensor_scalar", out=rk[:, j, :], in0=rk[:, j, :], scalar1=cols[:, K_RK + j:K_RK + j + 1],
                  scalar2=None, op0=ALU.mult)
            for j in range(4):
                A(V, "tensor_tensor_scan", out=tE[:, j, :], data0=cst[:, C_RESET:C_RESET + 128],
                  data1=sgdB[:, j, :], initial=0.0, op0=ALU.mult, op1=ALU.add)
            A(G, "tensor_tensor", out=tF, in0=tE, in1=sgdB, op=ALU.subtract)
            A(G, "tensor_tensor", out=v4(tG), in0=v4(tE)[:, :, :, 63:64].to_broadcast([128, 4, 2, 64]), in1=v4(tE),
              op=ALU.subtract)
            A(S_, "activation", out=tH, in_=tE, func=AF.Exp, scale=-DS)
            A(S_, "activation", out=tI, in_=tE, func=AF.Exp, scale=DS)
            A(S_, "activation", out=tF, in_=tF, func=AF.Exp, scale=-DS)
            A(S_, "activation", out=tG, in_=tG, func=AF.Exp, scale=-DS)
            A(G, "tensor_copy", out=wc, in_=v4(tH)[:, :, :, 63])
            A(V, "tensor_tensor", out=RT[:, :, :, 64:128], in0=v4(rT), in1=v4(tH), op=ALU.mult)
            A(G, "tensor_tensor", out=ZR[:, :, :, 64:128], in0=v4(rT), in1=v4(tH), op=ALU.mult)
            A(V, "scalar_tensor_tensor", out=tB, in0=tA, scalar=-1.0, in1=tF, op0=ALU.mult, op1=ALU.mult)
            A(G, "tensor_copy", out=RT[:, :, :, 0:64], in_=v4(tB))
            A(G, "tensor_copy", out=FZV[:, :, :, 0:64], in_=v4(tB))
            A(G, "tensor_copy", out=FZV[:, :, :, 64:128], in_=v4(vT))
            A(V, "tensor_tensor", out=LT[:, :, :, 0:64], in0=v4(tD), in1=v4(tI), op=ALU.mult)
            A(V, "tensor_tensor", out=LT[:, :, :, 64:128], in0=v4(tC), in1=v4(tI), op=ALU.mult)
            A(G, "tensor_tensor", out=FBK[:, :, :, 0:64], in0=v4(tD), in1=v4(tG), op=ALU.mult)
            A(G, "tensor_tensor", out=FBK[:, :, :, 64:128], in0=v4(tC), in1=v4(tG), op=ALU.mult)
            return L[:nsplit], L[nsplit:]

        def parallel_part(b):
            par = b % 2
            ZR, rk = ZR2[par], rk2[par]
            CS = (0, 1)
            Lp = []

            class _KB:
                @staticmethod
                def op(*a, **k):
                    Lp.append(Pp(kb_real.op, *a, **k))

            kb = _KB
            H = slice(64, 128)
            for c in CS:
                B0 = 3 * c
                pT1, pT2 = ps[B0], ps[B0 + 1]
                for j in range(4):
                    kb.op(PE, "matmul", out=pT1[:, j * 128:(j + 1) * 128], lhsT=FBK[:, j, c, :], rhs=identb[:], start=True,
                          stop=True)
                for j in range(4):
                    kb.op(PE, "matmul", out=pT2[:, j * 128:(j + 1) * 128], lhsT=FZV[:, j, c, :], rhs=identb[:], start=True,
                          stop=True)
                kb.op(S_, "activation", out=TBK[c], in_=pT1[:], func=AF.Copy)
                kb.op(S_, "activation", out=TZ[c], in_=pT2[0:64, :], func=AF.Copy)
                kb.op(S_, "activation", out=UV[c][64:128].rearrange("p h i -> p (h i)"), in_=pT2[64:128, :], func=AF.Copy)
            for c in CS:
                B0 = 3 * c
                for h in range(8):
                    j, hp = h // 2, (h % 2) * 64
                    kb.op(PE, "matmul", out=ps[B0 + h % 2][:, j * 128:(j + 1) * 128], lhsT=LT[hp:hp + 64, j, c, :],
                          rhs=RT[hp:hp + 64, j, c, :], start=True, stop=True)
                AAv = AAm[c].rearrange("p (j e) q -> p j e q", e=2)
                for e in range(2):
                    kb.op(V, "tensor_tensor", out=AAv[:, :, e, :], in0=ps[B0 + e][:].rearrange("p (j q) -> p j q", q=128),
                          in1=cst[:, C_M128:C_M128 + 128].unsqueeze(1).to_broadcast([128, 4, 128]), op=ALU.mult)
            for c in CS:
                B0 = 3 * c
                for h in range(8):
                    kb.op(PE, "matmul", out=ps[B0 + 2][0:64, h * 64:(h + 1) * 64], lhsT=AAm[c][0:64, h, 0:64],
                          rhs=identb[0:64, 0:64], start=True, stop=True)
                kb.op(S_, "activation", out=ATm[c].rearrange("p h q -> p (h q)"), in_=ps[B0 + 2][0:64, :], func=AF.Copy)
                kb.op(G, "tensor_tensor", out=Tm[c], in0=AAm[c][0:64, :, 0:64],
                      in1=identb[0:64, 0:64].unsqueeze(1).to_broadcast([64, 8, 64]), op=ALU.add)
            cur = {}
            for c in CS:
                cur[c] = ((lambda h, c=c: AAm[c][0:64, h, 0:64]), (lambda h, c=c: ATm[c][:, h, :]))
            for hop in range(1, 7):
                for c in CS:
                    B0 = 3 * c
                    Pc, PTc = cur[c]
                    pP, pPT, pD = ps[B0], ps[B0 + 1], ps[B0 + 2]
                    Tc = Tm[c]
                    if hop <= 4:
                        for h in range(8):
                            kb.op(PE, "matmul", out=pP[0:64, h * 64:(h + 1) * 64], lhsT=PTc(h), rhs=Pc(h), start=True, stop=True)
                    if hop <= 5:
                        for h in range(8):
                            kb.op(PE, "matmul", out=pPT[0:64, h * 64:(h + 1) * 64], lhsT=Pc(h), rhs=PTc(h), start=True, stop=True)
                    if hop >= 2:
                        for h in range(8):
                            kb.op(PE, "matmul", out=pD[0:64, h * 64:(h + 1) * 64], lhsT=PTc(h), rhs=Tc[:, h, :], start=True, stop=True)
                    nP, nPT = Pk[c][hop % 2], PTk[c][hop % 2]
                    if hop <= 4:
                        kb.op(S_, "activation", out=nP.rearrange("p h q -> p (h q)"), in_=pP[0:64, :], func=AF.Copy)
                    if hop <= 5:
                        kb.op(S_, "activation", out=nPT.rearrange("p h q -> p (h q)"), in_=pPT[0:64, :], func=AF.Copy)
                    if hop >= 2:
                        kb.op(V, "tensor_tensor", out=Tc.rearrange("p h q -> p (h q)"), in0=pD[0:64, :],
                              in1=Tc.rearrange("p h q -> p (h q)"), op=ALU.add)
                    cur[c] = ((lambda h, t=nP: t[:, h, :]), (lambda h, t=nPT: t[:, h, :]))
            for c in CS:
                B0 = 3 * c
                for h in range(8):
                    kb.op(PE, "matmul", out=ps[B0][0:64, h * 64:(h + 1) * 64], lhsT=AAm[c][64:128, h, 0:64],
                          rhs=UV[c][64:128, h, :], start=True, stop=True)
                kb.op(S_, "activation", out=XvT[c].rearrange("p h q -> p (h q)"), in_=ps[B0][0:64, :], func=AF.Copy)
            for c in CS:
                B0 = 3 * c
                for h in range(8):
                    kb.op(PE, "matmul", out=ps[B0 + 1][0:64, h * 64:(h + 1) * 64], lhsT=Tm[c][:, h, :], rhs=XvT[c][:, h, :],
                          start=True, stop=True)
                kb.op(S_, "activation", out=UVT[c].rearrange("p h q -> p (h q)"), in_=ps[B0 + 1][0:64, :], func=AF.Copy)
                for h in range(8):
                    j, hp = h // 2, (h % 2) * 64
                    kb.op(PE, "matmul", out=ps[B0 + 2][hp:hp + 64, j * 64:(j + 1) * 64], lhsT=TZ[c][:, h * 64:(h + 1) * 64],
                          rhs=Tm[c][:, h, :], start=True, stop=True)
                kb.op(V, "tensor_copy", out=ZR[:, :, c, 0:64], in_=ps[B0 + 2][:, 0:256].rearrange("p (j q) -> p j q", q=64))
            Lmain = Lp
            Lp = []

            class _KB2:
                @staticmethod
                def op(*a, **k):
                    Lp.append(Pp(kb_real.op, *a, **k))

            kb = _KB2
            for c in CS:
                B0 = 3 * c
                ts = slice(b * 128 + c * 64, b * 128 + c * 64 + 64)
                for j in range(4):
                    kb.op(PE, "matmul", out=ps[B0][H, 0:8], lhsT=rk[:, j, c * 64:(c + 1) * 64],
                          rhs=cst[:, C_HSEL + j * 8:C_HSEL + j * 8 + 8], start=(j == 0), stop=(j == 3))
                kb.op(PE, "matmul", out=ps[B0 + 1][H, :], lhsT=sgx[:, ts], rhs=gupb[:], start=True, stop=True)
                kb.op(V, "tensor_copy", out=srkc[c][H], in_=ps[B0][H, 0:8])
                kb.op(V, "tensor_tensor", out=G1[c][H], in0=ps[B0 + 1][H, :], in1=gnwb[H, 0, :], op=ALU.mult)
                kb.op(G, "tensor_tensor", out=G0[c][H], in0=UV[c][H], in1=srkc[c][H].unsqueeze(2).to_broadcast([64, 8, 64]),
                      op=ALU.mult)
                g0f = G0[c][H].rearrange("p h i -> p (h i)")
                kb.op(G, "tensor_tensor", out=g0f, in0=g0f, in1=gnwb[H, 1, :], op=ALU.add)
                kb.op(V, "tensor_tensor", out=g0f, in0=g0f, in1=ps[B0 + 1][H, :], op=ALU.mult)
            return Lmain, Lp

        def sequential_part(b):
            Ls = [[], []]
            par = b % 2
            ZR, wc, rk = ZR2[par], wc2[par], rk2[par]
            H = slice(64, 128)
            for c in (0, 1):
                A = lambda *a, _c=c, **k: Ls[_c].append(Pp(kb.op, *a, **k))
                B0 = 3 * c
                AA = AAm[c]
                ts = slice(b * 128 + c * 64, b * 128 + c * 64 + 64)
                for h in range(8):
                    j, hp = h // 2, (h % 2) * 64
                    A(PE, "matmul", out=ps[B0 + h % 2][:, j * 64:(j + 1) * 64], lhsT=ZR[hp:hp + 64, j, c, :],
                      rhs=STb[hp:hp + 64, j, :], start=True, stop=True)
                UVv = UV[c].rearrange("p (j e) i -> p j e i", e=2)
                UVTv = UVT[c].rearrange("p (j e) i -> p j e i", e=2)
                Y1v = Y1.rearrange("p (j e) i -> p j e i", e=2)
                for e in range(2):
                    A(V, "tensor_tensor", out=UVv[0:64, :, e, :],
                      in0=ps[B0 + e][0:64, 0:256].rearrange("p (j i) -> p j i", i=64), in1=UVTv[:, :, e, :], op=ALU.add)
                for e in range(2):
                    A(S_, "activation", out=Y1v[64:128, :, e, :],
                      in_=ps[B0 + e][64:128, 0:256].rearrange("p (j i) -> p j i", i=64), func=AF.Copy)
                for h in range(8):
                    j, hp = h // 2, (h % 2) * 64
                    A(PE, "matmul", out=ps[B0 + 2][hp:hp + 64, j * 64:(j + 1) * 64], lhsT=TBK[c][:, h * 64:(h + 1) * 64],
                      rhs=UV[c][:, h, :], start=True, stop=True)
                for j in range(4):
                    A(V, "scalar_tensor_tensor", out=STf[:, j, :], in0=STf[:, j, :], scalar=wc[:, j, c:c + 1],
                      in1=ps[B0 + 2][:, j * 64:(j + 1) * 64], op0=ALU.mult, op1=ALU.add)
                A(S_, "activation", out=STb[:], in_=STf[:], func=AF.Copy)
                for h in range(8):
                    A(PE, "matmul", out=ps[B0][64:128, h * 64:(h + 1) * 64], lhsT=AA[:, h, 64:128], rhs=UV[c][:, h, :],
                      start=True, stop=True)
                A(V, "tensor_tensor", out=yyc[c][H].rearrange("p h i -> p (h i)"), in0=ps[B0][H, :],
                  in1=Y1[H].rearrange("p h i -> p (h i)"), op=ALU.add)
            return Ls

        def post_part(b, c):
            L = []
            A = lambda *a, **k: L.append(Pp(kb.op, *a, **k))
            H = slice(64, 128)
            B0 = 3 * c
            ts = slice(b * 128 + c * 64, b * 128 + c * 64 + 64)
            yy, st8 = yyc[c], st8c[c]
            A(V, "tensor_reduce", out=st8[H, 0:8], in_=yy[H], axis=AX.X, op=ALU.add)
            A(G, "tensor_tensor", out=ysq[H], in0=yy[H], in1=yy[H], op=ALU.mult)
            A(V, "tensor_reduce", out=st8[H, 8:16], in_=ysq[H], axis=AX.X, op=ALU.add)
            A(V, "tensor_scalar", out=st8[H, 16:24], in0=st8[H, 0:8], scalar1=1.0 / 64, scalar2=None, op0=ALU.mult)
            A(V, "tensor_tensor", out=st8[H, 24:32], in0=st8[H, 16:24], in1=st8[H, 16:24], op=ALU.mult)
            A(V, "scalar_tensor_tensor", out=st8[H, 32:40], in0=st8[H, 8:16], scalar=1.0 / 64, in1=st8[H, 24:32],
              op0=ALU.mult, op1=ALU.subtract)
            A(V, "tensor_scalar", out=st8[H, 32:40], in0=st8[H, 32:40], scalar1=64e-5, scalar2=None, op0=ALU.add)
            A(S_, "activation", out=st8[H, 32:40], in_=st8[H, 32:40], func=AF.Sqrt)
            A(V, "reciprocal", out=st8[H, 40:48], in_=st8[H, 32:40])
            A(G, "tensor_tensor", out=yy[H], in0=yy[H], in1=st8[H, 16:24].unsqueeze(2).to_broadcast([64, 8, 64]),
              op=ALU.subtract)
            A(V, "tensor_tensor", out=yy[H], in0=yy[H], in1=st8[H, 40:48].unsqueeze(2).to_broadcast([64, 8, 64]),
              op=ALU.mult)
            yf = yy[H].rearrange("p h i -> p (h i)")
            A(G, "tensor_tensor", out=yf, in0=yf, in1=G1[c][H], op=ALU.mult)
            A(G, "tensor_tensor", out=yf, in0=yf, in1=G0[c][H].rearrange("p h i -> p (h i)"), op=ALU.add)
            ntail = len(L)
            for j in range(4):
                A(PE, "transpose", out=ps[7][:, j * 64:(j + 1) * 64],
                  in_=yy[H, 2 * j:2 * j + 2, :].rearrange("p h i -> p (h i)"), identity=cst[H, C_ID + 64:C_ID + 128])
            A(S_, "activation", out=catT[:, 0:4, ts], in_=ps[7][:, 0:256].rearrange("p (j t) -> p j t", t=64),
              func=AF.Copy)
            return L[:ntail], L[ntail:]

        def merged(La, Lb):
            na, nb = len(La), len(Lb)
            ib = 0
            for ia, f in enumerate(La):
                f()
                tgt = (ia + 1) * nb // max(na, 1)
                while ib < tgt:
                    Lb[ib]()
                    ib += 1
            while ib < nb:
                Lb[ib]()
                ib += 1

        p1, p2 = prep(0)
        merged(conv_section(), p1 + p2)
        pend = []
        for b in range(4):
            Lmain, Lepi = parallel_part(b)
            S0, S1 = sequential_part(b)
            p1, p2 = prep(b + 1) if b < 3 else ([], [])
            pp = p1 + p2
            nfirst = (len(pp) * 3) // 5
            merged(Lmain, pend + pp[:nfirst])
            merged(Lepi + S0 + S1, pp[nfirst:])
            e0, t0 = post_part(b, 0)
            e1, t1 = post_part(b, 1)
            pend = e0 + t0 + e1 + t1
        merged(pend, [])
        tokmajor_proj(lambda kc, b: catT[:, kc, b * 128:(b + 1) * 128], 4, woutS, 1, tmp + [fl(tA), fl(tB)], after_half)

    mset(G, carry[:], 0.0)
    mset(G, glu[:], 0.0)
    mset(G, STf[:], 0.0)
    mset(G, STb[:], 0.0)

    xv = x_d.rearrange("(n b p) d -> n p b d", b=4, p=128)
    ov = out_d.rearrange("(n b p) d -> n p b d", b=4, p=128)
    last = []

    def load_x(ti, blocks):
        for b in blocks:
            kb.op(SY, "dma_start", dma_key=f"xl{b}", out=xt[:, b, :], in_=xv[ti][:, b, :])

    load_x(0, range(4))
    prenorm(K_G1)
    for ti in range(NT):
        ffn(0, lambda blocks: prenorm(K_GM, blocks))
        mixer(ti, lambda blocks: prenorm(K_G2, blocks))

        def tail_hook(blocks, ti=ti):
            for b in blocks:
                kb.op(SY, "dma_start", dma_key=f"xs{b}", out=ov[ti][:, b, :], in_=xt[:, b, :])
            if ti + 1 < NT:
                load_x(ti + 1, blocks)
                prenorm(K_G1, blocks)

        ffn(1, tail_hook)
    for b in range(4):
        kb.finish([o for o in kb.ops["sync"] if o.dma_key == f"xs{b}"][-1])
    kb.emit()
    st.close()
    return nc


def make_inputs(T, x, p):
    g = lambda k: np.asarray(p[k], np.float32)[0]
    cols = np.zeros((128, K_END), np.float32)
    cols[:, K_MU:K_MU + 14] = _fm(g("shift_mu"), 14)
    cols[:, K_W0:K_W0 + 4] = _fm(g("w0"), 4)
    cols[:, K_A0:K_A0 + 4] = _fm(g("a0"), 4)
    cols[:, K_KK:K_KK + 4] = _fm(g("k_k"), 4)
    cols[:, K_KA:K_KA + 4] = _fm(g("k_a"), 4)
    cols[:, K_RK:K_RK + 4] = _fm(g("r_k").reshape(-1), 4)
    cols[:, K_CB:K_CB + 4] = _fm(g("conv_b"), 4)
    cols[:, K_LNW:K_LNW + 4] = _fm(g("conv_ln_w"), 4)
    cols[:, K_LNB:K_LNB + 4] = _fm(g("conv_ln_b"), 4)
    cols[:, K_G1:K_G1 + 8] = _fm(g("ffn1_norm_pre"), 8)
    cols[:, K_GM:K_GM + 8] = _fm(g("mix_norm_pre"), 8)
    cols[:, K_G2:K_G2 + 8] = _fm(g("ffn2_norm_pre"), 8)
    dw = g("conv_dw")
    cols[:, K_DW:K_DW + 124] = dw.T.reshape(4, 128, 31).transpose(1, 0, 2).reshape(128, 124)
    rows = np.zeros((5, D), np.float32)
    rows[0] = g("ffn1_norm_post")
    rows[1] = g("mix_norm_post")
    rows[2] = g("ffn2_norm_post")
    rows[3, 0:512] = g("gn_w")
    rows[3, 512:1024] = g("gn_b")
    lora = np.concatenate([g("w_up"), g("a_up")], 0)
    shared = {
        "wgu1": g("ffn1_w_gu"), "wgu2": g("ffn2_w_gu"), "wdn1": g("ffn1_w_down"), "wdn2": g("ffn2_w_down"),
        "win": g("w_in"), "wout": g("w_out"), "cols": cols, "consts": _consts(), "rows": rows,
        "lora": np.ascontiguousarray(lora), "gup": g("g_up"),
    }
    return shared


_CACHE = {}


def kernel(**inputs):
    x = np.asarray(inputs["x"], np.float32)
    B, T, _ = x.shape
    shared = make_inputs(T, x, inputs)
    if T not in _CACHE:
        _CACHE[T] = build(T)
    nc = _CACHE[T]
    in_maps = []
    for b in range(B):
        m = dict(shared)
        m["x"] = np.ascontiguousarray(x[b])
        in_maps.append(m)
    res = run_bass_kernel_spmd(nc, in_maps, core_ids=list(range(B)))
    return np.stack([np.asarray(r["out"], np.float32) for r in res.results], 0)
```

```python
import math
from functools import partial as functools_partial
from contextlib import ExitStack
import numpy as np
import concourse.bass as bass
import concourse.mybir as mybir
from concourse.bass_utils import run_bass_kernel_spmd

F32 = mybir.dt.float32
BF16 = mybir.dt.bfloat16
AF = mybir.ActivationFunctionType
ALU = mybir.AluOpType
AX = mybir.AxisListType

D = 1024
DFF = 2816
NF = 22
DRW = 512
DS = math.exp(-0.5)
TT = 512
WRITE_KEYS = ("out", "accum_out", "ap")
ENGS = ("sync", "tensor", "scalar", "vector", "gpsimd")


def _is_ap(v):
    return hasattr(v, "tensor") and hasattr(v, "ap") and hasattr(v, "offset")


def _region(ap):
    t = ap.tensor
    es = mybir.dt.size(ap.dtype)
    dims = [(int(s), int(c)) for s, c in ap.ap]
    off = int(ap.offset) * es
    space = str(ap.space)
    if "SB" in space or "PS" in space.upper():
        shp = [int(s) for s in t.shape]
        pstride = int(np.prod(shp[1:])) * mybir.dt.size(t.dtype)
        p0 = off // pstride
        f0 = off % pstride
        pc = dims[0][1]
        lo = hi = 0
        for s, c in dims[1:]:
            e = s * (c - 1) * es
            if e < 0:
                lo += e
            else:
                hi += e
        if "SB" not in space:
            return t.name, (p0 // 32 * 32, (p0 + pc + 31) // 32 * 32, 0, pstride)
        return t.name, (p0, p0 + pc, f0 + lo, f0 + hi + es)
    lo = hi = 0
    for s, c in dims:
        e = s * (c - 1) * es
        if e < 0:
            lo += e
        else:
            hi += e
    return "dram:" + t.name, (0, 1, off + lo, off + hi + es)


def _ovl(a, b):
    return a[0] < b[1] and b[0] < a[1] and a[2] < b[3] and b[2] < a[3]


def _contains(a, b):
    return a[0] <= b[0] and a[1] >= b[1] and a[2] <= b[2] and a[3] >= b[3]


class Op:
    __slots__ = ("eng", "meth", "kw", "deps", "marked", "dma_key", "tok", "idx")


class KB:
    def __init__(self, nc):
        self.nc = nc
        self.ops = {e: [] for e in ENGS}
        self.acc = {}
        self.dma_cnt = {}
        self.final = []
        self.capture = None

    def op(self, eng, meth, dma_key=None, **kw):
        if self.capture is not None:
            self.capture.append((eng, meth, dma_key, kw))
            return None
        o = Op()
        o.eng, o.meth, o.kw, o.marked, o.dma_key = eng, meth, kw, False, dma_key
        o.idx = len(self.ops[eng])
        is_dma = meth == "dma_start"
        if is_dma:
            assert dma_key is not None
            self.dma_cnt[dma_key] = self.dma_cnt.get(dma_key, 0) + 16
            o.tok = (dma_key, self.dma_cnt[dma_key])
        else:
            o.tok = None
        deps = {}
        accs = []
        for name, v in kw.items():
            if not _is_ap(v):
                continue
            kind = "w" if name in WRITE_KEYS else "r"
            tn, box = _region(v)
            accs.append((tn, box, kind))
            d = self.acc.get(tn)
            if not d:
                continue
            for (k2, box2), (o2, _) in d.items():
                if kind == "r" and k2[1] == "r":
                    continue
                if o2 is o:
                    continue
                if _ovl(box, box2):
                    if o2.eng == "tensor" and eng == "tensor" and o2.meth != "dma_start" and not is_dma:
                        continue
                    key = (o2.eng, o2.dma_key)
                    if key not in deps or deps[key].idx < o2.idx:
                        deps[key] = o2
        for o2 in deps.values():
            if o2.meth != "dma_start":
                o2.marked = True
        o.deps = list(deps.values())
        me = (eng, dma_key)
        for tn, box, kind in accs:
            d = self.acc.setdefault(tn, {})
            if kind == "w":
                for k in [k for k in d if _contains(box, k[1])]:
                    del d[k]
            d[((me, kind), box)] = (o, kind)
        self.ops[eng].append(o)
        return o

    def finish(self, o):
        self.final.append(o)

    def emit(self):
        nc = self.nc
        for e in ENGS:
            c = 0
            for o in self.ops[e]:
                if o.meth != "dma_start" and o.marked:
                    c += 1
                    o.tok = (e, c)
        with ExitStack() as st:
            sems = {}
            for e in ENGS:
                sems[e] = st.enter_context(nc.semaphore("s_" + e))
            for k in self.dma_cnt:
                sems[k] = st.enter_context(nc.semaphore("d_" + k))
            block = st.enter_context(nc.Block())

            def run(eng, ename):
                waited = {}
                for o in self.ops[ename]:
                    for o2 in o.deps:
                        k, v = o2.tok
                        if waited.get(k, 0) < v:
                            eng.wait_ge(sems[k], v)
                            waited[k] = v
                    inst = getattr(eng, o.meth)(**o.kw)
                    if o.meth == "dma_start":
                        inst.then_inc(sems[o.dma_key], 16)
                    elif o.marked:
                        inst.then_inc(sems[ename], 1)
                if ename == "sync":
                    for o2 in self.final:
                        k, v = o2.tok
                        if waited.get(k, 0) < v:
                            eng.wait_ge(sems[k], v)
                            waited[k] = v

            @block.sync
            def _(eng):
                run(eng, "sync")

            @block.tensor
            def _(eng):
                run(eng, "tensor")

            @block.scalar
            def _(eng):
                run(eng, "scalar")

            @block.vector
            def _(eng):
                run(eng, "vector")

            @block.gpsimd
            def _(eng):
                run(eng, "gpsimd")


C_ID = 0
C_M128 = 128
C_MAT = 256
C_BONES = 320
C_ONESM = 448
C_RESET = 576
C_HSEL = 704
C_NEGH = 736
C_END = 737


def _consts():
    c = np.zeros((128, C_END), np.float32)
    c[:, C_ID:C_ID + 128] = np.eye(128)
    p = np.arange(128)[:, None] % 64
    q = np.arange(128)[None, :]
    c[:, C_M128:C_M128 + 128] = np.where(q < 64, p < q, p <= (q - 64))
    t = np.arange(128)[:, None] % 64
    j = np.arange(64)[None, :]
    c[:, C_MAT:C_MAT + 64] = (j < t)
    c[:, C_BONES:C_BONES + 128] = (np.arange(128)[:, None] // 64 == np.arange(128)[None, :] // 64)
    c[:, C_ONESM:C_ONESM + 128] = 1.0 / 512.0
    r = np.ones((128, 128), np.float32)
    r[:, 0::64] = 0.0
    c[:, C_RESET:C_RESET + 128] = r
    hs = np.zeros((128, 4, 8), np.float32)
    for jj in range(4):
        hs[0:64, jj, 2 * jj] = 1.0
        hs[64:128, jj, 2 * jj + 1] = 1.0
    c[:, C_HSEL:C_HSEL + 32] = hs.reshape(128, 32)
    c[:, C_NEGH] = -0.5
    return c


K_MU = 0
K_W0 = 14
K_A0 = 18
K_KK = 22
K_KA = 26
K_RK = 30
K_CB = 34
K_LNW = 38
K_LNB = 42
K_G1 = 46
K_GM = 54
K_G2 = 62
K_DW = 70
K_END = 70 + 124


def _fm(v, n):
    return np.ascontiguousarray(np.asarray(v, np.float32).reshape(n, 128).T)


def build(T, stage=99, sub=99):
    NT = T // TT
    nc = bass.Bass("TRN2", target_bir_lowering=False)
    kb = KB(nc)
    kb_real = kb

    def din(name, shape, dt=F32):
        return nc.dram_tensor(name, list(shape), dt, kind="ExternalInput").ap()

    x_d = din("x", [T, D])
    out_d = nc.dram_tensor("out", [T, D], F32, kind="ExternalOutput").ap()
    wgu_d = [din("wgu1", [D, 2 * DFF]), din("wgu2", [D, 2 * DFF])]
    wdn_d = [din("wdn1", [DFF, D]), din("wdn2", [DFF, D])]
    win_d = din("win", [D, 2816])
    wout_d = din("wout", [D, D])
    cols_d = din("cols", [128, K_END])
    const_d = din("consts", [128, C_END])
    rows_d = din("rows", [5, D])
    lora_d = din("lora", [128, 512])
    gup_d = din("gup", [128, 512])

    def dscr(name, shape):
        return nc.dram_tensor(name, list(shape), BF16, kind="Internal").ap()

    wguS = [dscr("wguS1", [2 * NF, 128, 1024]), dscr("wguS2", [2 * NF, 128, 1024])]
    wdnS = [dscr("wdnS1", [2, 11, 128, 2, 512]), dscr("wdnS2", [2, 11, 128, 2, 512])]
    winS = dscr("winS", [NF, 128, 1024])
    woutS = dscr("woutS", [2, 4, 128, 2, 512])

    st = ExitStack()

    def sb(name, shape, dt=F32):
        return st.enter_context(nc.sbuf_tensor(name, list(shape), dt))

    cst = sb("cst", [128, C_END])
    cols = sb("cols_sb", [128, K_END])
    gcur = sb("gcur", [128, D])
    gnwb = sb("gnwb", [128, 2, 512])
    identb = sb("identb", [128, 128], BF16)
    bonesb = sb("bonesb", [128, 128], BF16)
    lorab = sb("lorab", [128, 512], BF16)
    gupb = sb("gupb", [128, 512], BF16)
    diag = sb("diag", [128, 124, 128], BF16)
    xt = sb("xt", [128, 4, D])
    slotA = [sb(f"slotA{i}", [128, 8, 256], BF16) for i in range(3)]
    slotB = [sb(f"slotB{i}", [128, 2, 512], BF16) for i in range(3)]
    sstat = sb("sstat", [128, 32])
    ARENA = 121 * 1024
    arena = sb("arena", [128, ARENA // 4])
    ident = cst[:, C_ID:C_ID + 128]

    class Carver:
        def __init__(self, start=18 * 1024, limit=ARENA, nxt=None):
            self.off = start
            self.limit = limit
            self.nxt = nxt

        def get(self, shape, dt=F32):
            n = int(np.prod(shape[1:])) * mybir.dt.size(dt)
            n = (n + 31) // 32 * 32
            if self.off + n > self.limit and self.nxt is not None:
                return self.nxt.get(shape, dt)
            assert self.off + n <= self.limit, (self.off, n, self.limit)
            a = arena[:, self.off // 4:(self.off + n) // 4]
            self.off += n
            if dt != F32:
                a = a.bitcast(dt)
            a = a[0:shape[0], 0:int(np.prod(shape[1:]))]
            if len(shape) == 3:
                a = a.rearrange("p (a b) -> p a b", b=shape[2])
            elif len(shape) == 4:
                a = a.rearrange("p (a b c) -> p a b c", b=shape[2], c=shape[3])
            return a

    _c0 = Carver(start=0)
    xs = [_c0.get([128, D]), _c0.get([128, D])]
    xs_cur = [xs]
    xnT = _c0.get([128, 8, TT], BF16)
    gjunk = _c0.get([128, D], BF16)
    assert _c0.off == 18 * 1024
    ps = [st.enter_context(nc.psum_tensor(f"ps{i}", [128, 512], F32)) for i in range(8)]

    V, S_, G, PE, SY = "vector", "scalar", "gpsimd", "tensor", "sync"

    kb.op(SY, "dma_start", dma_key="c0", out=cst[:], in_=const_d)
    kb.op(SY, "dma_start", dma_key="c1", out=cols[:], in_=cols_d)
    kb.op(SY, "dma_start", dma_key="c3", out=gnwb[:].rearrange("p a b -> p (a b)"),
          in_=rows_d[3:4, :].partition_broadcast(128))
    cv = Carver(start=0)
    stg0 = cv.get([128, 512])
    stg1 = cv.get([128, 512])
    kb.op(SY, "dma_start", dma_key="c4", out=stg0, in_=lora_d)
    kb.op(SY, "dma_start", dma_key="c5", out=stg1, in_=gup_d)
    kb.op(V, "tensor_copy", out=lorab[:], in_=stg0)
    kb.op(V, "tensor_copy", out=gupb[:], in_=stg1)
    kb.op(V, "tensor_copy", out=identb[:], in_=ident)
    kb.op(V, "tensor_copy", out=bonesb[:], in_=cst[:, C_BONES:C_BONES + 128])
    for i in range(124):
        kb.op(V if i % 2 else G, "tensor_scalar", out=diag[:, i, :], in0=ident,
              scalar1=cols[:, K_DW + i:K_DW + i + 1], scalar2=None, op0=ALU.mult)

    NSTG, NACC = 4, 3
    stage32 = [cv.get([128, 2048]) for _ in range(NSTG)]
    accb = [cv.get([128, 11, 1024], BF16) for _ in range(NACC)]
    cctr = [0]
    sctr = [0]
    hctr = [0]

    def cast(out, in_):
        eng = (S_, G, V)[cctr[0] % 3]
        cctr[0] += 1
        if eng == S_:
            kb.op(S_, "activation", out=out, in_=in_, func=AF.Copy)
        else:
            kb.op(eng, "tensor_copy", out=out, in_=in_)

    pend_st = []

    def flush_st():
        while pend_st:
            pend_st.pop(0)()

    def conv_A(src, dstS, nhalves):
        for qq in range(nhalves * 2):
            ai = hctr[0] % NACC
            a = accb[ai]
            hctr[0] += 1
            av = a.rearrange("p f (k c) -> p f k c", c=128)
            for kc in range(8):
                i = sctr[0] % NSTG
                sctr[0] += 1
                stg = stage32[i][:, 0:1408]
                kb.op(SY, "dma_start", dma_key=f"cv{i}", out=stg, in_=src[kc * 128:(kc + 1) * 128, qq * 1408:(qq + 1) * 1408])
                cast(av[:, :, kc, :], stg.rearrange("p (f c) -> p f c", c=128))
                if kc == 3:
                    flush_st()
            pend_st.append(lambda ai=ai, qq=qq, a=a, dstS=dstS: kb.op(
                SY, "dma_start", dma_key=f"csA{ai}", out=dstS[qq * 11:qq * 11 + 11].rearrange("f p x -> p f x"), in_=a[:]))

    def conv_B(src, dstS, nk2):
        sv = src.rearrange("(k s p) n -> k p s n", s=2, p=128)
        for k2 in range(nk2):
            i = sctr[0] % NSTG
            sctr[0] += 1
            stg = stage32[i]
            ai = hctr[0] % NACC
            a = accb[ai]
            hctr[0] += 1
            bst = a.rearrange("p f x -> p (f x)")[:, 0:2048]
            kb.op(SY, "dma_start", dma_key=f"cv{i}", out=stg.rearrange("p (s n) -> p s n", s=2), in_=sv[k2])
            cast(bst.rearrange("p (h s n) -> p h s n", h=2, s=2), stg.rearrange("p (s h n) -> p h s n", s=2, h=2))
            flush_st()
            for hf in range(2):
                pend_st.append(lambda ai=ai, hf=hf, k2=k2, bst=bst, dstS=dstS: kb.op(
                    SY, "dma_start", dma_key=f"csB{ai}_{hf}", out=dstS[hf, k2].rearrange("p s n -> p (s n)"),
                    in_=bst[:, hf * 1024:(hf + 1) * 1024]))

    conv_A(wgu_d[0], wguS[0], 2)
    conv_B(wdn_d[0], wdnS[0], 11)
    conv_A(win_d, winS, 1)
    conv_B(wout_d, woutS, 4)
    conv_A(wgu_d[1], wguS[1], 2)
    conv_B(wdn_d[1], wdnS[1], 11)
    flush_st()

    actr = [0]
    bctr = [0]

    def loadA(src):
        i = actr[0] % 3
        actr[0] += 1
        kb.op(SY, "dma_start", dma_key=f"A{i}", out=slotA[i][:], in_=src)
        return slotA[i]

    def loadA2(src):
        i = actr[0] % 3
        actr[0] += 1
        v = slotA[i][:].rearrange("p k c -> p (k c)").rearrange("p (g x) -> p g x", g=2)
        kb.op(SY, "dma_start", dma_key=f"A{i}", out=v, in_=src)
        return slotA[i][:].rearrange("p k c -> p (k c)").rearrange("p (g k c) -> p g k c", g=2, k=8)

    def loadB(src):
        i = bctr[0] % 3
        bctr[0] += 1
        kb.op(SY, "dma_start", dma_key=f"B{i}", out=slotB[i][:], in_=src)
        return slotB[i]

    def rstd_from_ss(ss_ap, out_ap, n, eps, npart=128):
        kb.op(V, "tensor_scalar", out=out_ap, in0=ss_ap, scalar1=1.0 / n, scalar2=eps, op0=ALU.mult, op1=ALU.add)
        kb.op(S_, "activation", out=out_ap, in_=out_ap, func=AF.Sqrt)
        kb.op(V, "reciprocal", out=out_ap, in_=out_ap)

    def prenorm(gcol, blocks=(0, 1, 2, 3)):
        junk = gjunk
        for b in blocks:
            kb.op(S_, "activation", out=junk, in_=xt[:, b, :], func=AF.Square, accum_out=sstat[:, b:b + 1])
            rstd_from_ss(sstat[:, b:b + 1], sstat[:, 4 + b:5 + b], D, 1e-6)
            xb = xs_cur[0][b % len(xs_cur[0])]
            kb.op(V, "tensor_scalar", out=xb, in0=xt[:, b, :], scalar1=sstat[:, 4 + b:5 + b], scalar2=None,
                  op0=ALU.mult)
            for hf in range(2):
                pb = ps[b * 2 + hf]
                for q in range(4):
                    kc = hf * 4 + q
                    kb.op(PE, "transpose", out=pb[:, q * 128:(q + 1) * 128], in_=xb[:, kc * 128:(kc + 1) * 128],
                          identity=ident)
                kb.op(V, "tensor_tensor", out=xnT[:, hf * 4:hf * 4 + 4, b * 128:(b + 1) * 128],
                      in0=pb[:].rearrange("p (a b) -> p a b", b=128),
                      in1=cols[:, gcol + hf * 4:gcol + hf * 4 + 4].unsqueeze(2).to_broadcast([128, 4, 128]),
                      op=ALU.mult)

    def load_gain(gi):
        kb.op(SY, "dma_start", dma_key="gc", out=gcur[:], in_=rows_d[gi:gi + 1, :].partition_broadcast(128))
        if gi != 1:
            kb.op(V, "tensor_scalar", out=gcur[:], in0=gcur[:], scalar1=0.5, scalar2=None, op0=ALU.mult)

    def postnorm_residual(gi, banks, tmp, blocks):
        for b in blocks:
            c0 = 16 + 4 * b
            for hf in range(2):
                kb.op(S_, "activation", out=gjunk[:, hf * 512:(hf + 1) * 512], in_=banks[b][hf][:], func=AF.Square,
                      accum_out=sstat[:, c0 + hf:c0 + hf + 1])
            kb.op(V, "tensor_tensor", out=sstat[:, c0 + 2:c0 + 3], in0=sstat[:, c0:c0 + 1], in1=sstat[:, c0 + 1:c0 + 2], op=ALU.add)
            rstd_from_ss(sstat[:, c0 + 2:c0 + 3], sstat[:, c0 + 3:c0 + 4], D, 1e-6)
            for hf in range(2):
                t = tmp[(b * 2 + hf) % len(tmp)]
                kb.op(V, "scalar_tensor_tensor", out=t, in0=banks[b][hf][:], scalar=sstat[:, c0 + 3:c0 + 4],
                      in1=gcur[:, hf * 512:(hf + 1) * 512], op0=ALU.mult, op1=ALU.mult)
                kb.op(G, "tensor_tensor", out=xt[:, b, hf * 512:(hf + 1) * 512], in0=xt[:, b, hf * 512:(hf + 1) * 512],
                      in1=t, op=ALU.add)

    def tokmajor_proj(lhs_fn, nk2, wS, gi, tmp, after_half=None):
        banks = {b: [ps[b * 2 + hf] for hf in range(2)] for b in range(4)}
        for hf in range(2):
            for k2 in range(nk2):
                sl = loadB(wS[hf, k2])
                for s in range(2):
                    kc = k2 * 2 + s
                    for b in range(4):
                        kb.op(PE, "matmul", out=banks[b][hf][:], lhsT=lhs_fn(kc, b), rhs=sl[:, s, :],
                              start=(kc == 0), stop=(kc == nk2 * 2 - 1))
        for pair in ((0, 1, 2, 3),):
            lists = []
            for b in pair:
                kb.capture = []
                postnorm_residual(gi, banks, tmp, (b,))
                if after_half is not None:
                    after_half((b,))
                lists.append(kb.capture)
                kb.capture = None
            for i in range(max(len(l) for l in lists)):
                for l in lists:
                    if i < len(l):
                        eng, meth, dk, kw = l[i]
                        kb.op(eng, meth, dma_key=dk, **kw)

    def ffn(l, after_half=None):
        cvf = Carver()
        hT = cvf.get([128, NF, TT], BF16)
        sg = [cvf.get([128, TT]) for _ in range(2)]
        tmp = [cvf.get([128, 512]) for _ in range(8)]
        xs_cur[0] = xs + [cvf.get([128, D]) for _ in range(2)]
        load_gain(0 if l == 0 else 2)
        for f in range(NF):
            sl = loadA(wguS[l].rearrange("(g f) p x -> f p g x", g=2)[f]).rearrange("p k (g c) -> p g k c", g=2) if False else loadA2(wguS[l].rearrange("(g f) p x -> f p g x", g=2)[f])
            pg = ps[4 + (f % 2) * 2]
            pu = ps[5 + (f % 2) * 2]
            for kc in range(8):
                kb.op(PE, "matmul", out=pg[:], lhsT=sl[:, 0, kc, :], rhs=xnT[:, kc, :], start=(kc == 0), stop=(kc == 7))
            for kc in range(8):
                kb.op(PE, "matmul", out=pu[:], lhsT=sl[:, 1, kc, :], rhs=xnT[:, kc, :], start=(kc == 0), stop=(kc == 7))
            kb.op(S_, "activation", out=sg[f % 2], in_=pg[:], func=AF.Silu)
            kb.op(V, "tensor_tensor", out=hT[:, f, :], in0=sg[f % 2], in1=pu[:], op=ALU.mult)
        tokmajor_proj(lambda kc, b: hT[:, kc, b * 128:(b + 1) * 128], 11, wdnS[l], 0 if l == 0 else 2, tmp, after_half)

    carry = sb("carry", [128, 14])
    glu = sb("glu", [128, 4, 30 + TT], BF16)
    STf = sb("STf", [128, 4, 64])
    STb = sb("STb", [128, 4, 64], BF16)

    def mset(eng, ap, val):
        kb.op(eng, "memset", ap=ap, constant=val)

    def mixer(ti, after_half=None):
        cvm = Carver()
        g = cvm.get
        praw = [g([128, 513]) for _ in range(2)]
        dtmp = g([128, 512])
        xwa = g([128, TT])
        xg = g([128, TT])
        cT = g([128, 4, TT])
        csq = g([128, TT])
        mean = g([128, TT])
        rstd = g([128, TT])
        tmp = [g([128, 512]) for _ in range(2)]
        endA = cvm.off
        rkvT = g([128, 12, TT])
        sgx = g([128, TT], BF16)
        xwab = g([128, TT], BF16)
        catT = g([128, 8, TT], BF16)
        endP = cvm.off
        load_gain(1)
        dst_of = {}
        for c in range(12):
            dst_of[c] = rkvT[:, c, :]
        dst_of[12] = xwa
        dst_of[13] = xg
        order = list(range(14)) + [14, 18, 15, 19, 16, 20, 17, 21]
        slots = {}
        sigt = {}
        for n, c in enumerate(order):
            c2 = c // 2
            if c2 not in slots:
                slots[c2] = loadA2(winS[2 * c2:2 * c2 + 2].rearrange("g p x -> p g x"))
            sl = slots[c2]
            pb = ps[4 + n % 4]
            for kc in range(8):
                kb.op(PE, "matmul", out=pb[:], lhsT=sl[:, c % 2, kc, :], rhs=xnT[:, kc, :],
                      start=(kc == 0), stop=(kc == 7))
            if c < 14:
                pr = praw[c % 2]
                kb.op(S_, "activation", out=pr[:, 1:513], in_=pb[:], func=AF.Copy)
                kb.op(G, "tensor_copy", out=pr[:, 0:1], in_=carry[:, c:c + 1])
                kb.op(G, "tensor_tensor", out=dtmp, in0=pr[:, 0:512], in1=pr[:, 1:513], op=ALU.subtract)
                kb.op(V, "scalar_tensor_tensor", out=dst_of[c], in0=dtmp, scalar=cols[:, K_MU + c:K_MU + c + 1],
                      in1=pr[:, 1:513], op0=ALU.mult, op1=ALU.add)
                kb.op(G, "tensor_copy", out=carry[:, c:c + 1], in_=pr[:, 512:513])
            elif c < 18:
                sigt[c] = pb
            else:
                kb.op(S_, "activation", out=tmp[c % 2], in_=pb[:], func=AF.Sigmoid)
                kb.op(V, "tensor_tensor", out=glu[:, c - 18, 30:30 + TT], in0=tmp[c % 2], in1=sigt[c - 4][:], op=ALU.mult)
        kb.op(S_, "activation", out=xwab[0:64, :], in_=xwa[0:64, :], func=AF.Tanh)
        kb.op(V, "tensor_copy", out=xwab[64:128, :], in_=xwa[64:128, :])
        kb.op(S_, "activation", out=sgx, in_=xg, func=AF.Sigmoid)
        def conv_section():
            Lc = []

            class _KBc:
                @staticmethod
                def op(*a, **k):
                    Lc.append(functools_partial(kb_real.op, *a, **k))

            kb = _KBc
            for j in range(4):
                pb = ps[4 + j % 2]
                for tau in range(31):
                    kb.op(PE, "matmul", out=pb[:], lhsT=diag[:, j * 31 + tau, :], rhs=glu[:, j, tau:tau + TT], start=(tau == 0),
                          stop=(tau == 30))
                kb.op(V, "tensor_scalar", out=cT[:, j, :], in0=pb[:], scalar1=cols[:, K_CB + j:K_CB + j + 1], scalar2=None,
                      op0=ALU.add)
            for j in range(4):
                kb.op(G, "tensor_copy", out=glu[:, j, 0:30], in_=glu[:, j, TT:TT + 30])
            onesm = cst[:, C_ONESM:C_ONESM + 128]
            pm, pv = ps[2], ps[3]
            for j in range(4):
                kb.op(PE, "matmul", out=pm[:], lhsT=onesm, rhs=cT[:, j, :], start=(j == 0), stop=(j == 3))
            for j in range(4):
                kb.op(S_, "activation", out=csq, in_=cT[:, j, :], func=AF.Square)
                kb.op(PE, "matmul", out=pv[:], lhsT=onesm, rhs=csq, start=(j == 0), stop=(j == 3))
            kb.op(S_, "activation", out=mean, in_=pm[:], func=AF.Copy)
            kb.op(V, "tensor_tensor", out=rstd, in0=mean, in1=mean, op=ALU.mult)
            kb.op(V, "tensor_tensor", out=rstd, in0=pv[:], in1=rstd, op=ALU.subtract)
            kb.op(V, "tensor_scalar", out=rstd, in0=rstd, scalar1=1e-5, scalar2=None, op0=ALU.add)
            kb.op(S_, "activation", out=rstd, in_=rstd, func=AF.Sqrt)
            kb.op(V, "reciprocal", out=rstd, in_=rstd)
            for j in range(4):
                kb.op(V, "tensor_tensor", out=csq, in0=cT[:, j, :], in1=mean, op=ALU.subtract)
                kb.op(V, "tensor_tensor", out=csq, in0=csq, in1=rstd, op=ALU.mult)
                kb.op(S_, "activation", out=catT[:, 4 + j, :], in_=csq, func=AF.Silu,
                      scale=cols[:, K_LNW + j:K_LNW + j + 1], bias=cols[:, K_LNB + j:K_LNB + j + 1])

            return Lc

        from functools import partial as Pp
        _tail = Carver(start=endP)
        gP = _tail.get
        gA = Carver(start=0, limit=endA, nxt=_tail).get
        g = gP

        def t4():
            return g([128, 4, 128])

        xsm = [g([128, D]) for _ in range(2)]
        tA, tB, tG, tH, tI, sgdB, aB = [t4() for _ in range(7)]
        tC, tD = [xsm[0][:, i * 512:(i + 1) * 512].rearrange("p (j t) -> p j t", j=4) for i in range(2)]
        tE, tF = [xsm[1][:, i * 512:(i + 1) * 512].rearrange("p (j t) -> p j t", j=4) for i in range(2)]
        rk2 = [t4(), t4()]
        sqb = g([128, 4, 128], BF16)
        LT = g([128, 4, 2, 128], BF16)
        RT = g([128, 4, 2, 128], BF16)
        FBK = g([128, 4, 2, 128], BF16)
        FZV = g([128, 4, 2, 128], BF16)
        ZR2 = [g([128, 4, 2, 128], BF16) for _ in range(2)]
        wc2 = [g([128, 4, 2]) for _ in range(2)]
        g = gA
        TBK = [g([128, 512], BF16) for _ in range(2)]
        TZ = [g([64, 512], BF16) for _ in range(2)]
        UV = [g([128, 8, 64], BF16) for _ in range(2)]
        AAm = [g([128, 8, 128], BF16) for _ in range(2)]
        ATm = [g([64, 8, 64], BF16) for _ in range(2)]
        Pk = [[g([64, 8, 64], BF16) for _ in range(2)] for _ in range(2)]
        PTk = [[g([64, 8, 64], BF16) for _ in range(2)] for _ in range(2)]
        Tm = [g([64, 8, 64], BF16) for _ in range(2)]
        XvT = [g([64, 8, 64], BF16) for _ in range(2)]
        UVT = [g([64, 8, 64]) for _ in range(2)]
        Y1 = g([128, 8, 64])
        ysq = g([128, 8, 64])
        yyc = [g([128, 8, 64]) for _ in range(2)]
        G1 = [g([128, 512]) for _ in range(2)]
        G0 = [g([128, 8, 64]) for _ in range(2)]
        st8c = [g([128, 48]) for _ in range(2)]
        srkc = [g([128, 8]) for _ in range(2)]

        def v4(ap):
            return ap.rearrange("p j (c t) -> p j c t", c=2)

        def fl(ap):
            return ap.rearrange("p j t -> p (j t)")

        def prep(b):
            L = []
            A = lambda *a, **k: L.append(Pp(kb.op, *a, **k))
            par = b % 2
            ZR, wc, rk = ZR2[par], wc2[par], rk2[par]
            bs = slice(b * 128, (b + 1) * 128)
            rT, kT, vT = rkvT[:, 0:4, bs], rkvT[:, 4:8, bs], rkvT[:, 8:12, bs]
            psd, psa, pn = ps[6], ps[6], ps[6]
            for j in range(4):
                A(PE, "matmul", out=psd[:, j * 128:(j + 1) * 128], lhsT=lorab[0:64, j * 128:(j + 1) * 128],
                  rhs=xwab[0:64, bs], start=True, stop=True)
            for j in range(4):
                A(S_, "activation", out=sgdB[:, j, :], in_=psd[:, j * 128:(j + 1) * 128], func=AF.Sigmoid,
                  bias=cols[:, K_W0 + j:K_W0 + j + 1])
            for j in range(4):
                A(PE, "matmul", out=psa[:, j * 128:(j + 1) * 128], lhsT=lorab[64:128, j * 128:(j + 1) * 128],
                  rhs=xwab[64:128, bs], start=True, stop=True)
            for j in range(4):
                A(S_, "activation", out=aB[:, j, :], in_=psa[:, j * 128:(j + 1) * 128], func=AF.Sigmoid,
                  bias=cols[:, K_A0 + j:K_A0 + j + 1])
            for j in range(4):
                A(V, "tensor_scalar", out=tA[:, j, :], in0=kT[:, j, :], scalar1=cols[:, K_KK + j:K_KK + j + 1],
                  scalar2=None, op0=ALU.mult)
            A(G, "tensor_tensor", out=sqb, in0=tA, in1=tA, op=ALU.mult)
            A(PE, "matmul", out=pn[:], lhsT=bonesb[:], rhs=fl(sqb), start=True, stop=True)
            A(V, "tensor_scalar", out=fl(tB), in0=pn[:], scalar1=1e-12, scalar2=None, op0=ALU.max)
            nsplit = len(L)
            A(S_, "activation", out=fl(tB), in_=fl(tB), func=AF.Ln)
            A(S_, "activation", out=fl(tB), in_=fl(tB), func=AF.Exp, scale=-0.5)
            A(V, "tensor_tensor", out=tA, in0=tA, in1=tB, op=ALU.mult)
            for j in range(4):
                A(V, "tensor_scalar", out=tC[:, j, :], in0=aB[:, j, :], scalar1=-1.0,
                  scalar2=cols[:, K_KA + j:K_KA + j + 1], op0=ALU.add, op1=ALU.mult)
            A(V, "scalar_tensor_tensor", out=tC, in0=tC, scalar=1.0, in1=kT, op0=ALU.add, op1=ALU.mult)
            A(G, "tensor_tensor", out=tD, in0=tA, in1=aB, op=ALU.mult)
            for j in range(4):
                A(G, "tensor_tensor", out=rk[:, j, :], in0=rT[:, j, :], in1=tC[:, j, :], op=ALU.mult)
                A(G, "tensor_scalar", out=rk[:, j, :], in0=rk[:, j, :], scalar1=cols[:, K_RK + j:K_RK + j + 1],
                  scalar2=None, op0=ALU.mult)
            for j in range(4):
                A(V, "tensor_tensor_scan", out=tE[:, j, :], data0=cst[:, C_RESET:C_RESET + 128],
                  data1=sgdB[:, j, :], initial=0.0, op0=ALU.mult, op1=ALU.add)
            A(G, "tensor_tensor", out=tF, in0=tE, in1=sgdB, op=ALU.subtract)
            A(G, "tensor_tensor", out=v4(tG), in0=v4(tE)[:, :, :, 63:64].to_broadcast([128, 4, 2, 64]), in1=v4(tE),
              op=ALU.subtract)
            A(S_, "activation", out=tH, in_=tE, func=AF.Exp, scale=-DS)
            A(S_, "activation", out=tI, in_=tE, func=AF.Exp, scale=DS)
            A(S_, "activation", out=tF, in_=tF, func=AF.Exp, scale=-DS)
            A(S_, "activation", out=tG, in_=tG, func=AF.Exp, scale=-DS)
            A(G, "tensor_copy", out=wc, in_=v4(tH)[:, :, :, 63])
            A(V, "tensor_tensor", out=RT[:, :, :, 64:128], in0=v4(rT), in1=v4(tH), op=ALU.mult)
            A(G, "tensor_tensor", out=ZR[:, :, :, 64:128], in0=v4(rT), in1=v4(tH), op=ALU.mult)
            A(V, "scalar_tensor_tensor", out=tB, in0=tA, scalar=-1.0, in1=tF, op0=ALU.mult, op1=ALU.mult)
            A(G, "tensor_copy", out=RT[:, :, :, 0:64], in_=v4(tB))
            A(G, "tensor_copy", out=FZV[:, :, :, 0:64], in_=v4(tB))
            A(G, "tensor_copy", out=FZV[:, :, :, 64:128], in_=v4(vT))
            A(V, "tensor_tensor", out=LT[:, :, :, 0:64], in0=v4(tD), in1=v4(tI), op=ALU.mult)
            A(V, "tensor_tensor", out=LT[:, :, :, 64:128], in0=v4(tC), in1=v4(tI), op=ALU.mult)
            A(G, "tensor_tensor", out=FBK[:, :, :, 0:64], in0=v4(tD), in1=v4(tG), op=ALU.mult)
            A(G, "tensor_tensor", out=FBK[:, :, :, 64:128], in0=v4(tC), in1=v4(tG), op=ALU.mult)
            return L[:nsplit], L[nsplit:]

        def parallel_part(b):
            par = b % 2
            ZR, rk = ZR2[par], rk2[par]
            CS = (0, 1)
            Lp = []

            class _KB:
                @staticmethod
                def op(*a, **k):
                    Lp.append(Pp(kb_real.op, *a, **k))

            kb = _KB
            H = slice(64, 128)
            for c in CS:
                B0 = 3 * c
                pT1, pT2 = ps[B0], ps[B0 + 1]
                for j in range(4):
                    kb.op(PE, "matmul", out=pT1[:, j * 128:(j + 1) * 128], lhsT=FBK[:, j, c, :], rhs=identb[:], start=True,
                          stop=True)
                for j in range(4):
                    kb.op(PE, "matmul", out=pT2[:, j * 128:(j + 1) * 128], lhsT=FZV[:, j, c, :], rhs=identb[:], start=True,
                          stop=True)
                kb.op(S_, "activation", out=TBK[c], in_=pT1[:], func=AF.Copy)
                kb.op(S_, "activation", out=TZ[c], in_=pT2[0:64, :], func=AF.Copy)
                kb.op(S_, "activation", out=UV[c][64:128].rearrange("p h i -> p (h i)"), in_=pT2[64:128, :], func=AF.Copy)
            for c in CS:
                B0 = 3 * c
                for h in range(8):
                    j, hp = h // 2, (h % 2) * 64
                    kb.op(PE, "matmul", out=ps[B0 + h % 2][:, j * 128:(j + 1) * 128], lhsT=LT[hp:hp + 64, j, c, :],
                          rhs=RT[hp:hp + 64, j, c, :], start=True, stop=True)
                AAv = AAm[c].rearrange("p (j e) q -> p j e q", e=2)
                for e in range(2):
                    kb.op(V, "tensor_tensor", out=AAv[:, :, e, :], in0=ps[B0 + e][:].rearrange("p (j q) -> p j q", q=128),
                          in1=cst[:, C_M128:C_M128 + 128].unsqueeze(1).to_broadcast([128, 4, 128]), op=ALU.mult)
            for c in CS:
                B0 = 3 * c
                for h in range(8):
                    kb.op(PE, "matmul", out=ps[B0 + 2][0:64, h * 64:(h + 1) * 64], lhsT=AAm[c][0:64, h, 0:64],
                          rhs=identb[0:64, 0:64], start=True, stop=True)
                kb.op(S_, "activation", out=ATm[c].rearrange("p h q -> p (h q)"), in_=ps[B0 + 2][0:64, :], func=AF.Copy)
                kb.op(G, "tensor_tensor", out=Tm[c], in0=AAm[c][0:64, :, 0:64],
                      in1=identb[0:64, 0:64].unsqueeze(1).to_broadcast([64, 8, 64]), op=ALU.add)
            cur = {}
            for c in CS:
                cur[c] = ((lambda h, c=c: AAm[c][0:64, h, 0:64]), (lambda h, c=c: ATm[c][:, h, :]))
            for hop in range(1, 7):
                for c in CS:
                    B0 = 3 * c
                    Pc, PTc = cur[c]
                    pP, pPT, pD = ps[B0], ps[B0 + 1], ps[B0 + 2]
                    Tc = Tm[c]
                    if hop <= 4:
                        for h in range(8):
                            kb.op(PE, "matmul", out=pP[0:64, h * 64:(h + 1) * 64], lhsT=PTc(h), rhs=Pc(h), start=True, stop=True)
                    if hop <= 5:
                        for h in range(8):
                            kb.op(PE, "matmul", out=pPT[0:64, h * 64:(h + 1) * 64], lhsT=Pc(h), rhs=PTc(h), start=True, stop=True)
                    if hop >= 2:
                        for h in range(8):
                            kb.op(PE, "matmul", out=pD[0:64, h * 64:(h + 1) * 64], lhsT=PTc(h), rhs=Tc[:, h, :], start=True, stop=True)
                    nP, nPT = Pk[c][hop % 2], PTk[c][hop % 2]
                    if hop <= 4:
                        kb.op(S_, "activation", out=nP.rearrange("p h q -> p (h q)"), in_=pP[0:64, :], func=AF.Copy)
                    if hop <= 5:
                        kb.op(S_, "activation", out=nPT.rearrange("p h q -> p (h q)"), in_=pPT[0:64, :], func=AF.Copy)
                    if hop >= 2:
                        kb.op(V, "tensor_tensor", out=Tc.rearrange("p h q -> p (h q)"), in0=pD[0:64, :],
                              in1=Tc.rearrange("p h q -> p (h q)"), op=ALU.add)
                    cur[c] = ((lambda h, t=nP: t[:, h, :]), (lambda h, t=nPT: t[:, h, :]))
            for c in CS:
                B0 = 3 * c
                for h in range(8):
                    kb.op(PE, "matmul", out=ps[B0][0:64, h * 64:(h + 1) * 64], lhsT=AAm[c][64:128, h, 0:64],
                          rhs=UV[c][64:128, h, :], start=True, stop=True)
                kb.op(S_, "activation", out=XvT[c].rearrange("p h q -> p (h q)"), in_=ps[B0][0:64, :], func=AF.Copy)
            for c in CS:
                B0 = 3 * c
                for h in range(8):
                    kb.op(PE, "matmul", out=ps[B0 + 1][0:64, h * 64:(h + 1) * 64], lhsT=Tm[c][:, h, :], rhs=XvT[c][:, h, :],
                          start=True, stop=True)
                kb.op(S_, "activation", out=UVT[c].rearrange("p h q -> p (h q)"), in_=ps[B0 + 1][0:64, :], func=AF.Copy)
                for h in range(8):
                    j, hp = h // 2, (h % 2) * 64
                    kb.op(PE, "matmul", out=ps[B0 + 2][hp:hp + 64, j * 64:(j + 1) * 64], lhsT=TZ[c][:, h * 64:(h + 1) * 64],
                          rhs=Tm[c][:, h, :], start=True, stop=True)
                kb.op(V, "tensor_copy", out=ZR[:, :, c, 0:64], in_=ps[B0 + 2][:, 0:256].rearrange("p (j q) -> p j q", q=64))
            Lmain = Lp
            Lp = []

            class _KB2:
                @staticmethod
                def op(*a, **k):
                    Lp.append(Pp(kb_real.op, *a, **k))

            kb = _KB2
            for c in CS:
                B0 = 3 * c
                ts = slice(b * 128 + c * 64, b * 128 + c * 64 + 64)
                for j in range(4):
                    kb.op(PE, "matmul", out=ps[B0][H, 0:8], lhsT=rk[:, j, c * 64:(c + 1) * 64],
                          rhs=cst[:, C_HSEL + j * 8:C_HSEL + j * 8 + 8], start=(j == 0), stop=(j == 3))
                kb.op(PE, "matmul", out=ps[B0 + 1][H, :], lhsT=sgx[:, ts], rhs=gupb[:], start=True, stop=True)
                kb.op(V, "tensor_copy", out=srkc[c][H], in_=ps[B0][H, 0:8])
                kb.op(V, "tensor_tensor", out=G1[c][H], in0=ps[B0 + 1][H, :], in1=gnwb[H, 0, :], op=ALU.mult)
                kb.op(G, "tensor_tensor", out=G0[c][H], in0=UV[c][H], in1=srkc[c][H].unsqueeze(2).to_broadcast([64, 8, 64]),
                      op=ALU.mult)
                g0f = G0[c][H].rearrange("p h i -> p (h i)")
                kb.op(G, "tensor_tensor", out=g0f, in0=g0f, in1=gnwb[H, 1, :], op=ALU.add)
                kb.op(V, "tensor_tensor", out=g0f, in0=g0f, in1=ps[B0 + 1][H, :], op=ALU.mult)
            return Lmain, Lp

        def sequential_part(b):
            Ls = [[], []]
            par = b % 2
            ZR, wc, rk = ZR2[par], wc2[par], rk2[par]
            H = slice(64, 128)
            for c in (0, 1):
                A = lambda *a, _c=c, **k: Ls[_c].append(Pp(kb.op, *a, **k))
                B0 = 3 * c
                AA = AAm[c]
                ts = slice(b * 128 + c * 64, b * 128 + c * 64 + 64)
                for h in range(8):
                    j, hp = h // 2, (h % 2) * 64
                    A(PE, "matmul", out=ps[B0 + h % 2][:, j * 64:(j + 1) * 64], lhsT=ZR[hp:hp + 64, j, c, :],
                      rhs=STb[hp:hp + 64, j, :], start=True, stop=True)
                UVv = UV[c].rearrange("p (j e) i -> p j e i", e=2)
                UVTv = UVT[c].rearrange("p (j e) i -> p j e i", e=2)
                Y1v = Y1.rearrange("p (j e) i -> p j e i", e=2)
                for e in range(2):
                    A(V, "tensor_tensor", out=UVv[0:64, :, e, :],
                      in0=ps[B0 + e][0:64, 0:256].rearrange("p (j i) -> p j i", i=64), in1=UVTv[:, :, e, :], op=ALU.add)
                for e in range(2):
                    A(S_, "activation", out=Y1v[64:128, :, e, :],
                      in_=ps[B0 + e][64:128, 0:256].rearrange("p (j i) -> p j i", i=64), func=AF.Copy)
                for h in range(8):
                    j, hp = h // 2, (h % 2) * 64
                    A(PE, "matmul", out=ps[B0 + 2][hp:hp + 64, j * 64:(j + 1) * 64], lhsT=TBK[c][:, h * 64:(h + 1) * 64],
                      rhs=UV[c][:, h, :], start=True, stop=True)
                for j in range(4):
                    A(V, "scalar_tensor_tensor", out=STf[:, j, :], in0=STf[:, j, :], scalar=wc[:, j, c:c + 1],
                      in1=ps[B0 + 2][:, j * 64:(j + 1) * 64], op0=ALU.mult, op1=ALU.add)
                A(S_, "activation", out=STb[:], in_=STf[:], func=AF.Copy)
                for h in range(8):
                    A(PE, "matmul", out=ps[B0][64:128, h * 64:(h + 1) * 64], lhsT=AA[:, h, 64:128], rhs=UV[c][:, h, :],
                      start=True, stop=True)
                A(V, "tensor_tensor", out=yyc[c][H].rearrange("p h i -> p (h i)"), in0=ps[B0][H, :],
                  in1=Y1[H].rearrange("p h i -> p (h i)"), op=ALU.add)
            return Ls

        def post_part(b, c):
            L = []
            A = lambda *a, **k: L.append(Pp(kb.op, *a, **k))
            H = slice(64, 128)
            B0 = 3 * c
            ts = slice(b * 128 + c * 64, b * 128 + c * 64 + 64)
            yy, st8 = yyc[c], st8c[c]
            A(V, "tensor_reduce", out=st8[H, 0:8], in_=yy[H], axis=AX.X, op=ALU.add)
            A(G, "tensor_tensor", out=ysq[H], in0=yy[H], in1=yy[H], op=ALU.mult)
            A(V, "tensor_reduce", out=st8[H, 8:16], in_=ysq[H], axis=AX.X, op=ALU.add)
            A(V, "tensor_scalar", out=st8[H, 16:24], in0=st8[H, 0:8], scalar1=1.0 / 64, scalar2=None, op0=ALU.mult)
            A(V, "tensor_tensor", out=st8[H, 24:32], in0=st8[H, 16:24], in1=st8[H, 16:24], op=ALU.mult)
            A(V, "scalar_tensor_tensor", out=st8[H, 32:40], in0=st8[H, 8:16], scalar=1.0 / 64, in1=st8[H, 24:32],
              op0=ALU.mult, op1=ALU.subtract)
            A(V, "tensor_scalar", out=st8[H, 32:40], in0=st8[H, 32:40], scalar1=64e-5, scalar2=None, op0=ALU.add)
            A(S_, "activation", out=st8[H, 32:40], in_=st8[H, 32:40], func=AF.Sqrt)
            A(V, "reciprocal", out=st8[H, 40:48], in_=st8[H, 32:40])
            A(G, "tensor_tensor", out=yy[H], in0=yy[H], in1=st8[H, 16:24].unsqueeze(2).to_broadcast([64, 8, 64]),
              op=ALU.subtract)
            A(V, "tensor_tensor", out=yy[H], in0=yy[H], in1=st8[H, 40:48].unsqueeze(2).to_broadcast([64, 8, 64]),
              op=ALU.mult)
            yf = yy[H].rearrange("p h i -> p (h i)")
            A(G, "tensor_tensor", out=yf, in0=yf, in1=G1[c][H], op=ALU.mult)
            A(G, "tensor_tensor", out=yf, in0=yf, in1=G0[c][H].rearrange("p h i -> p (h i)"), op=ALU.add)
            ntail = len(L)
            for j in range(4):
                A(PE, "transpose", out=ps[7][:, j * 64:(j + 1) * 64],
                  in_=yy[H, 2 * j:2 * j + 2, :].rearrange("p h i -> p (h i)"), identity=cst[H, C_ID + 64:C_ID + 128])
            A(S_, "activation", out=catT[:, 0:4, ts], in_=ps[7][:, 0:256].rearrange("p (j t) -> p j t", t=64),
              func=AF.Copy)
            return L[:ntail], L[ntail:]

        def merged(La, Lb):
            na, nb = len(La), len(Lb)
            ib = 0
            for ia, f in enumerate(La):
                f()
                tgt = (ia + 1) * nb // max(na, 1)
                while ib < tgt:
                    Lb[ib]()
                    ib += 1
            while ib < nb:
                Lb[ib]()
                ib += 1

        p1, p2 = prep(0)
        merged(conv_section(), p1 + p2)
        pend = []
        for b in range(4):
            Lmain, Lepi = parallel_part(b)
            S0, S1 = sequential_part(b)
            p1, p2 = prep(b + 1) if b < 3 else ([], [])
            pp = p1 + p2
            nfirst = (len(pp) * 3) // 5
            merged(Lmain, pend + pp[:nfirst])
            merged(Lepi + S0 + S1, pp[nfirst:])
            e0, t0 = post_part(b, 0)
            e1, t1 = post_part(b, 1)
            pend = e0 + t0 + e1 + t1
        merged(pend, [])
        xs_cur[0] = xs + xsm
        tokmajor_proj(lambda kc, b: catT[:, kc, b * 128:(b + 1) * 128], 4, woutS, 1,
                      tmp + [fl(tA), fl(tB), fl(tG), fl(tH), fl(tI), fl(sgdB)], after_half)

    mset(G, carry[:], 0.0)
    mset(G, glu[:], 0.0)
    mset(G, STf[:], 0.0)
    mset(G, STb[:], 0.0)

    xv = x_d.rearrange("(n b p) d -> n p b d", b=4, p=128)
    ov = out_d.rearrange("(n b p) d -> n p b d", b=4, p=128)
    last = []

    def load_x(ti, blocks):
        for b in blocks:
            kb.op(SY, "dma_start", dma_key=f"xl{b}", out=xt[:, b, :], in_=xv[ti][:, b, :])

    load_x(0, range(4))
    prenorm(K_G1)
    for ti in range(NT):
        ffn(0, lambda blocks: prenorm(K_GM, blocks))
        mixer(ti, lambda blocks: prenorm(K_G2, blocks))

        def tail_hook(blocks, ti=ti):
            for b in blocks:
                kb.op(SY, "dma_start", dma_key=f"xs{b}", out=ov[ti][:, b, :], in_=xt[:, b, :])
            if ti + 1 < NT:
                load_x(ti + 1, blocks)
                prenorm(K_G1, blocks)

        ffn(1, tail_hook)
    for b in range(4):
        kb.finish([o for o in kb.ops["sync"] if o.dma_key == f"xs{b}"][-1])
    kb.emit()
    st.close()
    return nc


def make_inputs(T, x, p):
    g = lambda k: np.asarray(p[k], np.float32)[0]
    cols = np.zeros((128, K_END), np.float32)
    cols[:, K_MU:K_MU + 14] = _fm(g("shift_mu"), 14)
    cols[:, K_W0:K_W0 + 4] = _fm(g("w0"), 4)
    cols[:, K_A0:K_A0 + 4] = _fm(g("a0"), 4)
    cols[:, K_KK:K_KK + 4] = _fm(g("k_k"), 4)
    cols[:, K_KA:K_KA + 4] = _fm(g("k_a"), 4)
    cols[:, K_RK:K_RK + 4] = _fm(g("r_k").reshape(-1), 4)
    cols[:, K_CB:K_CB + 4] = _fm(g("conv_b"), 4)
    cols[:, K_LNW:K_LNW + 4] = _fm(g("conv_ln_w"), 4)
    cols[:, K_LNB:K_LNB + 4] = _fm(g("conv_ln_b"), 4)
    cols[:, K_G1:K_G1 + 8] = _fm(g("ffn1_norm_pre"), 8)
    cols[:, K_GM:K_GM + 8] = _fm(g("mix_norm_pre"), 8)
    cols[:, K_G2:K_G2 + 8] = _fm(g("ffn2_norm_pre"), 8)
    dw = g("conv_dw")
    cols[:, K_DW:K_DW + 124] = dw.T.reshape(4, 128, 31).transpose(1, 0, 2).reshape(128, 124)
    rows = np.zeros((5, D), np.float32)
    rows[0] = g("ffn1_norm_post")
    rows[1] = g("mix_norm_post")
    rows[2] = g("ffn2_norm_post")
    rows[3, 0:512] = g("gn_w")
    rows[3, 512:1024] = g("gn_b")
    lora = np.concatenate([g("w_up"), g("a_up")], 0)
    shared = {
        "wgu1": g("ffn1_w_gu"), "wgu2": g("ffn2_w_gu"), "wdn1": g("ffn1_w_down"), "wdn2": g("ffn2_w_down"),
        "win": g("w_in"), "wout": g("w_out"), "cols": cols, "consts": _consts(), "rows": rows,
        "lora": np.ascontiguousarray(lora), "gup": g("g_up"),
    }
    return shared


_CACHE = {}


def kernel(**inputs):
    x = np.asarray(inputs["x"], np.float32)
    B, T, _ = x.shape
    shared = make_inputs(T, x, inputs)
    if T not in _CACHE:
        _CACHE[T] = build(T)
    nc = _CACHE[T]
    in_maps = []
    for b in range(B):
        m = dict(shared)
        m["x"] = np.ascontiguousarray(x[b])
        in_maps.append(m)
    res = run_bass_kernel_spmd(nc, in_maps, core_ids=list(range(B)))
    return np.stack([np.asarray(r["out"], np.float32) for r in res.results], 0)
```

```python
import math
from functools import partial as functools_partial
from contextlib import ExitStack
import numpy as np
import concourse.bass as bass
import concourse.mybir as mybir
from concourse.bass_utils import run_bass_kernel_spmd

F32 = mybir.dt.float32
BF16 = mybir.dt.bfloat16
AF = mybir.ActivationFunctionType
ALU = mybir.AluOpType
AX = mybir.AxisListType

D = 1024
DFF = 2816
NF = 22
DRW = 512
DS = math.exp(-0.5)
TT = 512
WRITE_KEYS = ("out", "accum_out", "ap")
ENGS = ("sync", "tensor", "scalar", "vector", "gpsimd")


def _is_ap(v):
    return hasattr(v, "tensor") and hasattr(v, "ap") and hasattr(v, "offset")


def _region(ap):
    t = ap.tensor
    es = mybir.dt.size(ap.dtype)
    dims = [(int(s), int(c)) for s, c in ap.ap]
    off = int(ap.offset) * es
    space = str(ap.space)
    if "SB" in space or "PS" in space.upper():
        shp = [int(s) for s in t.shape]
        pstride = int(np.prod(shp[1:])) * mybir.dt.size(t.dtype)
        p0 = off // pstride
        f0 = off % pstride
        pc = dims[0][1]
        lo = hi = 0
        for s, c in dims[1:]:
            e = s * (c - 1) * es
            if e < 0:
                lo += e
            else:
                hi += e
        if "SB" not in space:
            return t.name, (p0 // 32 * 32, (p0 + pc + 31) // 32 * 32, 0, pstride)
        return t.name, (p0, p0 + pc, f0 + lo, f0 + hi + es)
    lo = hi = 0
    for s, c in dims:
        e = s * (c - 1) * es
        if e < 0:
            lo += e
        else:
            hi += e
    return "dram:" + t.name, (0, 1, off + lo, off + hi + es)


def _ovl(a, b):
    return a[0] < b[1] and b[0] < a[1] and a[2] < b[3] and b[2] < a[3]


def _contains(a, b):
    return a[0] <= b[0] and a[1] >= b[1] and a[2] <= b[2] and a[3] >= b[3]


class Op:
    __slots__ = ("eng", "meth", "kw", "deps", "marked", "dma_key", "tok", "idx")


class KB:
    def __init__(self, nc):
        self.nc = nc
        self.ops = {e: [] for e in ENGS}
        self.acc = {}
        self.dma_cnt = {}
        self.final = []
        self.capture = None

    def op(self, eng, meth, dma_key=None, **kw):
        if self.capture is not None:
            self.capture.append((eng, meth, dma_key, kw))
            return None
        o = Op()
        o.eng, o.meth, o.kw, o.marked, o.dma_key = eng, meth, kw, False, dma_key
        o.idx = len(self.ops[eng])
        is_dma = meth == "dma_start"
        if is_dma:
            assert dma_key is not None
            self.dma_cnt[dma_key] = self.dma_cnt.get(dma_key, 0) + 16
            o.tok = (dma_key, self.dma_cnt[dma_key])
        else:
            o.tok = None
        deps = {}
        accs = []
        for name, v in kw.items():
            if not _is_ap(v):
                continue
            kind = "w" if name in WRITE_KEYS else "r"
            tn, box = _region(v)
            accs.append((tn, box, kind))
            d = self.acc.get(tn)
            if not d:
                continue
            for (k2, box2), (o2, _) in d.items():
                if kind == "r" and k2[1] == "r":
                    continue
                if o2 is o:
                    continue
                if _ovl(box, box2):
                    if o2.eng == "tensor" and eng == "tensor" and o2.meth != "dma_start" and not is_dma:
                        continue
                    key = (o2.eng, o2.dma_key)
                    if key not in deps or deps[key].idx < o2.idx:
                        deps[key] = o2
        for o2 in deps.values():
            if o2.meth != "dma_start":
                o2.marked = True
        o.deps = list(deps.values())
        me = (eng, dma_key)
        for tn, box, kind in accs:
            d = self.acc.setdefault(tn, {})
            if kind == "w":
                for k in [k for k in d if _contains(box, k[1])]:
                    del d[k]
            d[((me, kind), box)] = (o, kind)
        self.ops[eng].append(o)
        return o

    def finish(self, o):
        self.final.append(o)

    def emit(self):
        nc = self.nc
        for e in ENGS:
            c = 0
            for o in self.ops[e]:
                if o.meth != "dma_start" and o.marked:
                    c += 1
                    o.tok = (e, c)
        with ExitStack() as st:
            sems = {}
            for e in ENGS:
                sems[e] = st.enter_context(nc.semaphore("s_" + e))
            for k in self.dma_cnt:
                sems[k] = st.enter_context(nc.semaphore("d_" + k))
            block = st.enter_context(nc.Block())

            def run(eng, ename):
                waited = {}
                for o in self.ops[ename]:
                    for o2 in o.deps:
                        k, v = o2.tok
                        if waited.get(k, 0) < v:
                            eng.wait_ge(sems[k], v)
                            waited[k] = v
                    inst = getattr(eng, o.meth)(**o.kw)
                    if o.meth == "dma_start":
                        inst.then_inc(sems[o.dma_key], 16)
                    elif o.marked:
                        inst.then_inc(sems[ename], 1)
                if ename == "sync":
                    for o2 in self.final:
                        k, v = o2.tok
                        if waited.get(k, 0) < v:
                            eng.wait_ge(sems[k], v)
                            waited[k] = v

            @block.sync
            def _(eng):
                run(eng, "sync")

            @block.tensor
            def _(eng):
                run(eng, "tensor")

            @block.scalar
            def _(eng):
                run(eng, "scalar")

            @block.vector
            def _(eng):
                run(eng, "vector")

            @block.gpsimd
            def _(eng):
                run(eng, "gpsimd")


C_ID = 0
C_M128 = 128
C_MAT = 256
C_BONES = 320
C_ONESM = 448
C_RESET = 576
C_HSEL = 704
C_NEGH = 736
C_END = 737


def _consts():
    c = np.zeros((128, C_END), np.float32)
    c[:, C_ID:C_ID + 128] = np.eye(128)
    p = np.arange(128)[:, None] % 64
    q = np.arange(128)[None, :]
    c[:, C_M128:C_M128 + 128] = np.where(q < 64, p < q, p <= (q - 64))
    t = np.arange(128)[:, None] % 64
    j = np.arange(64)[None, :]
    c[:, C_MAT:C_MAT + 64] = (j < t)
    c[:, C_BONES:C_BONES + 128] = (np.arange(128)[:, None] // 64 == np.arange(128)[None, :] // 64)
    c[:, C_ONESM:C_ONESM + 128] = 1.0 / 512.0
    r = np.ones((128, 128), np.float32)
    r[:, 0::64] = 0.0
    c[:, C_RESET:C_RESET + 128] = r
    hs = np.zeros((128, 4, 8), np.float32)
    for jj in range(4):
        hs[0:64, jj, 2 * jj] = 1.0
        hs[64:128, jj, 2 * jj + 1] = 1.0
    c[:, C_HSEL:C_HSEL + 32] = hs.reshape(128, 32)
    c[:, C_NEGH] = -0.5
    return c


K_MU = 0
K_W0 = 14
K_A0 = 18
K_KK = 22
K_KA = 26
K_RK = 30
K_CB = 34
K_LNW = 38
K_LNB = 42
K_G1 = 46
K_GM = 54
K_G2 = 62
K_DW = 70
K_END = 70 + 124


def _fm(v, n):
    return np.ascontiguousarray(np.asarray(v, np.float32).reshape(n, 128).T)


def build(T, stage=99, sub=99):
    NT = T // TT
    nc = bass.Bass("TRN2", target_bir_lowering=False)
    kb = KB(nc)
    kb_real = kb

    def din(name, shape, dt=F32):
        return nc.dram_tensor(name, list(shape), dt, kind="ExternalInput").ap()

    x_d = din("x", [T, D])
    out_d = nc.dram_tensor("out", [T, D], F32, kind="ExternalOutput").ap()
    wgu_d = [din("wgu1", [D, 2 * DFF]), din("wgu2", [D, 2 * DFF])]
    wdn_d = [din("wdn1", [DFF, D]), din("wdn2", [DFF, D])]
    win_d = din("win", [D, 2816])
    wout_d = din("wout", [D, D])
    cols_d = din("cols", [128, K_END])
    const_d = din("consts", [128, C_END])
    rows_d = din("rows", [5, D])
    lora_d = din("lora", [128, 512])
    gup_d = din("gup", [128, 512])

    def dscr(name, shape):
        return nc.dram_tensor(name, list(shape), BF16, kind="Internal").ap()

    wguS = [dscr("wguS1", [2 * NF, 128, 1024]), dscr("wguS2", [2 * NF, 128, 1024])]
    wdnS = [dscr("wdnS1", [2, 11, 128, 2, 512]), dscr("wdnS2", [2, 11, 128, 2, 512])]
    winS = dscr("winS", [NF, 128, 1024])
    woutS = dscr("woutS", [2, 4, 128, 2, 512])

    st = ExitStack()

    def sb(name, shape, dt=F32):
        return st.enter_context(nc.sbuf_tensor(name, list(shape), dt))

    cst = sb("cst", [128, C_END])
    cols = sb("cols_sb", [128, K_END])
    gcur = sb("gcur", [128, D])
    gnwb = sb("gnwb", [128, 2, 512])
    identb = sb("identb", [128, 128], BF16)
    bonesb = sb("bonesb", [128, 128], BF16)
    lorab = sb("lorab", [128, 512], BF16)
    gupb = sb("gupb", [128, 512], BF16)
    diag = sb("diag", [128, 124, 128], BF16)
    xt = sb("xt", [128, 4, D])
    slotA = [sb(f"slotA{i}", [128, 8, 256], BF16) for i in range(3)]
    slotB = [sb(f"slotB{i}", [128, 2, 512], BF16) for i in range(3)]
    sstat = sb("sstat", [128, 32])
    hselr = sb("hselr", [128, 32])
    ARENA = 121 * 1024
    arena = sb("arena", [128, ARENA // 4])
    ident = cst[:, C_ID:C_ID + 128]

    class Carver:
        def __init__(self, start=18 * 1024, limit=ARENA, nxt=None):
            self.off = start
            self.limit = limit
            self.nxt = nxt

        def get(self, shape, dt=F32):
            n = int(np.prod(shape[1:])) * mybir.dt.size(dt)
            n = (n + 31) // 32 * 32
            if self.off + n > self.limit and self.nxt is not None:
                return self.nxt.get(shape, dt)
            assert self.off + n <= self.limit, (self.off, n, self.limit)
            a = arena[:, self.off // 4:(self.off + n) // 4]
            self.off += n
            if dt != F32:
                a = a.bitcast(dt)
            a = a[0:shape[0], 0:int(np.prod(shape[1:]))]
            if len(shape) == 3:
                a = a.rearrange("p (a b) -> p a b", b=shape[2])
            elif len(shape) == 4:
                a = a.rearrange("p (a b c) -> p a b c", b=shape[2], c=shape[3])
            return a

    _c0 = Carver(start=0)
    xs = [_c0.get([128, D]), _c0.get([128, D])]
    xs_cur = [xs]
    xnT = _c0.get([128, 8, TT], BF16)
    gjunk = _c0.get([128, D], BF16)
    assert _c0.off == 18 * 1024
    ps = [st.enter_context(nc.psum_tensor(f"ps{i}", [128, 512], F32)) for i in range(8)]

    V, S_, G, PE, SY = "vector", "scalar", "gpsimd", "tensor", "sync"

    kb.op(SY, "dma_start", dma_key="c0", out=cst[:], in_=const_d)
    kb.op(SY, "dma_start", dma_key="c1", out=cols[:], in_=cols_d)
    kb.op(SY, "dma_start", dma_key="c3", out=gnwb[:].rearrange("p a b -> p (a b)"),
          in_=rows_d[3:4, :].partition_broadcast(128))
    cv = Carver(start=0)
    stg0 = cv.get([128, 512])
    stg1 = cv.get([128, 512])
    kb.op(SY, "dma_start", dma_key="c4", out=stg0, in_=lora_d)
    kb.op(SY, "dma_start", dma_key="c5", out=stg1, in_=gup_d)
    kb.op(V, "tensor_copy", out=lorab[:], in_=stg0)
    kb.op(V, "tensor_copy", out=gupb[:], in_=stg1)
    kb.op(V, "tensor_copy", out=identb[:], in_=ident)
    kb.op(V, "tensor_copy", out=bonesb[:], in_=cst[:, C_BONES:C_BONES + 128])
    for j in range(4):
        kb.op(V, "tensor_scalar", out=hselr[:, j * 8:(j + 1) * 8], in0=cst[:, C_HSEL + j * 8:C_HSEL + j * 8 + 8],
              scalar1=cols[:, K_RK + j:K_RK + j + 1], scalar2=None, op0=ALU.mult)
    for i in range(124):
        kb.op(V if i % 2 else G, "tensor_scalar", out=diag[:, i, :], in0=ident,
              scalar1=cols[:, K_DW + i:K_DW + i + 1], scalar2=None, op0=ALU.mult)

    NSTG, NACC = 4, 3
    stage32 = [cv.get([128, 2048]) for _ in range(NSTG)]
    accb = [cv.get([128, 11, 1024], BF16) for _ in range(NACC)]
    cctr = [0]
    sctr = [0]
    hctr = [0]

    def cast(out, in_):
        eng = (S_, V)[cctr[0] % 2]
        cctr[0] += 1
        if eng == S_:
            kb.op(S_, "activation", out=out, in_=in_, func=AF.Copy)
        else:
            kb.op(eng, "tensor_copy", out=out, in_=in_)

    pend_st = []

    def flush_st():
        while pend_st:
            pend_st.pop(0)()

    def conv_A(src, dstS, nhalves):
        for qq in range(nhalves * 2):
            ai = hctr[0] % NACC
            a = accb[ai]
            hctr[0] += 1
            av = a.rearrange("p f (k c) -> p f k c", c=128)
            for kc in range(8):
                i = sctr[0] % NSTG
                sctr[0] += 1
                stg = stage32[i][:, 0:1408]
                kb.op(SY, "dma_start", dma_key=f"cv{i}", out=stg, in_=src[kc * 128:(kc + 1) * 128, qq * 1408:(qq + 1) * 1408])
                cast(av[:, :, kc, :], stg.rearrange("p (f c) -> p f c", c=128))
                if kc == 3:
                    flush_st()
            pend_st.append(lambda ai=ai, qq=qq, a=a, dstS=dstS: kb.op(
                SY, "dma_start", dma_key=f"csA{ai}", out=dstS[qq * 11:qq * 11 + 11].rearrange("f p x -> p f x"), in_=a[:]))

    def conv_B(src, dstS, nk2):
        sv = src.rearrange("(k s p) n -> k p s n", s=2, p=128)
        for k2 in range(nk2):
            i = sctr[0] % NSTG
            sctr[0] += 1
            stg = stage32[i]
            ai = hctr[0] % NACC
            a = accb[ai]
            hctr[0] += 1
            bst = a.rearrange("p f x -> p (f x)")[:, 0:2048]
            kb.op(SY, "dma_start", dma_key=f"cv{i}", out=stg.rearrange("p (s n) -> p s n", s=2), in_=sv[k2])
            cast(bst.rearrange("p (h s n) -> p h s n", h=2, s=2), stg.rearrange("p (s h n) -> p h s n", s=2, h=2))
            flush_st()
            for hf in range(2):
                pend_st.append(lambda ai=ai, hf=hf, k2=k2, bst=bst, dstS=dstS: kb.op(
                    SY, "dma_start", dma_key=f"csB{ai}_{hf}", out=dstS[hf, k2].rearrange("p s n -> p (s n)"),
                    in_=bst[:, hf * 1024:(hf + 1) * 1024]))

    conv_A(wgu_d[0], wguS[0], 2)
    conv_B(wdn_d[0], wdnS[0], 11)
    conv_A(win_d, winS, 1)
    conv_B(wout_d, woutS, 4)
    conv_A(wgu_d[1], wguS[1], 2)
    conv_B(wdn_d[1], wdnS[1], 11)
    flush_st()

    actr = [0]
    bctr = [0]

    def loadA(src):
        i = actr[0] % 3
        actr[0] += 1
        kb.op(SY, "dma_start", dma_key=f"A{i}", out=slotA[i][:], in_=src)
        return slotA[i]

    def loadA2(src):
        i = actr[0] % 3
        actr[0] += 1
        v = slotA[i][:].rearrange("p k c -> p (k c)").rearrange("p (g x) -> p g x", g=2)
        kb.op(SY, "dma_start", dma_key=f"A{i}", out=v, in_=src)
        return slotA[i][:].rearrange("p k c -> p (k c)").rearrange("p (g k c) -> p g k c", g=2, k=8)

    def loadB(src):
        i = bctr[0] % 3
        bctr[0] += 1
        kb.op(SY, "dma_start", dma_key=f"B{i}", out=slotB[i][:], in_=src)
        return slotB[i]

    def rstd_from_ss(ss_ap, out_ap, n, eps, npart=128):
        kb.op(V, "tensor_scalar", out=out_ap, in0=ss_ap, scalar1=1.0 / n, scalar2=eps, op0=ALU.mult, op1=ALU.add)
        kb.op(S_, "activation", out=out_ap, in_=out_ap, func=AF.Sqrt)
        kb.op(V, "reciprocal", out=out_ap, in_=out_ap)

    def prenorm(gcol, blocks=(0, 1, 2, 3)):
        junk = gjunk
        for b in blocks:
            kb.op(S_, "activation", out=junk, in_=xt[:, b, :], func=AF.Square, accum_out=sstat[:, b:b + 1])
            rstd_from_ss(sstat[:, b:b + 1], sstat[:, 4 + b:5 + b], D, 1e-6)
            xb = xs_cur[0][b % len(xs_cur[0])]
            kb.op(V, "tensor_scalar", out=xb, in0=xt[:, b, :], scalar1=sstat[:, 4 + b:5 + b], scalar2=None,
                  op0=ALU.mult)
            for hf in range(2):
                pb = ps[b * 2 + hf]
                for q in range(4):
                    kc = hf * 4 + q
                    kb.op(PE, "transpose", out=pb[:, q * 128:(q + 1) * 128], in_=xb[:, kc * 128:(kc + 1) * 128],
                          identity=ident)
                kb.op(V, "tensor_tensor", out=xnT[:, hf * 4:hf * 4 + 4, b * 128:(b + 1) * 128],
                      in0=pb[:].rearrange("p (a b) -> p a b", b=128),
                      in1=cols[:, gcol + hf * 4:gcol + hf * 4 + 4].unsqueeze(2).to_broadcast([128, 4, 128]),
                      op=ALU.mult)

    def load_gain(gi):
        kb.op(SY, "dma_start", dma_key="gc", out=gcur[:], in_=rows_d[gi:gi + 1, :].partition_broadcast(128))
        if gi != 1:
            kb.op(V, "tensor_scalar", out=gcur[:], in0=gcur[:], scalar1=0.5, scalar2=None, op0=ALU.mult)

    def postnorm_residual(gi, banks, tmp, blocks):
        for b in blocks:
            c0 = 16 + 4 * b
            for hf in range(2):
                kb.op(S_, "activation", out=gjunk[:, hf * 512:(hf + 1) * 512], in_=banks[b][hf][:], func=AF.Square,
                      accum_out=sstat[:, c0 + hf:c0 + hf + 1])
            kb.op(V, "tensor_tensor", out=sstat[:, c0 + 2:c0 + 3], in0=sstat[:, c0:c0 + 1], in1=sstat[:, c0 + 1:c0 + 2], op=ALU.add)
            rstd_from_ss(sstat[:, c0 + 2:c0 + 3], sstat[:, c0 + 3:c0 + 4], D, 1e-6)
            for hf in range(2):
                t = tmp[(b * 2 + hf) % len(tmp)]
                kb.op(V, "scalar_tensor_tensor", out=t, in0=banks[b][hf][:], scalar=sstat[:, c0 + 3:c0 + 4],
                      in1=gcur[:, hf * 512:(hf + 1) * 512], op0=ALU.mult, op1=ALU.mult)
                kb.op(G, "tensor_tensor", out=xt[:, b, hf * 512:(hf + 1) * 512], in0=xt[:, b, hf * 512:(hf + 1) * 512],
                      in1=t, op=ALU.add)

    def tokmajor_proj(lhs_fn, nk2, wS, gi, tmp, after_half=None):
        banks = {b: [ps[b * 2 + hf] for hf in range(2)] for b in range(4)}
        for hf in range(2):
            for k2 in range(nk2):
                sl = loadB(wS[hf, k2])
                for s in range(2):
                    kc = k2 * 2 + s
                    for b in range(4):
                        kb.op(PE, "matmul", out=banks[b][hf][:], lhsT=lhs_fn(kc, b), rhs=sl[:, s, :],
                              start=(kc == 0), stop=(kc == nk2 * 2 - 1))
        for pair in ((0, 1, 2, 3),):
            lists = []
            for b in pair:
                kb.capture = []
                postnorm_residual(gi, banks, tmp, (b,))
                if after_half is not None:
                    after_half((b,))
                lists.append(kb.capture)
                kb.capture = None
            for i in range(max(len(l) for l in lists)):
                for l in lists:
                    if i < len(l):
                        eng, meth, dk, kw = l[i]
                        kb.op(eng, meth, dma_key=dk, **kw)

    def ffn(l, after_half=None):
        cvf = Carver()
        hT = cvf.get([128, NF, TT], BF16)
        sg = [cvf.get([128, TT]) for _ in range(2)]
        tmp = [cvf.get([128, 512]) for _ in range(8)]
        xs_cur[0] = xs + [cvf.get([128, D]) for _ in range(2)]
        load_gain(0 if l == 0 else 2)
        for f in range(NF):
            sl = loadA(wguS[l].rearrange("(g f) p x -> f p g x", g=2)[f]).rearrange("p k (g c) -> p g k c", g=2) if False else loadA2(wguS[l].rearrange("(g f) p x -> f p g x", g=2)[f])
            pg = ps[4 + (f % 2) * 2]
            pu = ps[5 + (f % 2) * 2]
            for kc in range(8):
                kb.op(PE, "matmul", out=pg[:], lhsT=sl[:, 0, kc, :], rhs=xnT[:, kc, :], start=(kc == 0), stop=(kc == 7))
            for kc in range(8):
                kb.op(PE, "matmul", out=pu[:], lhsT=sl[:, 1, kc, :], rhs=xnT[:, kc, :], start=(kc == 0), stop=(kc == 7))
            kb.op(S_, "activation", out=sg[f % 2], in_=pg[:], func=AF.Silu)
            kb.op(V, "tensor_tensor", out=hT[:, f, :], in0=sg[f % 2], in1=pu[:], op=ALU.mult)
        tokmajor_proj(lambda kc, b: hT[:, kc, b * 128:(b + 1) * 128], 11, wdnS[l], 0 if l == 0 else 2, tmp, after_half)

    carry = sb("carry", [128, 14])
    glu = sb("glu", [128, 4, 30 + TT], BF16)
    STf = sb("STf", [128, 4, 64])
    STb = sb("STb", [128, 4, 64], BF16)

    def mset(eng, ap, val):
        kb.op(eng, "memset", ap=ap, constant=val)

    def mixer(ti, after_half=None):
        cvm = Carver()
        g = cvm.get
        praw = [g([128, 513]) for _ in range(2)]
        dtmp = g([128, 512])
        xwa = g([128, TT])
        xg = g([128, TT])
        cT = g([128, 4, TT])
        csq = g([128, TT])
        mean = g([128, TT])
        rstd = g([128, TT])
        tmp = [g([128, 512]) for _ in range(2)]
        endA = cvm.off
        rkvT = g([128, 12, TT])
        sgx = g([128, TT], BF16)
        xwab = g([128, TT], BF16)
        catT = g([128, 8, TT], BF16)
        endP = cvm.off
        load_gain(1)
        dst_of = {}
        for c in range(12):
            dst_of[c] = rkvT[:, c, :]
        dst_of[12] = xwa
        dst_of[13] = xg
        order = list(range(14)) + [14, 18, 15, 19, 16, 20, 17, 21]
        slots = {}
        sigt = {}
        for n, c in enumerate(order):
            c2 = c // 2
            if c2 not in slots:
                slots[c2] = loadA2(winS[2 * c2:2 * c2 + 2].rearrange("g p x -> p g x"))
            sl = slots[c2]
            pb = ps[4 + n % 4]
            for kc in range(8):
                kb.op(PE, "matmul", out=pb[:], lhsT=sl[:, c % 2, kc, :], rhs=xnT[:, kc, :],
                      start=(kc == 0), stop=(kc == 7))
            if c < 14:
                pr = praw[c % 2]
                kb.op(S_, "activation", out=pr[:, 1:513], in_=pb[:], func=AF.Copy)
                kb.op(G, "tensor_copy", out=pr[:, 0:1], in_=carry[:, c:c + 1])
                kb.op(G, "tensor_tensor", out=dtmp, in0=pr[:, 0:512], in1=pr[:, 1:513], op=ALU.subtract)
                kb.op(V, "scalar_tensor_tensor", out=dst_of[c], in0=dtmp, scalar=cols[:, K_MU + c:K_MU + c + 1],
                      in1=pr[:, 1:513], op0=ALU.mult, op1=ALU.add)
                kb.op(G, "tensor_copy", out=carry[:, c:c + 1], in_=pr[:, 512:513])
            elif c < 18:
                sigt[c] = pb
            else:
                kb.op(S_, "activation", out=tmp[c % 2], in_=pb[:], func=AF.Sigmoid)
                kb.op(V, "tensor_tensor", out=glu[:, c - 18, 30:30 + TT], in0=tmp[c % 2], in1=sigt[c - 4][:], op=ALU.mult)
        kb.op(S_, "activation", out=xwab[0:64, :], in_=xwa[0:64, :], func=AF.Tanh)
        kb.op(V, "tensor_copy", out=xwab[64:128, :], in_=xwa[64:128, :])
        kb.op(S_, "activation", out=sgx, in_=xg, func=AF.Sigmoid)
        def conv_section():
            Lc = []

            class _KBc:
                @staticmethod
                def op(*a, **k):
                    Lc.append(functools_partial(kb_real.op, *a, **k))

            kb = _KBc
            for j in range(4):
                pb = ps[4 + j % 2]
                for tau in range(31):
                    kb.op(PE, "matmul", out=pb[:], lhsT=diag[:, j * 31 + tau, :], rhs=glu[:, j, tau:tau + TT], start=(tau == 0),
                          stop=(tau == 30))
                kb.op(V, "tensor_scalar", out=cT[:, j, :], in0=pb[:], scalar1=cols[:, K_CB + j:K_CB + j + 1], scalar2=None,
                      op0=ALU.add)
            for j in range(4):
                kb.op(G, "tensor_copy", out=glu[:, j, 0:30], in_=glu[:, j, TT:TT + 30])
            onesm = cst[:, C_ONESM:C_ONESM + 128]
            pm, pv = ps[2], ps[3]
            for j in range(4):
                kb.op(PE, "matmul", out=pm[:], lhsT=onesm, rhs=cT[:, j, :], start=(j == 0), stop=(j == 3))
            for j in range(4):
                kb.op(S_, "activation", out=csq, in_=cT[:, j, :], func=AF.Square)
                kb.op(PE, "matmul", out=pv[:], lhsT=onesm, rhs=csq, start=(j == 0), stop=(j == 3))
            kb.op(S_, "activation", out=mean, in_=pm[:], func=AF.Copy)
            kb.op(V, "tensor_tensor", out=rstd, in0=mean, in1=mean, op=ALU.mult)
            kb.op(V, "tensor_tensor", out=rstd, in0=pv[:], in1=rstd, op=ALU.subtract)
            kb.op(V, "tensor_scalar", out=rstd, in0=rstd, scalar1=1e-5, scalar2=None, op0=ALU.add)
            kb.op(S_, "activation", out=rstd, in_=rstd, func=AF.Sqrt)
            kb.op(V, "reciprocal", out=rstd, in_=rstd)
            for j in range(4):
                kb.op(V, "tensor_tensor", out=csq, in0=cT[:, j, :], in1=mean, op=ALU.subtract)
                kb.op(V, "tensor_tensor", out=csq, in0=csq, in1=rstd, op=ALU.mult)
                kb.op(S_, "activation", out=catT[:, 4 + j, :], in_=csq, func=AF.Silu,
                      scale=cols[:, K_LNW + j:K_LNW + j + 1], bias=cols[:, K_LNB + j:K_LNB + j + 1])

            return Lc

        from functools import partial as Pp
        _tail = Carver(start=endP)
        gP = _tail.get
        gA = Carver(start=0, limit=endA, nxt=_tail).get
        g = gP

        def t4():
            return g([128, 4, 128])

        xsm = [g([128, D]) for _ in range(2)]
        tA, tB, tG, tH, tI, sgdB, aB = [t4() for _ in range(7)]
        tC, tD = [xsm[0][:, i * 512:(i + 1) * 512].rearrange("p (j t) -> p j t", j=4) for i in range(2)]
        tE, tF = [xsm[1][:, i * 512:(i + 1) * 512].rearrange("p (j t) -> p j t", j=4) for i in range(2)]
        rk2 = [t4(), t4()]
        sqb = g([128, 4, 128], BF16)
        LT = g([128, 4, 2, 128], BF16)
        FBK = g([128, 4, 2, 128], BF16)
        FZV = g([128, 4, 2, 128], BF16)
        ZR2 = [g([128, 4, 2, 128], BF16) for _ in range(2)]
        wc2 = [g([128, 4, 2]) for _ in range(2)]
        g = gA
        TBK = [g([128, 512], BF16) for _ in range(2)]
        TZ = [g([64, 512], BF16) for _ in range(2)]
        UV = [g([128, 8, 64], BF16) for _ in range(2)]
        AAm = [g([128, 8, 128], BF16) for _ in range(2)]
        ATm = [g([64, 8, 64], BF16) for _ in range(2)]
        Pk = [[g([64, 8, 64], BF16) for _ in range(2)] for _ in range(2)]
        PTk = [[g([64, 8, 64], BF16) for _ in range(2)] for _ in range(2)]
        Tm = [g([64, 8, 64], BF16) for _ in range(2)]
        XvT = [g([64, 8, 64], BF16) for _ in range(2)]
        UVT = [g([64, 8, 64]) for _ in range(2)]
        Y1 = g([128, 8, 64])
        ysq = g([128, 8, 64])
        yyc = [g([128, 8, 64]) for _ in range(2)]
        G1 = [g([128, 512]) for _ in range(2)]
        G0 = [g([128, 8, 64]) for _ in range(2)]
        st8c = [g([128, 48]) for _ in range(2)]
        srkc = [g([128, 8]) for _ in range(2)]

        def v4(ap):
            return ap.rearrange("p j (c t) -> p j c t", c=2)

        def fl(ap):
            return ap.rearrange("p j t -> p (j t)")

        def prep(b):
            L = []
            A = lambda *a, **k: L.append(Pp(kb.op, *a, **k))
            par = b % 2
            ZR, wc, rk = ZR2[par], wc2[par], rk2[par]
            bs = slice(b * 128, (b + 1) * 128)
            rT, kT, vT = rkvT[:, 0:4, bs], rkvT[:, 4:8, bs], rkvT[:, 8:12, bs]
            psd, psa, pn = ps[6], ps[6], ps[6]
            for j in range(4):
                A(PE, "matmul", out=psd[:, j * 128:(j + 1) * 128], lhsT=lorab[0:64, j * 128:(j + 1) * 128],
                  rhs=xwab[0:64, bs], start=True, stop=True)
            for j in range(4):
                A(S_, "activation", out=sgdB[:, j, :], in_=psd[:, j * 128:(j + 1) * 128], func=AF.Sigmoid,
                  bias=cols[:, K_W0 + j:K_W0 + j + 1])
            for j in range(4):
                A(PE, "matmul", out=psa[:, j * 128:(j + 1) * 128], lhsT=lorab[64:128, j * 128:(j + 1) * 128],
                  rhs=xwab[64:128, bs], start=True, stop=True)
            for j in range(4):
                A(S_, "activation", out=aB[:, j, :], in_=psa[:, j * 128:(j + 1) * 128], func=AF.Sigmoid,
                  bias=cols[:, K_A0 + j:K_A0 + j + 1])
            for j in range(4):
                A(V, "tensor_scalar", out=tA[:, j, :], in0=kT[:, j, :], scalar1=cols[:, K_KK + j:K_KK + j + 1],
                  scalar2=None, op0=ALU.mult)
            A(G, "tensor_tensor", out=sqb, in0=tA, in1=tA, op=ALU.mult)
            A(PE, "matmul", out=pn[:], lhsT=bonesb[:], rhs=fl(sqb), start=True, stop=True)
            A(V, "tensor_scalar", out=fl(tB), in0=pn[:], scalar1=1e-12, scalar2=None, op0=ALU.max)
            nsplit = len(L)
            A(S_, "activation", out=fl(tB), in_=fl(tB), func=AF.Ln)
            A(S_, "activation", out=fl(tB), in_=fl(tB), func=AF.Exp, scale=-0.5)
            A(V, "tensor_tensor", out=tA, in0=tA, in1=tB, op=ALU.mult)
            for j in range(4):
                A(V, "tensor_scalar", out=tC[:, j, :], in0=aB[:, j, :], scalar1=-1.0,
                  scalar2=cols[:, K_KA + j:K_KA + j + 1], op0=ALU.add, op1=ALU.mult)
            A(V, "scalar_tensor_tensor", out=tC, in0=tC, scalar=1.0, in1=kT, op0=ALU.add, op1=ALU.mult)
            A(G, "tensor_tensor", out=tD, in0=tA, in1=aB, op=ALU.mult)
            A(G, "tensor_tensor", out=rk, in0=rT, in1=tC, op=ALU.mult)
            for j in range(4):
                A(V, "tensor_tensor_scan", out=tE[:, j, :], data0=cst[:, C_RESET:C_RESET + 128],
                  data1=sgdB[:, j, :], initial=0.0, op0=ALU.mult, op1=ALU.add)
            A(G, "tensor_tensor", out=tF, in0=tE, in1=sgdB, op=ALU.subtract)
            A(G, "tensor_tensor", out=v4(tG), in0=v4(tE)[:, :, :, 63:64].to_broadcast([128, 4, 2, 64]), in1=v4(tE),
              op=ALU.subtract)
            A(S_, "activation", out=tH, in_=tE, func=AF.Exp, scale=-DS)
            A(S_, "activation", out=tI, in_=tE, func=AF.Exp, scale=DS)
            A(S_, "activation", out=tF, in_=tF, func=AF.Exp, scale=-DS)
            A(S_, "activation", out=tG, in_=tG, func=AF.Exp, scale=-DS)
            A(G, "tensor_copy", out=wc, in_=v4(tH)[:, :, :, 63])
            A(G, "tensor_tensor", out=ZR[:, :, :, 64:128], in0=v4(rT), in1=v4(tH), op=ALU.mult)
            A(V, "scalar_tensor_tensor", out=FZV[:, :, :, 0:64], in0=v4(tA), scalar=-1.0, in1=v4(tF), op0=ALU.mult,
              op1=ALU.mult)
            A(G, "tensor_copy", out=FZV[:, :, :, 64:128], in_=v4(vT))
            A(V, "tensor_tensor", out=LT[:, :, :, 0:64], in0=v4(tD), in1=v4(tI), op=ALU.mult)
            A(V, "tensor_tensor", out=LT[:, :, :, 64:128], in0=v4(tC), in1=v4(tI), op=ALU.mult)
            A(G, "tensor_tensor", out=FBK[:, :, :, 0:64], in0=v4(tD), in1=v4(tG), op=ALU.mult)
            A(G, "tensor_tensor", out=FBK[:, :, :, 64:128], in0=v4(tC), in1=v4(tG), op=ALU.mult)
            return L[:nsplit], L[nsplit:]

        def parallel_part(b):
            par = b % 2
            ZR, rk = ZR2[par], rk2[par]
            CS = (0, 1)
            Lp = []

            class _KB:
                @staticmethod
                def op(*a, **k):
                    Lp.append(Pp(kb_real.op, *a, **k))

            kb = _KB
            H = slice(64, 128)
            for c in CS:
                B0 = 3 * c
                pT1, pT2 = ps[B0], ps[B0 + 1]
                for j in range(4):
                    kb.op(PE, "matmul", out=pT1[:, j * 128:(j + 1) * 128], lhsT=FBK[:, j, c, :], rhs=identb[:], start=True,
                          stop=True)
                for j in range(4):
                    kb.op(PE, "matmul", out=pT2[:, j * 128:(j + 1) * 128], lhsT=FZV[:, j, c, :], rhs=identb[:], start=True,
                          stop=True)
                kb.op(S_, "activation", out=TBK[c], in_=pT1[:], func=AF.Copy)
                kb.op(S_, "activation", out=TZ[c], in_=pT2[0:64, :], func=AF.Copy)
                kb.op(S_, "activation", out=UV[c][64:128].rearrange("p h i -> p (h i)"), in_=pT2[64:128, :], func=AF.Copy)
            for c in CS:
                B0 = 3 * c
                for h in range(8):
                    j, hp = h // 2, (h % 2) * 64
                    kb.op(PE, "matmul", out=ps[B0 + h % 2][:, j * 128:j * 128 + 64], lhsT=LT[hp:hp + 64, j, c, :],
                          rhs=FZV[hp:hp + 64, j, c, 0:64], start=True, stop=True)
                    kb.op(PE, "matmul", out=ps[B0 + h % 2][:, j * 128 + 64:(j + 1) * 128], lhsT=LT[hp:hp + 64, j, c, :],
                          rhs=ZR[hp:hp + 64, j, c, 64:128], start=True, stop=True)
                AAv = AAm[c].rearrange("p (j e) q -> p j e q", e=2)
                for e in range(2):
                    kb.op(V, "tensor_tensor", out=AAv[:, :, e, :], in0=ps[B0 + e][:].rearrange("p (j q) -> p j q", q=128),
                          in1=cst[:, C_M128:C_M128 + 128].unsqueeze(1).to_broadcast([128, 4, 128]), op=ALU.mult)
            for c in CS:
                B0 = 3 * c
                for h in range(8):
                    kb.op(PE, "matmul", out=ps[B0 + 2][0:64, h * 64:(h + 1) * 64], lhsT=AAm[c][0:64, h, 0:64],
                          rhs=identb[0:64, 0:64], start=True, stop=True)
                kb.op(S_, "activation", out=ATm[c].rearrange("p h q -> p (h q)"), in_=ps[B0 + 2][0:64, :], func=AF.Copy)
                kb.op(G, "tensor_tensor", out=Tm[c], in0=AAm[c][0:64, :, 0:64],
                      in1=identb[0:64, 0:64].unsqueeze(1).to_broadcast([64, 8, 64]), op=ALU.add)
            cur = {}
            for c in CS:
                cur[c] = ((lambda h, c=c: AAm[c][0:64, h, 0:64]), (lambda h, c=c: ATm[c][:, h, :]))
            for hop in range(1, 7):
                for c in CS:
                    B0 = 3 * c
                    Pc, PTc = cur[c]
                    pP, pPT, pD = ps[B0], ps[B0 + 1], ps[B0 + 2]
                    Tc = Tm[c]
                    if hop <= 4:
                        for h in range(8):
                            kb.op(PE, "matmul", out=pP[0:64, h * 64:(h + 1) * 64], lhsT=PTc(h), rhs=Pc(h), start=True, stop=True)
                    if hop <= 5:
                        for h in range(8):
                            kb.op(PE, "matmul", out=pPT[0:64, h * 64:(h + 1) * 64], lhsT=Pc(h), rhs=PTc(h), start=True, stop=True)
                    if hop >= 2:
                        for h in range(8):
                            kb.op(PE, "matmul", out=pD[0:64, h * 64:(h + 1) * 64], lhsT=PTc(h), rhs=Tc[:, h, :], start=True, stop=True)
                    nP, nPT = Pk[c][hop % 2], PTk[c][hop % 2]
                    if hop <= 4:
                        kb.op(S_, "activation", out=nP.rearrange("p h q -> p (h q)"), in_=pP[0:64, :], func=AF.Copy)
                    if hop <= 5:
                        kb.op(S_, "activation", out=nPT.rearrange("p h q -> p (h q)"), in_=pPT[0:64, :], func=AF.Copy)
                    if hop >= 2:
                        kb.op(V, "tensor_tensor", out=Tc.rearrange("p h q -> p (h q)"), in0=pD[0:64, :],
                              in1=Tc.rearrange("p h q -> p (h q)"), op=ALU.add)
                    cur[c] = ((lambda h, t=nP: t[:, h, :]), (lambda h, t=nPT: t[:, h, :]))
            for c in CS:
                B0 = 3 * c
                for h in range(8):
                    kb.op(PE, "matmul", out=ps[B0][0:64, h * 64:(h + 1) * 64], lhsT=AAm[c][64:128, h, 0:64],
                          rhs=UV[c][64:128, h, :], start=True, stop=True)
                kb.op(S_, "activation", out=XvT[c].rearrange("p h q -> p (h q)"), in_=ps[B0][0:64, :], func=AF.Copy)
            for c in CS:
                B0 = 3 * c
                for h in range(8):
                    kb.op(PE, "matmul", out=ps[B0 + 1][0:64, h * 64:(h + 1) * 64], lhsT=Tm[c][:, h, :], rhs=XvT[c][:, h, :],
                          start=True, stop=True)
                kb.op(S_, "activation", out=UVT[c].rearrange("p h q -> p (h q)"), in_=ps[B0 + 1][0:64, :], func=AF.Copy)
                for h in range(8):
                    j, hp = h // 2, (h % 2) * 64
                    kb.op(PE, "matmul", out=ps[B0 + 2][hp:hp + 64, j * 64:(j + 1) * 64], lhsT=TZ[c][:, h * 64:(h + 1) * 64],
                          rhs=Tm[c][:, h, :], start=True, stop=True)
                kb.op(V, "tensor_copy", out=ZR[:, :, c, 0:64], in_=ps[B0 + 2][:, 0:256].rearrange("p (j q) -> p j q", q=64))
            Lmain = Lp
            Lp = []

            class _KB2:
                @staticmethod
                def op(*a, **k):
                    Lp.append(Pp(kb_real.op, *a, **k))

            kb = _KB2
            for c in CS:
                B0 = 3 * c
                ts = slice(b * 128 + c * 64, b * 128 + c * 64 + 64)
                for j in range(4):
                    kb.op(PE, "matmul", out=ps[B0][H, 0:8], lhsT=rk[:, j, c * 64:(c + 1) * 64],
                          rhs=hselr[:, j * 8:(j + 1) * 8], start=(j == 0), stop=(j == 3))
                kb.op(PE, "matmul", out=ps[B0 + 1][H, :], lhsT=sgx[:, ts], rhs=gupb[:], start=True, stop=True)
                kb.op(V, "tensor_copy", out=srkc[c][H], in_=ps[B0][H, 0:8])
                kb.op(V, "tensor_tensor", out=G1[c][H], in0=ps[B0 + 1][H, :], in1=gnwb[H, 0, :], op=ALU.mult)
                kb.op(G, "tensor_tensor", out=G0[c][H], in0=UV[c][H], in1=srkc[c][H].unsqueeze(2).to_broadcast([64, 8, 64]),
                      op=ALU.mult)
                g0f = G0[c][H].rearrange("p h i -> p (h i)")
                kb.op(G, "tensor_tensor", out=g0f, in0=g0f, in1=gnwb[H, 1, :], op=ALU.add)
                kb.op(V, "tensor_tensor", out=g0f, in0=g0f, in1=ps[B0 + 1][H, :], op=ALU.mult)
            return Lmain, Lp

        def sequential_part(b):
            Ls = [[], []]
            par = b % 2
            ZR, wc, rk = ZR2[par], wc2[par], rk2[par]
            H = slice(64, 128)
            for c in (0, 1):
                A = lambda *a, _c=c, **k: Ls[_c].append(Pp(kb.op, *a, **k))
                B0 = 3 * c
                AA = AAm[c]
                ts = slice(b * 128 + c * 64, b * 128 + c * 64 + 64)
                for h in range(8):
                    j, hp = h // 2, (h % 2) * 64
                    A(PE, "matmul", out=ps[B0 + h % 2][:, j * 64:(j + 1) * 64], lhsT=ZR[hp:hp + 64, j, c, :],
                      rhs=STb[hp:hp + 64, j, :], start=True, stop=True)
                UVv = UV[c].rearrange("p (j e) i -> p j e i", e=2)
                UVTv = UVT[c].rearrange("p (j e) i -> p j e i", e=2)
                Y1v = Y1.rearrange("p (j e) i -> p j e i", e=2)
                for e in range(2):
                    A(V, "tensor_tensor", out=UVv[0:64, :, e, :],
                      in0=ps[B0 + e][0:64, 0:256].rearrange("p (j i) -> p j i", i=64), in1=UVTv[:, :, e, :], op=ALU.add)
                for e in range(2):
                    A(S_, "activation", out=Y1v[64:128, :, e, :],
                      in_=ps[B0 + e][64:128, 0:256].rearrange("p (j i) -> p j i", i=64), func=AF.Copy)
                for h in range(8):
                    j, hp = h // 2, (h % 2) * 64
                    A(PE, "matmul", out=ps[B0 + 2][hp:hp + 64, j * 64:(j + 1) * 64], lhsT=TBK[c][:, h * 64:(h + 1) * 64],
                      rhs=UV[c][:, h, :], start=True, stop=True)
                for j in range(4):
                    A(V, "scalar_tensor_tensor", out=STf[:, j, :], in0=STf[:, j, :], scalar=wc[:, j, c:c + 1],
                      in1=ps[B0 + 2][:, j * 64:(j + 1) * 64], op0=ALU.mult, op1=ALU.add)
                A(S_, "activation", out=STb[:], in_=STf[:], func=AF.Copy)
                for h in range(8):
                    A(PE, "matmul", out=ps[B0][64:128, h * 64:(h + 1) * 64], lhsT=AA[:, h, 64:128], rhs=UV[c][:, h, :],
                      start=True, stop=True)
                A(V, "tensor_tensor", out=yyc[c][H].rearrange("p h i -> p (h i)"), in0=ps[B0][H, :],
                  in1=Y1[H].rearrange("p h i -> p (h i)"), op=ALU.add)
            return Ls

        def post_part(b, c):
            L = []
            A = lambda *a, **k: L.append(Pp(kb.op, *a, **k))
            H = slice(64, 128)
            B0 = 3 * c
            ts = slice(b * 128 + c * 64, b * 128 + c * 64 + 64)
            yy, st8 = yyc[c], st8c[c]
            A(V, "tensor_reduce", out=st8[H, 0:8], in_=yy[H], axis=AX.X, op=ALU.add)
            A(G, "tensor_tensor", out=ysq[H], in0=yy[H], in1=yy[H], op=ALU.mult)
            A(V, "tensor_reduce", out=st8[H, 8:16], in_=ysq[H], axis=AX.X, op=ALU.add)
            A(V, "tensor_scalar", out=st8[H, 16:24], in0=st8[H, 0:8], scalar1=1.0 / 64, scalar2=None, op0=ALU.mult)
            A(V, "tensor_tensor", out=st8[H, 24:32], in0=st8[H, 16:24], in1=st8[H, 16:24], op=ALU.mult)
            A(V, "scalar_tensor_tensor", out=st8[H, 32:40], in0=st8[H, 8:16], scalar=1.0 / 64, in1=st8[H, 24:32],
              op0=ALU.mult, op1=ALU.subtract)
            A(V, "tensor_scalar", out=st8[H, 32:40], in0=st8[H, 32:40], scalar1=64e-5, scalar2=None, op0=ALU.add)
            A(S_, "activation", out=st8[H, 32:40], in_=st8[H, 32:40], func=AF.Sqrt)
            A(V, "reciprocal", out=st8[H, 40:48], in_=st8[H, 32:40])
            A(G, "tensor_tensor", out=yy[H], in0=yy[H], in1=st8[H, 16:24].unsqueeze(2).to_broadcast([64, 8, 64]),
              op=ALU.subtract)
            A(V, "tensor_tensor", out=yy[H], in0=yy[H], in1=st8[H, 40:48].unsqueeze(2).to_broadcast([64, 8, 64]),
              op=ALU.mult)
            yf = yy[H].rearrange("p h i -> p (h i)")
            A(G, "tensor_tensor", out=yf, in0=yf, in1=G1[c][H], op=ALU.mult)
            A(G, "tensor_tensor", out=yf, in0=yf, in1=G0[c][H].rearrange("p h i -> p (h i)"), op=ALU.add)
            ntail = len(L)
            for j in range(4):
                A(PE, "transpose", out=ps[7][:, j * 64:(j + 1) * 64],
                  in_=yy[H, 2 * j:2 * j + 2, :].rearrange("p h i -> p (h i)"), identity=cst[H, C_ID + 64:C_ID + 128])
            A(S_, "activation", out=catT[:, 0:4, ts], in_=ps[7][:, 0:256].rearrange("p (j t) -> p j t", t=64),
              func=AF.Copy)
            return L[:ntail], L[ntail:]

        def merged(La, Lb):
            na, nb = len(La), len(Lb)
            ib = 0
            for ia, f in enumerate(La):
                f()
                tgt = (ia + 1) * nb // max(na, 1)
                while ib < tgt:
                    Lb[ib]()
                    ib += 1
            while ib < nb:
                Lb[ib]()
                ib += 1

        p1, p2 = prep(0)
        merged(conv_section(), p1 + p2)
        pend = []
        for b in range(4):
            Lmain, Lepi = parallel_part(b)
            S0, S1 = sequential_part(b)
            p1, p2 = prep(b + 1) if b < 3 else ([], [])
            pp = p1 + p2
            nfirst = (len(pp) * 3) // 5
            merged(Lmain, pend + pp[:nfirst])
            merged(Lepi + S0 + S1, pp[nfirst:])
            e0, t0 = post_part(b, 0)
            e1, t1 = post_part(b, 1)
            pend = e0 + t0 + e1 + t1
        merged(pend, [])
        xs_cur[0] = xs + xsm
        tokmajor_proj(lambda kc, b: catT[:, kc, b * 128:(b + 1) * 128], 4, woutS, 1,
                      tmp + [fl(tA), fl(tB), fl(tG), fl(tH), fl(tI), fl(sgdB)], after_half)

    mset(G, carry[:], 0.0)
    mset(G, glu[:], 0.0)
    mset(G, STf[:], 0.0)
    mset(G, STb[:], 0.0)

    xv = x_d.rearrange("(n b p) d -> n p b d", b=4, p=128)
    ov = out_d.rearrange("(n b p) d -> n p b d", b=4, p=128)
    last = []

    def load_x(ti, blocks):
        for b in blocks:
            kb.op(SY, "dma_start", dma_key=f"xl{b}", out=xt[:, b, :], in_=xv[ti][:, b, :])

    load_x(0, range(4))
    prenorm(K_G1)
    for ti in range(NT):
        ffn(0, lambda blocks: prenorm(K_GM, blocks))
        mixer(ti, lambda blocks: prenorm(K_G2, blocks))

        def tail_hook(blocks, ti=ti):
            for b in blocks:
                kb.op(SY, "dma_start", dma_key=f"xs{b}", out=ov[ti][:, b, :], in_=xt[:, b, :])
            if ti + 1 < NT:
                load_x(ti + 1, blocks)
                prenorm(K_G1, blocks)

        ffn(1, tail_hook)
    for b in range(4):
        kb.finish([o for o in kb.ops["sync"] if o.dma_key == f"xs{b}"][-1])
    kb.emit()
    st.close()
    return nc


def make_inputs(T, x, p):
    g = lambda k: np.asarray(p[k], np.float32)[0]
    cols = np.zeros((128, K_END), np.float32)
    cols[:, K_MU:K_MU + 14] = _fm(g("shift_mu"), 14)
    cols[:, K_W0:K_W0 + 4] = _fm(g("w0"), 4)
    cols[:, K_A0:K_A0 + 4] = _fm(g("a0"), 4)
    cols[:, K_KK:K_KK + 4] = _fm(g("k_k"), 4)
    cols[:, K_KA:K_KA + 4] = _fm(g("k_a"), 4)
    cols[:, K_RK:K_RK + 4] = _fm(g("r_k").reshape(-1), 4)
    cols[:, K_CB:K_CB + 4] = _fm(g("conv_b"), 4)
    cols[:, K_LNW:K_LNW + 4] = _fm(g("conv_ln_w"), 4)
    cols[:, K_LNB:K_LNB + 4] = _fm(g("conv_ln_b"), 4)
    cols[:, K_G1:K_G1 + 8] = _fm(g("ffn1_norm_pre"), 8)
    cols[:, K_GM:K_GM + 8] = _fm(g("mix_norm_pre"), 8)
    cols[:, K_G2:K_G2 + 8] = _fm(g("ffn2_norm_pre"), 8)
    dw = g("conv_dw")
    cols[:, K_DW:K_DW + 124] = dw.T.reshape(4, 128, 31).transpose(1, 0, 2).reshape(128, 124)
    rows = np.zeros((5, D), np.float32)
    rows[0] = g("ffn1_norm_post")
    rows[1] = g("mix_norm_post")
    rows[2] = g("ffn2_norm_post")
    rows[3, 0:512] = g("gn_w")
    rows[3, 512:1024] = g("gn_b")
    lora = np.concatenate([g("w_up"), g("a_up")], 0)
    shared = {
        "wgu1": g("ffn1_w_gu"), "wgu2": g("ffn2_w_gu"), "wdn1": g("ffn1_w_down"), "wdn2": g("ffn2_w_down"),
        "win": g("w_in"), "wout": g("w_out"), "cols": cols, "consts": _consts(), "rows": rows,
        "lora": np.ascontiguousarray(lora), "gup": g("g_up"),
    }
    return shared


_CACHE = {}


def kernel(**inputs):
    x = np.asarray(inputs["x"], np.float32)
    B, T, _ = x.shape
    shared = make_inputs(T, x, inputs)
    if T not in _CACHE:
        _CACHE[T] = build(T)
    nc = _CACHE[T]
    in_maps = []
    for b in range(B):
        m = dict(shared)
        m["x"] = np.ascontiguousarray(x[b])
        in_maps.append(m)
    res = run_bass_kernel_spmd(nc, in_maps, core_ids=list(range(B)))
    return np.stack([np.asarray(r["out"], np.float32) for r in res.results], 0)
```

```python
import math
from functools import partial as functools_partial
from contextlib import ExitStack
import numpy as np
import concourse.bass as bass
import concourse.mybir as mybir
from concourse.bass_utils import run_bass_kernel_spmd

F32 = mybir.dt.float32
BF16 = mybir.dt.bfloat16
AF = mybir.ActivationFunctionType
ALU = mybir.AluOpType
AX = mybir.AxisListType

D = 1024
DFF = 2816
NF = 22
DRW = 512
DS = math.exp(-0.5)
TT = 512
WRITE_KEYS = ("out", "accum_out", "ap")
ENGS = ("sync", "tensor", "scalar", "vector", "gpsimd")


def _is_ap(v):
    return hasattr(v, "tensor") and hasattr(v, "ap") and hasattr(v, "offset")


def _region(ap):
    t = ap.tensor
    es = mybir.dt.size(ap.dtype)
    dims = [(int(s), int(c)) for s, c in ap.ap]
    off = int(ap.offset) * es
    space = str(ap.space)
    if "SB" in space or "PS" in space.upper():
        shp = [int(s) for s in t.shape]
        pstride = int(np.prod(shp[1:])) * mybir.dt.size(t.dtype)
        p0 = off // pstride
        f0 = off % pstride
        pc = dims[0][1]
        lo = hi = 0
        for s, c in dims[1:]:
            e = s * (c - 1) * es
            if e < 0:
                lo += e
            else:
                hi += e
        if "SB" not in space:
            return t.name, (p0 // 32 * 32, (p0 + pc + 31) // 32 * 32, 0, pstride)
        return t.name, (p0, p0 + pc, f0 + lo, f0 + hi + es)
    lo = hi = 0
    for s, c in dims:
        e = s * (c - 1) * es
        if e < 0:
            lo += e
        else:
            hi += e
    return "dram:" + t.name, (0, 1, off + lo, off + hi + es)


def _ovl(a, b):
    return a[0] < b[1] and b[0] < a[1] and a[2] < b[3] and b[2] < a[3]


def _contains(a, b):
    return a[0] <= b[0] and a[1] >= b[1] and a[2] <= b[2] and a[3] >= b[3]


class Op:
    __slots__ = ("eng", "meth", "kw", "deps", "marked", "dma_key", "tok", "idx")


class KB:
    def __init__(self, nc):
        self.nc = nc
        self.ops = {e: [] for e in ENGS}
        self.acc = {}
        self.dma_cnt = {}
        self.final = []
        self.capture = None

    def op(self, eng, meth, dma_key=None, **kw):
        if self.capture is not None:
            self.capture.append((eng, meth, dma_key, kw))
            return None
        o = Op()
        o.eng, o.meth, o.kw, o.marked, o.dma_key = eng, meth, kw, False, dma_key
        o.idx = len(self.ops[eng])
        is_dma = meth == "dma_start"
        if is_dma:
            assert dma_key is not None
            self.dma_cnt[dma_key] = self.dma_cnt.get(dma_key, 0) + 16
            o.tok = (dma_key, self.dma_cnt[dma_key])
        else:
            o.tok = None
        deps = {}
        accs = []
        for name, v in kw.items():
            if not _is_ap(v):
                continue
            kind = "w" if name in WRITE_KEYS else "r"
            tn, box = _region(v)
            accs.append((tn, box, kind))
            d = self.acc.get(tn)
            if not d:
                continue
            for (k2, box2), (o2, _) in d.items():
                if kind == "r" and k2[1] == "r":
                    continue
                if o2 is o:
                    continue
                if _ovl(box, box2):
                    if o2.eng == "tensor" and eng == "tensor" and o2.meth != "dma_start" and not is_dma:
                        continue
                    key = (o2.eng, o2.dma_key)
                    if key not in deps or deps[key].idx < o2.idx:
                        deps[key] = o2
        for o2 in deps.values():
            if o2.meth != "dma_start":
                o2.marked = True
        o.deps = list(deps.values())
        me = (eng, dma_key)
        for tn, box, kind in accs:
            d = self.acc.setdefault(tn, {})
            if kind == "w":
                for k in [k for k in d if _contains(box, k[1])]:
                    del d[k]
            d[((me, kind), box)] = (o, kind)
        self.ops[eng].append(o)
        return o

    def finish(self, o):
        self.final.append(o)

    def emit(self):
        nc = self.nc
        for e in ENGS:
            c = 0
            for o in self.ops[e]:
                if o.meth != "dma_start" and o.marked:
                    c += 1
                    o.tok = (e, c)
        with ExitStack() as st:
            sems = {}
            for e in ENGS:
                sems[e] = st.enter_context(nc.semaphore("s_" + e))
            for k in self.dma_cnt:
                sems[k] = st.enter_context(nc.semaphore("d_" + k))
            block = st.enter_context(nc.Block())

            def run(eng, ename):
                waited = {}
                for o in self.ops[ename]:
                    for o2 in o.deps:
                        k, v = o2.tok
                        if waited.get(k, 0) < v:
                            eng.wait_ge(sems[k], v)
                            waited[k] = v
                    inst = getattr(eng, o.meth)(**o.kw)
                    if o.meth == "dma_start":
                        inst.then_inc(sems[o.dma_key], 16)
                    elif o.marked:
                        inst.then_inc(sems[ename], 1)
                if ename == "sync":
                    for o2 in self.final:
                        k, v = o2.tok
                        if waited.get(k, 0) < v:
                            eng.wait_ge(sems[k], v)
                            waited[k] = v

            @block.sync
            def _(eng):
                run(eng, "sync")

            @block.tensor
            def _(eng):
                run(eng, "tensor")

            @block.scalar
            def _(eng):
                run(eng, "scalar")

            @block.vector
            def _(eng):
                run(eng, "vector")

            @block.gpsimd
            def _(eng):
                run(eng, "gpsimd")


C_ID = 0
C_M128 = 128
C_MAT = 256
C_BONES = 320
C_ONESM = 448
C_RESET = 576
C_HSEL = 704
C_NEGH = 736
C_END = 737


def _consts():
    c = np.zeros((128, C_END), np.float32)
    c[:, C_ID:C_ID + 128] = np.eye(128)
    p = np.arange(128)[:, None] % 64
    q = np.arange(128)[None, :]
    c[:, C_M128:C_M128 + 128] = np.where(q < 64, p < q, p <= (q - 64))
    t = np.arange(128)[:, None] % 64
    j = np.arange(64)[None, :]
    c[:, C_MAT:C_MAT + 64] = (j < t)
    c[:, C_BONES:C_BONES + 128] = (np.arange(128)[:, None] // 64 == np.arange(128)[None, :] // 64)
    c[:, C_ONESM:C_ONESM + 128] = 1.0 / 512.0
    r = np.ones((128, 128), np.float32)
    r[:, 0::64] = 0.0
    c[:, C_RESET:C_RESET + 128] = r
    hs = np.zeros((128, 4, 8), np.float32)
    for jj in range(4):
        hs[0:64, jj, 2 * jj] = 1.0
        hs[64:128, jj, 2 * jj + 1] = 1.0
    c[:, C_HSEL:C_HSEL + 32] = hs.reshape(128, 32)
    c[:, C_NEGH] = -0.5
    return c


K_MU = 0
K_W0 = 14
K_A0 = 18
K_KK = 22
K_KA = 26
K_RK = 30
K_CB = 34
K_LNW = 38
K_LNB = 42
K_G1 = 46
K_GM = 54
K_G2 = 62
K_DW = 70
K_END = 70 + 124


def _fm(v, n):
    return np.ascontiguousarray(np.asarray(v, np.float32).reshape(n, 128).T)


def build(T, stage=99, sub=99):
    NT = T // TT
    nc = bass.Bass("TRN2", target_bir_lowering=False)
    kb = KB(nc)
    kb_real = kb

    def din(name, shape, dt=F32):
        return nc.dram_tensor(name, list(shape), dt, kind="ExternalInput").ap()

    x_d = din("x", [T, D])
    out_d = nc.dram_tensor("out", [T, D], F32, kind="ExternalOutput").ap()
    wgu_d = [din("wgu1", [D, 2 * DFF]), din("wgu2", [D, 2 * DFF])]
    wdn_d = [din("wdn1", [DFF, D]), din("wdn2", [DFF, D])]
    win_d = din("win", [D, 2816])
    wout_d = din("wout", [D, D])
    cols_d = din("cols", [128, K_END])
    const_d = din("consts", [128, C_END])
    rows_d = din("rows", [5, D])
    lora_d = din("lora", [128, 512])
    gup_d = din("gup", [128, 512])

    def dscr(name, shape):
        return nc.dram_tensor(name, list(shape), BF16, kind="Internal").ap()

    wguS = [dscr("wguS1", [2 * NF, 128, 1024]), dscr("wguS2", [2 * NF, 128, 1024])]
    wdnS = [dscr("wdnS1", [2, 11, 128, 2, 512]), dscr("wdnS2", [2, 11, 128, 2, 512])]
    winS = dscr("winS", [NF, 128, 1024])
    woutS = dscr("woutS", [2, 4, 128, 2, 512])

    st = ExitStack()

    def sb(name, shape, dt=F32):
        return st.enter_context(nc.sbuf_tensor(name, list(shape), dt))

    cst = sb("cst", [128, C_END])
    cols = sb("cols_sb", [128, K_END])
    gcur = sb("gcur", [128, D])
    gnwb = sb("gnwb", [128, 2, 512])
    identb = sb("identb", [128, 128], BF16)
    bonesb = sb("bonesb", [128, 128], BF16)
    lorab = sb("lorab", [128, 512], BF16)
    gupb = sb("gupb", [128, 512], BF16)
    diag = sb("diag", [128, 124, 128], BF16)
    xt = sb("xt", [128, 4, D])
    slotA = [sb(f"slotA{i}", [128, 8, 256], BF16) for i in range(3)]
    slotB = [sb(f"slotB{i}", [128, 2, 512], BF16) for i in range(3)]
    sstat = sb("sstat", [128, 32])
    hselr = sb("hselr", [128, 32])
    ARENA = 121 * 1024
    arena = sb("arena", [128, ARENA // 4])
    ident = cst[:, C_ID:C_ID + 128]

    class Carver:
        def __init__(self, start=18 * 1024, limit=ARENA, nxt=None):
            self.off = start
            self.limit = limit
            self.nxt = nxt

        def get(self, shape, dt=F32):
            n = int(np.prod(shape[1:])) * mybir.dt.size(dt)
            n = (n + 31) // 32 * 32
            if self.off + n > self.limit and self.nxt is not None:
                return self.nxt.get(shape, dt)
            assert self.off + n <= self.limit, (self.off, n, self.limit)
            a = arena[:, self.off // 4:(self.off + n) // 4]
            self.off += n
            if dt != F32:
                a = a.bitcast(dt)
            a = a[0:shape[0], 0:int(np.prod(shape[1:]))]
            if len(shape) == 3:
                a = a.rearrange("p (a b) -> p a b", b=shape[2])
            elif len(shape) == 4:
                a = a.rearrange("p (a b c) -> p a b c", b=shape[2], c=shape[3])
            return a

    _c0 = Carver(start=0)
    xs = [_c0.get([128, D]), _c0.get([128, D])]
    xs_cur = [xs]
    xnT = _c0.get([128, 8, TT], BF16)
    gjunk = _c0.get([128, D], BF16)
    assert _c0.off == 18 * 1024
    ps = [st.enter_context(nc.psum_tensor(f"ps{i}", [128, 512], F32)) for i in range(8)]

    V, S_, G, PE, SY = "vector", "scalar", "gpsimd", "tensor", "sync"

    kb.op(SY, "dma_start", dma_key="c0", out=cst[:], in_=const_d)
    kb.op(SY, "dma_start", dma_key="c1", out=cols[:], in_=cols_d)
    kb.op(SY, "dma_start", dma_key="c3", out=gnwb[:].rearrange("p a b -> p (a b)"),
          in_=rows_d[3:4, :].partition_broadcast(128))
    cv = Carver(start=0)
    stg0 = cv.get([128, 512])
    stg1 = cv.get([128, 512])
    kb.op(SY, "dma_start", dma_key="c4", out=stg0, in_=lora_d)
    kb.op(SY, "dma_start", dma_key="c5", out=stg1, in_=gup_d)
    kb.op(V, "tensor_copy", out=lorab[:], in_=stg0)
    kb.op(V, "tensor_copy", out=gupb[:], in_=stg1)
    kb.op(V, "tensor_copy", out=identb[:], in_=ident)
    kb.op(V, "tensor_copy", out=bonesb[:], in_=cst[:, C_BONES:C_BONES + 128])
    for j in range(4):
        kb.op(V, "tensor_scalar", out=hselr[:, j * 8:(j + 1) * 8], in0=cst[:, C_HSEL + j * 8:C_HSEL + j * 8 + 8],
              scalar1=cols[:, K_RK + j:K_RK + j + 1], scalar2=None, op0=ALU.mult)

    NSTG, NACC = 4, 3
    stage32 = [cv.get([128, 2048]) for _ in range(NSTG)]
    accb = [cv.get([128, 11, 1024], BF16) for _ in range(NACC)]
    cctr = [0]
    sctr = [0]
    hctr = [0]

    def cast(out, in_):
        eng = (S_, V)[cctr[0] % 2]
        cctr[0] += 1
        if eng == S_:
            kb.op(S_, "activation", out=out, in_=in_, func=AF.Copy)
        else:
            kb.op(eng, "tensor_copy", out=out, in_=in_)

    pend_st = []

    def flush_st():
        while pend_st:
            pend_st.pop(0)()

    def conv_A(src, dstS, nhalves):
        for qq in range(nhalves * 2):
            ai = hctr[0] % NACC
            a = accb[ai]
            hctr[0] += 1
            av = a.rearrange("p f (k c) -> p f k c", c=128)
            for kc in range(8):
                i = sctr[0] % NSTG
                sctr[0] += 1
                stg = stage32[i][:, 0:1408]
                kb.op(SY, "dma_start", dma_key=f"cv{i}", out=stg, in_=src[kc * 128:(kc + 1) * 128, qq * 1408:(qq + 1) * 1408])
                cast(av[:, :, kc, :], stg.rearrange("p (f c) -> p f c", c=128))
                if kc == 3:
                    flush_st()
            pend_st.append(lambda ai=ai, qq=qq, a=a, dstS=dstS: kb.op(
                SY, "dma_start", dma_key=f"csA{ai}", out=dstS[qq * 11:qq * 11 + 11].rearrange("f p x -> p f x"), in_=a[:]))

    def conv_B(src, dstS, nk2):
        sv = src.rearrange("(k s p) n -> k p s n", s=2, p=128)
        for k2 in range(nk2):
            i = sctr[0] % NSTG
            sctr[0] += 1
            stg = stage32[i]
            ai = hctr[0] % NACC
            a = accb[ai]
            hctr[0] += 1
            bst = a.rearrange("p f x -> p (f x)")[:, 0:2048]
            kb.op(SY, "dma_start", dma_key=f"cv{i}", out=stg.rearrange("p (s n) -> p s n", s=2), in_=sv[k2])
            cast(bst.rearrange("p (h s n) -> p h s n", h=2, s=2), stg.rearrange("p (s h n) -> p h s n", s=2, h=2))
            flush_st()
            for hf in range(2):
                pend_st.append(lambda ai=ai, hf=hf, k2=k2, bst=bst, dstS=dstS: kb.op(
                    SY, "dma_start", dma_key=f"csB{ai}_{hf}", out=dstS[hf, k2].rearrange("p s n -> p (s n)"),
                    in_=bst[:, hf * 1024:(hf + 1) * 1024]))

    conv_A(wgu_d[0], wguS[0], 2)
    conv_B(wdn_d[0], wdnS[0], 11)
    conv_A(win_d, winS, 1)
    conv_B(wout_d, woutS, 4)
    conv_A(wgu_d[1], wguS[1], 2)
    conv_B(wdn_d[1], wdnS[1], 11)
    flush_st()
    for i in range(124):
        kb.op(G, "tensor_scalar", out=diag[:, i, :], in0=ident,
              scalar1=cols[:, K_DW + i:K_DW + i + 1], scalar2=None, op0=ALU.mult)

    actr = [0]
    bctr = [0]

    def loadA(src):
        i = actr[0] % 3
        actr[0] += 1
        kb.op(SY, "dma_start", dma_key=f"A{i}", out=slotA[i][:], in_=src)
        return slotA[i]

    def loadA2(src):
        i = actr[0] % 3
        actr[0] += 1
        v = slotA[i][:].rearrange("p k c -> p (k c)").rearrange("p (g x) -> p g x", g=2)
        kb.op(SY, "dma_start", dma_key=f"A{i}", out=v, in_=src)
        return slotA[i][:].rearrange("p k c -> p (k c)").rearrange("p (g k c) -> p g k c", g=2, k=8)

    def loadB(src):
        i = bctr[0] % 3
        bctr[0] += 1
        kb.op(SY, "dma_start", dma_key=f"B{i}", out=slotB[i][:], in_=src)
        return slotB[i]

    def rstd_from_ss(ss_ap, out_ap, n, eps, npart=128):
        kb.op(V, "tensor_scalar", out=out_ap, in0=ss_ap, scalar1=1.0 / n, scalar2=eps, op0=ALU.mult, op1=ALU.add)
        kb.op(S_, "activation", out=out_ap, in_=out_ap, func=AF.Sqrt)
        kb.op(V, "reciprocal", out=out_ap, in_=out_ap)

    def prenorm_stats(b, src):
        kb.op(S_, "activation", out=gjunk, in_=src, func=AF.Square, accum_out=sstat[:, b:b + 1])
        rstd_from_ss(sstat[:, b:b + 1], sstat[:, 4 + b:5 + b], D, 1e-6)
        xb = xs_cur[0][b % len(xs_cur[0])]
        kb.op(V, "tensor_scalar", out=xb, in0=src, scalar1=sstat[:, 4 + b:5 + b], scalar2=None, op0=ALU.mult)
        return xb

    def prenorm_transpose(gcol, b, xb):
        for hf in range(2):
            pb = ps[b * 2 + hf]
            for q in range(4):
                kc = hf * 4 + q
                kb.op(PE, "transpose", out=pb[:, q * 128:(q + 1) * 128], in_=xb[:, kc * 128:(kc + 1) * 128],
                      identity=ident)
            kb.op(V, "tensor_tensor", out=xnT[:, hf * 4:hf * 4 + 4, b * 128:(b + 1) * 128],
                  in0=pb[:].rearrange("p (a b) -> p a b", b=128),
                  in1=cols[:, gcol + hf * 4:gcol + hf * 4 + 4].unsqueeze(2).to_broadcast([128, 4, 128]),
                  op=ALU.mult)

    def prenorm(gcol, blocks=(0, 1, 2, 3)):
        for b in blocks:
            xb = prenorm_stats(b, xt[:, b, :])
            prenorm_transpose(gcol, b, xb)

    def load_gain(gi):
        kb.op(SY, "dma_start", dma_key="gc", out=gcur[:], in_=rows_d[gi:gi + 1, :].partition_broadcast(128))
        if gi != 1:
            kb.op(V, "tensor_scalar", out=gcur[:], in0=gcur[:], scalar1=0.5, scalar2=None, op0=ALU.mult)

    def postnorm_residual(gi, banks, tmp, blocks):
        for b in blocks:
            c0 = 16 + 4 * b
            for hf in range(2):
                kb.op(S_, "activation", out=gjunk[:, hf * 512:(hf + 1) * 512], in_=banks[b][hf][:], func=AF.Square,
                      accum_out=sstat[:, c0 + hf:c0 + hf + 1])
            kb.op(V, "tensor_tensor", out=sstat[:, c0 + 2:c0 + 3], in0=sstat[:, c0:c0 + 1], in1=sstat[:, c0 + 1:c0 + 2], op=ALU.add)
            rstd_from_ss(sstat[:, c0 + 2:c0 + 3], sstat[:, c0 + 3:c0 + 4], D, 1e-6)
            for hf in range(2):
                t = tmp[(b * 2 + hf) % len(tmp)]
                kb.op(V, "scalar_tensor_tensor", out=t, in0=banks[b][hf][:], scalar=sstat[:, c0 + 3:c0 + 4],
                      in1=gcur[:, hf * 512:(hf + 1) * 512], op0=ALU.mult, op1=ALU.mult)
                kb.op(G, "tensor_tensor", out=xt[:, b, hf * 512:(hf + 1) * 512], in0=xt[:, b, hf * 512:(hf + 1) * 512],
                      in1=t, op=ALU.add)

    def tokmajor_proj(lhs_fn, nk2, wS, gi, tmp, after_half=None):
        banks = {b: [ps[b * 2 + hf] for hf in range(2)] for b in range(4)}
        for hf in range(2):
            for k2 in range(nk2):
                sl = loadB(wS[hf, k2])
                for s in range(2):
                    kc = k2 * 2 + s
                    for b in range(4):
                        kb.op(PE, "matmul", out=banks[b][hf][:], lhsT=lhs_fn(kc, b), rhs=sl[:, s, :],
                              start=(kc == 0), stop=(kc == nk2 * 2 - 1))
        for pair in ((0, 1, 2, 3),):
            lists = []
            for b in pair:
                kb.capture = []
                postnorm_residual(gi, banks, tmp, (b,))
                if after_half is not None:
                    after_half((b,))
                lists.append(kb.capture)
                kb.capture = None
            for i in range(max(len(l) for l in lists)):
                for l in lists:
                    if i < len(l):
                        eng, meth, dk, kw = l[i]
                        kb.op(eng, meth, dma_key=dk, **kw)

    def ffn(l, after_half=None, mid_hook=None):
        cvf = Carver()
        hT = cvf.get([128, NF, TT], BF16)
        sg = [cvf.get([128, TT]) for _ in range(2)]
        tmp = [cvf.get([128, 512]) for _ in range(8)]
        xs_cur[0] = xs + [cvf.get([128, D]) for _ in range(2)]
        load_gain(0 if l == 0 else 2)
        for f in range(NF):
            sl = loadA(wguS[l].rearrange("(g f) p x -> f p g x", g=2)[f]).rearrange("p k (g c) -> p g k c", g=2) if False else loadA2(wguS[l].rearrange("(g f) p x -> f p g x", g=2)[f])
            pg = ps[4 + (f % 2) * 2]
            pu = ps[5 + (f % 2) * 2]
            for kc in range(8):
                kb.op(PE, "matmul", out=pg[:], lhsT=sl[:, 0, kc, :], rhs=xnT[:, kc, :], start=(kc == 0), stop=(kc == 7))
            for kc in range(8):
                kb.op(PE, "matmul", out=pu[:], lhsT=sl[:, 1, kc, :], rhs=xnT[:, kc, :], start=(kc == 0), stop=(kc == 7))
            kb.op(S_, "activation", out=sg[f % 2], in_=pg[:], func=AF.Silu)
            kb.op(V, "tensor_tensor", out=hT[:, f, :], in0=sg[f % 2], in1=pu[:], op=ALU.mult)
        if mid_hook is not None:
            mid_hook(cvf)
        tokmajor_proj(lambda kc, b: hT[:, kc, b * 128:(b + 1) * 128], 11, wdnS[l], 0 if l == 0 else 2, tmp, after_half)

    carry = sb("carry", [128, 14])
    glu = sb("glu", [128, 4, 30 + TT], BF16)
    STf = sb("STf", [128, 4, 64])
    STb = sb("STb", [128, 4, 64], BF16)

    def mset(eng, ap, val):
        kb.op(eng, "memset", ap=ap, constant=val)

    def mixer(ti, after_half=None):
        cvm = Carver()
        g = cvm.get
        praw = [g([128, 513]) for _ in range(2)]
        dtmp = g([128, 512])
        xwa = g([128, TT])
        xg = g([128, TT])
        cT = g([128, 4, TT])
        csq = g([128, TT])
        mean = g([128, TT])
        rstd = g([128, TT])
        tmp = [g([128, 512]) for _ in range(2)]
        endA = cvm.off
        rkvT = g([128, 12, TT])
        sgx = g([128, TT], BF16)
        xwab = g([128, TT], BF16)
        catT = g([128, 8, TT], BF16)
        endP = cvm.off
        load_gain(1)
        dst_of = {}
        for c in range(12):
            dst_of[c] = rkvT[:, c, :]
        dst_of[12] = xwa
        dst_of[13] = xg
        order = list(range(14)) + [14, 18, 15, 19, 16, 20, 17, 21]
        slots = {}
        sigt = {}
        for n, c in enumerate(order):
            c2 = c // 2
            if c2 not in slots:
                slots[c2] = loadA2(winS[2 * c2:2 * c2 + 2].rearrange("g p x -> p g x"))
            sl = slots[c2]
            pb = ps[4 + n % 4]
            for kc in range(8):
                kb.op(PE, "matmul", out=pb[:], lhsT=sl[:, c % 2, kc, :], rhs=xnT[:, kc, :],
                      start=(kc == 0), stop=(kc == 7))
            if c < 14:
                pr = praw[c % 2]
                kb.op(S_, "activation", out=pr[:, 1:513], in_=pb[:], func=AF.Copy)
                kb.op(G, "tensor_copy", out=pr[:, 0:1], in_=carry[:, c:c + 1])
                kb.op(G, "tensor_tensor", out=dtmp, in0=pr[:, 0:512], in1=pr[:, 1:513], op=ALU.subtract)
                kb.op(V, "scalar_tensor_tensor", out=dst_of[c], in0=dtmp, scalar=cols[:, K_MU + c:K_MU + c + 1],
                      in1=pr[:, 1:513], op0=ALU.mult, op1=ALU.add)
                kb.op(G, "tensor_copy", out=carry[:, c:c + 1], in_=pr[:, 512:513])
            elif c < 18:
                sigt[c] = pb
            else:
                kb.op(S_, "activation", out=tmp[c % 2], in_=pb[:], func=AF.Sigmoid)
                kb.op(V, "tensor_tensor", out=glu[:, c - 18, 30:30 + TT], in0=tmp[c % 2], in1=sigt[c - 4][:], op=ALU.mult)
        kb.op(S_, "activation", out=xwab[0:64, :], in_=xwa[0:64, :], func=AF.Tanh)
        kb.op(V, "tensor_copy", out=xwab[64:128, :], in_=xwa[64:128, :])
        kb.op(S_, "activation", out=sgx, in_=xg, func=AF.Sigmoid)
        def conv_section():
            Lc = []

            class _KBc:
                @staticmethod
                def op(*a, **k):
                    Lc.append(functools_partial(kb_real.op, *a, **k))

            kb = _KBc
            for j in range(4):
                pb = ps[4 + j % 2]
                for tau in range(31):
                    kb.op(PE, "matmul", out=pb[:], lhsT=diag[:, j * 31 + tau, :], rhs=glu[:, j, tau:tau + TT], start=(tau == 0),
                          stop=(tau == 30))
                kb.op(V, "tensor_scalar", out=cT[:, j, :], in0=pb[:], scalar1=cols[:, K_CB + j:K_CB + j + 1], scalar2=None,
                      op0=ALU.add)
            for j in range(4):
                kb.op(G, "tensor_copy", out=glu[:, j, 0:30], in_=glu[:, j, TT:TT + 30])
            onesm = cst[:, C_ONESM:C_ONESM + 128]
            pm, pv = ps[2], ps[3]
            for j in range(4):
                kb.op(PE, "matmul", out=pm[:], lhsT=onesm, rhs=cT[:, j, :], start=(j == 0), stop=(j == 3))
            for j in range(4):
                kb.op(S_, "activation", out=csq, in_=cT[:, j, :], func=AF.Square)
                kb.op(PE, "matmul", out=pv[:], lhsT=onesm, rhs=csq, start=(j == 0), stop=(j == 3))
            kb.op(S_, "activation", out=mean, in_=pm[:], func=AF.Copy)
            kb.op(V, "tensor_tensor", out=rstd, in0=mean, in1=mean, op=ALU.mult)
            kb.op(V, "tensor_tensor", out=rstd, in0=pv[:], in1=rstd, op=ALU.subtract)
            kb.op(V, "tensor_scalar", out=rstd, in0=rstd, scalar1=1e-5, scalar2=None, op0=ALU.add)
            kb.op(S_, "activation", out=rstd, in_=rstd, func=AF.Sqrt)
            kb.op(V, "reciprocal", out=rstd, in_=rstd)
            for j in range(4):
                kb.op(V, "tensor_tensor", out=csq, in0=cT[:, j, :], in1=mean, op=ALU.subtract)
                kb.op(V, "tensor_tensor", out=csq, in0=csq, in1=rstd, op=ALU.mult)
                kb.op(S_, "activation", out=catT[:, 4 + j, :], in_=csq, func=AF.Silu,
                      scale=cols[:, K_LNW + j:K_LNW + j + 1], bias=cols[:, K_LNB + j:K_LNB + j + 1])

            return Lc

        from functools import partial as Pp
        _tail = Carver(start=endP)
        gP = _tail.get
        gA = Carver(start=0, limit=endA, nxt=_tail).get
        g = gP

        def t4():
            return g([128, 4, 128])

        xsm = [g([128, D]) for _ in range(2)]
        tA, tB, tG, tH, tI, sgdB, aB = [t4() for _ in range(7)]
        tC, tD = [xsm[0][:, i * 512:(i + 1) * 512].rearrange("p (j t) -> p j t", j=4) for i in range(2)]
        tE, tF = [xsm[1][:, i * 512:(i + 1) * 512].rearrange("p (j t) -> p j t", j=4) for i in range(2)]
        rk2 = [t4(), t4()]
        sqb = g([128, 4, 128], BF16)
        LT = g([128, 4, 2, 128], BF16)
        FBK = g([128, 4, 2, 128], BF16)
        FZV = g([128, 4, 2, 128], BF16)
        ZR2 = [g([128, 4, 2, 128], BF16) for _ in range(2)]
        wc2 = [g([128, 4, 2]) for _ in range(2)]
        g = gA
        TBK = [g([128, 512], BF16) for _ in range(2)]
        TZ = [g([64, 512], BF16) for _ in range(2)]
        UV = [g([128, 8, 64], BF16) for _ in range(2)]
        AAm = [g([128, 8, 128], BF16) for _ in range(2)]
        ATm = [g([64, 8, 64], BF16) for _ in range(2)]
        Pk = [[g([64, 8, 64], BF16) for _ in range(2)] for _ in range(2)]
        PTk = [[g([64, 8, 64], BF16) for _ in range(2)] for _ in range(2)]
        Tm = [g([64, 8, 64], BF16) for _ in range(2)]
        XvT = [g([64, 8, 64], BF16) for _ in range(2)]
        UVT = [g([64, 8, 64]) for _ in range(2)]
        Y1 = g([128, 8, 64])
        ysq = g([128, 8, 64])
        yyc = [g([128, 8, 64]) for _ in range(2)]
        G1 = [g([128, 512]) for _ in range(2)]
        G0 = [g([128, 8, 64]) for _ in range(2)]
        st8c = [g([128, 48]) for _ in range(2)]
        srkc = [g([128, 8]) for _ in range(2)]

        def v4(ap):
            return ap.rearrange("p j (c t) -> p j c t", c=2)

        def fl(ap):
            return ap.rearrange("p j t -> p (j t)")

        def prep(b):
            L = []
            A = lambda *a, **k: L.append(Pp(kb.op, *a, **k))
            par = b % 2
            ZR, wc, rk = ZR2[par], wc2[par], rk2[par]
            bs = slice(b * 128, (b + 1) * 128)
            rT, kT, vT = rkvT[:, 0:4, bs], rkvT[:, 4:8, bs], rkvT[:, 8:12, bs]
            psd, psa, pn = ps[6], ps[6], ps[6]
            for j in range(4):
                A(PE, "matmul", out=psd[:, j * 128:(j + 1) * 128], lhsT=lorab[0:64, j * 128:(j + 1) * 128],
                  rhs=xwab[0:64, bs], start=True, stop=True)
            for j in range(4):
                A(S_, "activation", out=sgdB[:, j, :], in_=psd[:, j * 128:(j + 1) * 128], func=AF.Sigmoid,
                  bias=cols[:, K_W0 + j:K_W0 + j + 1])
            for j in range(4):
                A(PE, "matmul", out=psa[:, j * 128:(j + 1) * 128], lhsT=lorab[64:128, j * 128:(j + 1) * 128],
                  rhs=xwab[64:128, bs], start=True, stop=True)
            for j in range(4):
                A(S_, "activation", out=aB[:, j, :], in_=psa[:, j * 128:(j + 1) * 128], func=AF.Sigmoid,
                  bias=cols[:, K_A0 + j:K_A0 + j + 1])
            for j in range(4):
                A(V, "tensor_scalar", out=tA[:, j, :], in0=kT[:, j, :], scalar1=cols[:, K_KK + j:K_KK + j + 1],
                  scalar2=None, op0=ALU.mult)
            A(G, "tensor_tensor", out=sqb, in0=tA, in1=tA, op=ALU.mult)
            A(PE, "matmul", out=pn[:], lhsT=bonesb[:], rhs=fl(sqb), start=True, stop=True)
            A(V, "tensor_scalar", out=fl(tB), in0=pn[:], scalar1=1e-12, scalar2=None, op0=ALU.max)
            nsplit = len(L)
            A(S_, "activation", out=fl(tB), in_=fl(tB), func=AF.Ln)
            A(S_, "activation", out=fl(tB), in_=fl(tB), func=AF.Exp, scale=-0.5)
            A(V, "tensor_tensor", out=tA, in0=tA, in1=tB, op=ALU.mult)
            for j in range(4):
                A(V, "tensor_scalar", out=tC[:, j, :], in0=aB[:, j, :], scalar1=-1.0,
                  scalar2=cols[:, K_KA + j:K_KA + j + 1], op0=ALU.add, op1=ALU.mult)
            A(V, "scalar_tensor_tensor", out=tC, in0=tC, scalar=1.0, in1=kT, op0=ALU.add, op1=ALU.mult)
            A(G, "tensor_tensor", out=tD, in0=tA, in1=aB, op=ALU.mult)
            A(G, "tensor_tensor", out=rk, in0=rT, in1=tC, op=ALU.mult)
            for j in range(4):
                A(V, "tensor_tensor_scan", out=tE[:, j, :], data0=cst[:, C_RESET:C_RESET + 128],
                  data1=sgdB[:, j, :], initial=0.0, op0=ALU.mult, op1=ALU.add)
            A(G, "tensor_tensor", out=tF, in0=tE, in1=sgdB, op=ALU.subtract)
            A(G, "tensor_tensor", out=v4(tG), in0=v4(tE)[:, :, :, 63:64].to_broadcast([128, 4, 2, 64]), in1=v4(tE),
              op=ALU.subtract)
            A(S_, "activation", out=tH, in_=tE, func=AF.Exp, scale=-DS)
            A(S_, "activation", out=tI, in_=tE, func=AF.Exp, scale=DS)
            A(S_, "activation", out=tF, in_=tF, func=AF.Exp, scale=-DS)
            A(S_, "activation", out=tG, in_=tG, func=AF.Exp, scale=-DS)
            A(G, "tensor_copy", out=wc, in_=v4(tH)[:, :, :, 63])
            A(G, "tensor_tensor", out=ZR[:, :, :, 64:128], in0=v4(rT), in1=v4(tH), op=ALU.mult)
            A(V, "scalar_tensor_tensor", out=FZV[:, :, :, 0:64], in0=v4(tA), scalar=-1.0, in1=v4(tF), op0=ALU.mult,
              op1=ALU.mult)
            A(G, "tensor_copy", out=FZV[:, :, :, 64:128], in_=v4(vT))
            A(V, "tensor_tensor", out=LT[:, :, :, 0:64], in0=v4(tD), in1=v4(tI), op=ALU.mult)
            A(V, "tensor_tensor", out=LT[:, :, :, 64:128], in0=v4(tC), in1=v4(tI), op=ALU.mult)
            A(G, "tensor_tensor", out=FBK[:, :, :, 0:64], in0=v4(tD), in1=v4(tG), op=ALU.mult)
            A(G, "tensor_tensor", out=FBK[:, :, :, 64:128], in0=v4(tC), in1=v4(tG), op=ALU.mult)
            return L[:nsplit], L[nsplit:]

        def parallel_part(b):
            par = b % 2
            ZR, rk = ZR2[par], rk2[par]
            CS = (0, 1)
            Lp = []

            class _KB:
                @staticmethod
                def op(*a, **k):
                    Lp.append(Pp(kb_real.op, *a, **k))

            kb = _KB
            H = slice(64, 128)
            for c in CS:
                B0 = 3 * c
                pT1, pT2 = ps[B0], ps[B0 + 1]
                for j in range(4):
                    kb.op(PE, "matmul", out=pT1[:, j * 128:(j + 1) * 128], lhsT=FBK[:, j, c, :], rhs=identb[:], start=True,
                          stop=True)
                for j in range(4):
                    kb.op(PE, "matmul", out=pT2[:, j * 128:(j + 1) * 128], lhsT=FZV[:, j, c, :], rhs=identb[:], start=True,
                          stop=True)
                kb.op(S_, "activation", out=TBK[c], in_=pT1[:], func=AF.Copy)
                kb.op(S_, "activation", out=TZ[c], in_=pT2[0:64, :], func=AF.Copy)
                kb.op(S_, "activation", out=UV[c][64:128].rearrange("p h i -> p (h i)"), in_=pT2[64:128, :], func=AF.Copy)
            for c in CS:
                B0 = 3 * c
                for h in range(8):
                    j, hp = h // 2, (h % 2) * 64
                    kb.op(PE, "matmul", out=ps[B0 + h % 2][:, j * 128:j * 128 + 64], lhsT=LT[hp:hp + 64, j, c, :],
                          rhs=FZV[hp:hp + 64, j, c, 0:64], start=True, stop=True)
                    kb.op(PE, "matmul", out=ps[B0 + h % 2][:, j * 128 + 64:(j + 1) * 128], lhsT=LT[hp:hp + 64, j, c, :],
                          rhs=ZR[hp:hp + 64, j, c, 64:128], start=True, stop=True)
                AAv = AAm[c].rearrange("p (j e) q -> p j e q", e=2)
                for e in range(2):
                    kb.op(V, "tensor_tensor", out=AAv[:, :, e, :], in0=ps[B0 + e][:].rearrange("p (j q) -> p j q", q=128),
                          in1=cst[:, C_M128:C_M128 + 128].unsqueeze(1).to_broadcast([128, 4, 128]), op=ALU.mult)
            for c in CS:
                B0 = 3 * c
                for h in range(8):
                    kb.op(PE, "matmul", out=ps[B0 + 2][0:64, h * 64:(h + 1) * 64], lhsT=AAm[c][0:64, h, 0:64],
                          rhs=identb[0:64, 0:64], start=True, stop=True)
                kb.op(S_, "activation", out=ATm[c].rearrange("p h q -> p (h q)"), in_=ps[B0 + 2][0:64, :], func=AF.Copy)
                kb.op(G, "tensor_tensor", out=Tm[c], in0=AAm[c][0:64, :, 0:64],
                      in1=identb[0:64, 0:64].unsqueeze(1).to_broadcast([64, 8, 64]), op=ALU.add)
            cur = {}
            for c in CS:
                cur[c] = ((lambda h, c=c: AAm[c][0:64, h, 0:64]), (lambda h, c=c: ATm[c][:, h, :]))
            for hop in range(1, 7):
                for c in CS:
                    B0 = 3 * c
                    Pc, PTc = cur[c]
                    pP, pPT, pD = ps[B0], ps[B0 + 1], ps[B0 + 2]
                    Tc = Tm[c]
                    if hop <= 4:
                        for h in range(8):
                            kb.op(PE, "matmul", out=pP[0:64, h * 64:(h + 1) * 64], lhsT=PTc(h), rhs=Pc(h), start=True, stop=True)
                    if hop <= 5:
                        for h in range(8):
                            kb.op(PE, "matmul", out=pPT[0:64, h * 64:(h + 1) * 64], lhsT=Pc(h), rhs=PTc(h), start=True, stop=True)
                    if hop >= 2:
                        for h in range(8):
                            kb.op(PE, "matmul", out=pD[0:64, h * 64:(h + 1) * 64], lhsT=PTc(h), rhs=Tc[:, h, :], start=True, stop=True)
                    nP, nPT = Pk[c][hop % 2], PTk[c][hop % 2]
                    if hop <= 4:
                        kb.op(S_, "activation", out=nP.rearrange("p h q -> p (h q)"), in_=pP[0:64, :], func=AF.Copy)
                    if hop <= 5:
                        kb.op(S_, "activation", out=nPT.rearrange("p h q -> p (h q)"), in_=pPT[0:64, :], func=AF.Copy)
                    if hop >= 2:
                        kb.op(V, "tensor_tensor", out=Tc.rearrange("p h q -> p (h q)"), in0=pD[0:64, :],
                              in1=Tc.rearrange("p h q -> p (h q)"), op=ALU.add)
                    cur[c] = ((lambda h, t=nP: t[:, h, :]), (lambda h, t=nPT: t[:, h, :]))
            for c in CS:
                B0 = 3 * c
                for h in range(8):
                    kb.op(PE, "matmul", out=ps[B0][0:64, h * 64:(h + 1) * 64], lhsT=AAm[c][64:128, h, 0:64],
                          rhs=UV[c][64:128, h, :], start=True, stop=True)
                kb.op(S_, "activation", out=XvT[c].rearrange("p h q -> p (h q)"), in_=ps[B0][0:64, :], func=AF.Copy)
            for c in CS:
                B0 = 3 * c
                for h in range(8):
                    kb.op(PE, "matmul", out=ps[B0 + 1][0:64, h * 64:(h + 1) * 64], lhsT=Tm[c][:, h, :], rhs=XvT[c][:, h, :],
                          start=True, stop=True)
                kb.op(S_, "activation", out=UVT[c].rearrange("p h q -> p (h q)"), in_=ps[B0 + 1][0:64, :], func=AF.Copy)
                for h in range(8):
                    j, hp = h // 2, (h % 2) * 64
                    kb.op(PE, "matmul", out=ps[B0 + 2][hp:hp + 64, j * 64:(j + 1) * 64], lhsT=TZ[c][:, h * 64:(h + 1) * 64],
                          rhs=Tm[c][:, h, :], start=True, stop=True)
                kb.op(V, "tensor_copy", out=ZR[:, :, c, 0:64], in_=ps[B0 + 2][:, 0:256].rearrange("p (j q) -> p j q", q=64))
            Lmain = Lp
            Lp = []

            class _KB2:
                @staticmethod
                def op(*a, **k):
                    Lp.append(Pp(kb_real.op, *a, **k))

            kb = _KB2
            for c in CS:
                B0 = 3 * c
                ts = slice(b * 128 + c * 64, b * 128 + c * 64 + 64)
                for j in range(4):
                    kb.op(PE, "matmul", out=ps[B0][H, 0:8], lhsT=rk[:, j, c * 64:(c + 1) * 64],
                          rhs=hselr[:, j * 8:(j + 1) * 8], start=(j == 0), stop=(j == 3))
                kb.op(PE, "matmul", out=ps[B0 + 1][H, :], lhsT=sgx[:, ts], rhs=gupb[:], start=True, stop=True)
                kb.op(V, "tensor_copy", out=srkc[c][H], in_=ps[B0][H, 0:8])
                kb.op(V, "tensor_tensor", out=G1[c][H], in0=ps[B0 + 1][H, :], in1=gnwb[H, 0, :], op=ALU.mult)
                kb.op(G, "tensor_tensor", out=G0[c][H], in0=UV[c][H], in1=srkc[c][H].unsqueeze(2).to_broadcast([64, 8, 64]),
                      op=ALU.mult)
                g0f = G0[c][H].rearrange("p h i -> p (h i)")
                kb.op(G, "tensor_tensor", out=g0f, in0=g0f, in1=gnwb[H, 1, :], op=ALU.add)
                kb.op(V, "tensor_tensor", out=g0f, in0=g0f, in1=ps[B0 + 1][H, :], op=ALU.mult)
            return Lmain, Lp

        def sequential_part(b):
            Ls = [[], []]
            par = b % 2
            ZR, wc, rk = ZR2[par], wc2[par], rk2[par]
            H = slice(64, 128)
            for c in (0, 1):
                A = lambda *a, _c=c, **k: Ls[_c].append(Pp(kb.op, *a, **k))
                B0 = 3 * c
                AA = AAm[c]
                ts = slice(b * 128 + c * 64, b * 128 + c * 64 + 64)
                for h in range(8):
                    j, hp = h // 2, (h % 2) * 64
                    A(PE, "matmul", out=ps[B0 + h % 2][:, j * 64:(j + 1) * 64], lhsT=ZR[hp:hp + 64, j, c, :],
                      rhs=STb[hp:hp + 64, j, :], start=True, stop=True)
                UVv = UV[c].rearrange("p (j e) i -> p j e i", e=2)
                UVTv = UVT[c].rearrange("p (j e) i -> p j e i", e=2)
                Y1v = Y1.rearrange("p (j e) i -> p j e i", e=2)
                for e in range(2):
                    A(V, "tensor_tensor", out=UVv[0:64, :, e, :],
                      in0=ps[B0 + e][0:64, 0:256].rearrange("p (j i) -> p j i", i=64), in1=UVTv[:, :, e, :], op=ALU.add)
                for e in range(2):
                    A(S_, "activation", out=Y1v[64:128, :, e, :],
                      in_=ps[B0 + e][64:128, 0:256].rearrange("p (j i) -> p j i", i=64), func=AF.Copy)
                for h in range(8):
                    j, hp = h // 2, (h % 2) * 64
                    A(PE, "matmul", out=ps[B0 + 2][hp:hp + 64, j * 64:(j + 1) * 64], lhsT=TBK[c][:, h * 64:(h + 1) * 64],
                      rhs=UV[c][:, h, :], start=True, stop=True)
                for j in range(4):
                    A(V, "scalar_tensor_tensor", out=STf[:, j, :], in0=STf[:, j, :], scalar=wc[:, j, c:c + 1],
                      in1=ps[B0 + 2][:, j * 64:(j + 1) * 64], op0=ALU.mult, op1=ALU.add)
                A(S_, "activation", out=STb[:], in_=STf[:], func=AF.Copy)
                for h in range(8):
                    A(PE, "matmul", out=ps[B0][64:128, h * 64:(h + 1) * 64], lhsT=AA[:, h, 64:128], rhs=UV[c][:, h, :],
                      start=True, stop=True)
                A(V, "tensor_tensor", out=yyc[c][H].rearrange("p h i -> p (h i)"), in0=ps[B0][H, :],
                  in1=Y1[H].rearrange("p h i -> p (h i)"), op=ALU.add)
            return Ls

        def post_part(b, c):
            L = []
            A = lambda *a, **k: L.append(Pp(kb.op, *a, **k))
            H = slice(64, 128)
            B0 = 3 * c
            ts = slice(b * 128 + c * 64, b * 128 + c * 64 + 64)
            yy, st8 = yyc[c], st8c[c]
            A(V, "tensor_reduce", out=st8[H, 0:8], in_=yy[H], axis=AX.X, op=ALU.add)
            A(G, "tensor_tensor", out=ysq[H], in0=yy[H], in1=yy[H], op=ALU.mult)
            A(V, "tensor_reduce", out=st8[H, 8:16], in_=ysq[H], axis=AX.X, op=ALU.add)
            A(V, "tensor_scalar", out=st8[H, 16:24], in0=st8[H, 0:8], scalar1=1.0 / 64, scalar2=None, op0=ALU.mult)
            A(V, "tensor_tensor", out=st8[H, 24:32], in0=st8[H, 16:24], in1=st8[H, 16:24], op=ALU.mult)
            A(V, "scalar_tensor_tensor", out=st8[H, 32:40], in0=st8[H, 8:16], scalar=1.0 / 64, in1=st8[H, 24:32],
              op0=ALU.mult, op1=ALU.subtract)
            A(V, "tensor_scalar", out=st8[H, 32:40], in0=st8[H, 32:40], scalar1=64e-5, scalar2=None, op0=ALU.add)
            A(S_, "activation", out=st8[H, 32:40], in_=st8[H, 32:40], func=AF.Sqrt)
            A(V, "reciprocal", out=st8[H, 40:48], in_=st8[H, 32:40])
            A(G, "tensor_tensor", out=yy[H], in0=yy[H], in1=st8[H, 16:24].unsqueeze(2).to_broadcast([64, 8, 64]),
              op=ALU.subtract)
            A(V, "tensor_tensor", out=yy[H], in0=yy[H], in1=st8[H, 40:48].unsqueeze(2).to_broadcast([64, 8, 64]),
              op=ALU.mult)
            yf = yy[H].rearrange("p h i -> p (h i)")
            A(G, "tensor_tensor", out=yf, in0=yf, in1=G1[c][H], op=ALU.mult)
            A(G, "tensor_tensor", out=yf, in0=yf, in1=G0[c][H].rearrange("p h i -> p (h i)"), op=ALU.add)
            ntail = len(L)
            for j in range(4):
                A(PE, "transpose", out=ps[7][:, j * 64:(j + 1) * 64],
                  in_=yy[H, 2 * j:2 * j + 2, :].rearrange("p h i -> p (h i)"), identity=cst[H, C_ID + 64:C_ID + 128])
            A(S_, "activation", out=catT[:, 0:4, ts], in_=ps[7][:, 0:256].rearrange("p (j t) -> p j t", t=64),
              func=AF.Copy)
            return L[:ntail], L[ntail:]

        def merged(La, Lb):
            na, nb = len(La), len(Lb)
            ib = 0
            for ia, f in enumerate(La):
                f()
                tgt = (ia + 1) * nb // max(na, 1)
                while ib < tgt:
                    Lb[ib]()
                    ib += 1
            while ib < nb:
                Lb[ib]()
                ib += 1

        p1, p2 = prep(0)
        merged(conv_section(), p1 + p2)
        pend = []
        for b in range(4):
            Lmain, Lepi = parallel_part(b)
            S0, S1 = sequential_part(b)
            p1, p2 = prep(b + 1) if b < 3 else ([], [])
            pp = p1 + p2
            nfirst = (len(pp) * 3) // 5
            merged(Lmain, pend + pp[:nfirst])
            merged(Lepi + S0 + S1, pp[nfirst:])
            e0, t0 = post_part(b, 0)
            e1, t1 = post_part(b, 1)
            pend = e0 + t0 + e1 + t1
        merged(pend, [])
        xs_cur[0] = xs + xsm
        tokmajor_proj(lambda kc, b: catT[:, kc, b * 128:(b + 1) * 128], 4, woutS, 1,
                      tmp + [fl(tA), fl(tB), fl(tG), fl(tH), fl(tI), fl(sgdB)], after_half)

    mset(G, carry[:], 0.0)
    mset(G, glu[:], 0.0)
    mset(G, STf[:], 0.0)
    mset(G, STb[:], 0.0)

    xv = x_d.rearrange("(n b p) d -> n p b d", b=4, p=128)
    ov = out_d.rearrange("(n b p) d -> n p b d", b=4, p=128)
    last = []

    def load_x(ti, blocks):
        for b in blocks:
            kb.op(SY, "dma_start", dma_key=f"xl{b}", out=xt[:, b, :], in_=xv[ti][:, b, :])

    load_x(0, range(4))
    prenorm(K_G1)
    for ti in range(NT):
        ffn(0, lambda blocks: prenorm(K_GM, blocks))
        mixer(ti, lambda blocks: prenorm(K_G2, blocks))

        nxt = {}

        def mid_hook(cvf, ti=ti, nxt=nxt):
            if ti + 1 >= NT:
                return
            xnext = cvf.get([128, 4, D])
            for b in range(4):
                kb.op(SY, "dma_start", dma_key=f"xn{b}", out=xnext[:, b, :], in_=xv[ti + 1][:, b, :])
            for b in range(4):
                nxt[b] = (xnext[:, b, :], prenorm_stats(b, xnext[:, b, :]))

        def tail_hook(blocks, ti=ti, nxt=nxt):
            for b in blocks:
                kb.op(SY, "dma_start", dma_key=f"xs{b}", out=ov[ti][:, b, :], in_=xt[:, b, :])
                if ti + 1 < NT:
                    src, xb = nxt[b]
                    prenorm_transpose(K_G1, b, xb)
                    kb.op(G, "tensor_copy", out=xt[:, b, :], in_=src)

        ffn(1, tail_hook, mid_hook)
    for b in range(4):
        kb.finish([o for o in kb.ops["sync"] if o.dma_key == f"xs{b}"][-1])
    kb.emit()
    st.close()
    return nc


def make_inputs(T, x, p):
    g = lambda k: np.asarray(p[k], np.float32)[0]
    cols = np.zeros((128, K_END), np.float32)
    cols[:, K_MU:K_MU + 14] = _fm(g("shift_mu"), 14)
    cols[:, K_W0:K_W0 + 4] = _fm(g("w0"), 4)
    cols[:, K_A0:K_A0 + 4] = _fm(g("a0"), 4)
    cols[:, K_KK:K_KK + 4] = _fm(g("k_k"), 4)
    cols[:, K_KA:K_KA + 4] = _fm(g("k_a"), 4)
    cols[:, K_RK:K_RK + 4] = _fm(g("r_k").reshape(-1), 4)
    cols[:, K_CB:K_CB + 4] = _fm(g("conv_b"), 4)
    cols[:, K_LNW:K_LNW + 4] = _fm(g("conv_ln_w"), 4)
    cols[:, K_LNB:K_LNB + 4] = _fm(g("conv_ln_b"), 4)
    cols[:, K_G1:K_G1 + 8] = _fm(g("ffn1_norm_pre"), 8)
    cols[:, K_GM:K_GM + 8] = _fm(g("mix_norm_pre"), 8)
    cols[:, K_G2:K_G2 + 8] = _fm(g("ffn2_norm_pre"), 8)
    dw = g("conv_dw")
    cols[:, K_DW:K_DW + 124] = dw.T.reshape(4, 128, 31).transpose(1, 0, 2).reshape(128, 124)
    rows = np.zeros((5, D), np.float32)
    rows[0] = g("ffn1_norm_post")
    rows[1] = g("mix_norm_post")
    rows[2] = g("ffn2_norm_post")
    rows[3, 0:512] = g("gn_w")
    rows[3, 512:1024] = g("gn_b")
    lora = np.concatenate([g("w_up"), g("a_up")], 0)
    shared = {
        "wgu1": g("ffn1_w_gu"), "wgu2": g("ffn2_w_gu"), "wdn1": g("ffn1_w_down"), "wdn2": g("ffn2_w_down"),
        "win": g("w_in"), "wout": g("w_out"), "cols": cols, "consts": _consts(), "rows": rows,
        "lora": np.ascontiguousarray(lora), "gup": g("g_up"),
    }
    return shared


_CACHE = {}


def kernel(**inputs):
    x = np.asarray(inputs["x"], np.float32)
    B, T, _ = x.shape
    shared = make_inputs(T, x, inputs)
    if T not in _CACHE:
        _CACHE[T] = build(T)
    nc = _CACHE[T]
    in_maps = []
    for b in range(B):
        m = dict(shared)
        m["x"] = np.ascontiguousarray(x[b])
        in_maps.append(m)
    res = run_bass_kernel_spmd(nc, in_maps, core_ids=list(range(B)))
    return np.stack([np.asarray(r["out"], np.float32) for r in res.results], 0)
```
